# Optimizing a Trainium2 kernel written in Bass

```python
import math
import jax, jax.numpy as jnp
from jax import lax
import numpy as np

D_MODEL = 1024
BATCH = 8
SEQ = 4096
DEPTH = 1

D_MIX = D_MODEL
A_HEAD_DIM = 64
A_HEADS = (D_MIX // 2) // A_HEAD_DIM
A_WIDTH = A_HEADS * A_HEAD_DIM
DILATED_PATTERNS = ((128, 1), (512, 4), (2048, 16))
BLOCK = 128
B_HEADS = 8
QK_NOPE_DIM = 64
QK_ROPE_DIM = 32
V_HEAD_DIM = (D_MIX - A_WIDTH) // B_HEADS
B_WIDTH = B_HEADS * V_HEAD_DIM
Q_LORA_RANK = 256
KV_LORA_RANK = 128
ROPE_THETA = 10000.0
REL_BUCKETS = 32
REL_MAX_DISTANCE = 2048
EPS = 1e-6

IN_SPLITS = (A_WIDTH, A_WIDTH, A_WIDTH, A_WIDTH,
             Q_LORA_RANK, KV_LORA_RANK, QK_ROPE_DIM, B_WIDTH)
IN_COLS = sum(IN_SPLITS)

kernel_name = "hybrid_dilated_swa_mla_gated"


def rmsnorm(t, gain):
    tf = t.astype(jnp.float32)
    return tf * lax.rsqrt(jnp.mean(tf * tf, axis=-1, keepdims=True) + EPS) * gain.astype(jnp.float32)


def rope(t, cos, sin):
    t1, t2 = jnp.split(t, 2, axis=-1)
    return jnp.concatenate([t1 * cos - t2 * sin, t1 * sin + t2 * cos], axis=-1)


def t5_bucket(dist):
    max_exact = REL_BUCKETS // 2
    d = jnp.maximum(dist.astype(jnp.float32), 1.0)
    large = max_exact + (jnp.log(d / max_exact) / math.log(REL_MAX_DISTANCE / max_exact)
                         * (REL_BUCKETS - max_exact)).astype(jnp.int32)
    large = jnp.minimum(large, REL_BUCKETS - 1)
    return jnp.where(dist < max_exact, dist, large)


def dilated_pattern(q, k, v, rel_bias, window, dilation):
    B, S, H, D = q.shape
    steps = window // dilation
    span = dilation * BLOCK
    Sp = -(-S // span) * span
    nb = Sp // span
    pad = ((0, 0), (0, Sp - S), (0, 0), (0, 0))

    def blocks(t):
        return jnp.pad(t, pad).reshape(B, nb, BLOCK, dilation, H, D)

    def with_prev(t):
        prev = jnp.pad(t[:, :-1], ((0, 0), (1, 0), (0, 0), (0, 0), (0, 0), (0, 0)))
        return jnp.concatenate([prev, t], axis=2)

    qb = blocks(q)
    kw = with_prev(blocks(k))
    vw = with_prev(blocks(v))

    qi = jnp.arange(BLOCK)[:, None]
    ki = jnp.arange(2 * BLOCK)[None, :]
    j = qi + BLOCK - ki
    bias = rel_bias[t5_bucket(jnp.maximum(j, 0) * dilation)]
    bias = jnp.transpose(bias, (2, 0, 1)).astype(jnp.float32)
    valid = ((j >= 0) & (j <= steps))[None] & \
        ((jnp.arange(nb)[:, None, None] > 0) | (ki >= BLOCK)[None])

    s = jnp.einsum('bnqrhd,bnkrhd->bnrhqk', qb, kw) * (A_HEAD_DIM ** -0.5) + bias
    s = jnp.where(valid[None, :, None, None], s, -jnp.inf)
    m = jnp.max(s, axis=-1, keepdims=True)
    p = jnp.exp(s - m)
    den = jnp.sum(p, axis=-1)
    den_q = jnp.transpose(den, (0, 1, 4, 2, 3))
    o = jnp.einsum('bnrhqk,bnkrhd->bnqrhd', p, vw) / den_q[..., None]
    o = o.reshape(B, Sp, H, D)[:, :S]
    m = jnp.transpose(m[..., 0], (0, 1, 4, 2, 3)).reshape(B, Sp, H)[:, :S]
    den = den_q.reshape(B, Sp, H)[:, :S]
    return o, m, den


def dilated_window_attention(q, k, v, rel_bias):
    outs, log_dens = [], []
    for window, dilation in DILATED_PATTERNS:
        o, m, den = dilated_pattern(q, k, v, rel_bias, window, dilation)
        outs.append(o)
        log_dens.append(m + jnp.log(den))
    alpha = jax.nn.softmax(jnp.stack(log_dens, axis=0), axis=0)
    return jnp.sum(alpha[..., None] * jnp.stack(outs, axis=0), axis=0)


def latent_attention(c_q, c_kv, k_rope_in, positions, q_c_gain, w_uq, kv_c_gain, w_ukv,
                     qn_gain, qr_gain, kn_gain, kr_gain):
    B, S, _ = c_q.shape
    q = jnp.matmul(rmsnorm(c_q, q_c_gain), w_uq.astype(jnp.float32))
    q = q.reshape(B, S, B_HEADS, QK_NOPE_DIM + QK_ROPE_DIM)
    kv = jnp.matmul(rmsnorm(c_kv, kv_c_gain), w_ukv.astype(jnp.float32))
    kv = kv.reshape(B, S, B_HEADS, QK_NOPE_DIM + V_HEAD_DIM)
    q_nope = rmsnorm(q[..., :QK_NOPE_DIM], qn_gain)
    q_rope = rmsnorm(q[..., QK_NOPE_DIM:], qr_gain)
    k_nope = rmsnorm(kv[..., :QK_NOPE_DIM], kn_gain)
    v = kv[..., QK_NOPE_DIM:]
    k_rope = rmsnorm(k_rope_in, kr_gain)

    inv_freq = ROPE_THETA ** (-jnp.arange(0, QK_ROPE_DIM, 2, dtype=jnp.float32) / QK_ROPE_DIM)
    ang = positions.astype(jnp.float32)[..., None] * inv_freq
    cos, sin = jnp.cos(ang), jnp.sin(ang)
    q_rope = rope(q_rope, cos[:, :, None], sin[:, :, None])
    k_rope = rope(k_rope, cos, sin)

    scale = (QK_NOPE_DIM + QK_ROPE_DIM) ** -0.5
    nb = S // BLOCK
    qn_b = q_nope.reshape(B, nb, BLOCK, B_HEADS, QK_NOPE_DIM).transpose(1, 0, 2, 3, 4)
    qr_b = q_rope.reshape(B, nb, BLOCK, B_HEADS, QK_ROPE_DIM).transpose(1, 0, 2, 3, 4)
    kpos = jnp.arange(S)

    def one_block(args):
        qn, qr, i = args
        s = (jnp.einsum('bqhd,bkhd->bhqk', qn, k_nope)
             + jnp.einsum('bqhd,bkd->bhqk', qr, k_rope)) * scale
        qpos = i * BLOCK + jnp.arange(BLOCK)
        s = jnp.where((qpos[:, None] >= kpos[None, :])[None, None], s, -jnp.inf)
        p = jax.nn.softmax(s, axis=-1)
        return jnp.einsum('bhqk,bkhd->bqhd', p, v)

    o = lax.map(one_block, (qn_b, qr_b, jnp.arange(nb)))
    return o.transpose(1, 0, 2, 3, 4).reshape(B, S, B_WIDTH)


def hybrid_layer(x, positions, rel_bias, norm_gain, w_in, a_q_gain, a_k_gain, q_c_gain, w_uq,
                 kv_c_gain, w_ukv, qn_gain, qr_gain, kn_gain, kr_gain, w_out):
    B, S, _ = x.shape
    h = rmsnorm(x, norm_gain)
    proj = jnp.matmul(h, w_in.astype(jnp.float32))
    idx = np.cumsum(IN_SPLITS)[:-1].tolist()
    q_a, k_a, v_a, z_a, c_q, c_kv, k_rope, z_b = jnp.split(proj, idx, axis=-1)

    q_a = rmsnorm(q_a.reshape(B, S, A_HEADS, A_HEAD_DIM), a_q_gain)
    k_a = rmsnorm(k_a.reshape(B, S, A_HEADS, A_HEAD_DIM), a_k_gain)
    v_a = v_a.reshape(B, S, A_HEADS, A_HEAD_DIM)
    o_a = dilated_window_attention(q_a, k_a, v_a, rel_bias).reshape(B, S, A_WIDTH)

    o_b = latent_attention(c_q, c_kv, k_rope, positions, q_c_gain, w_uq, kv_c_gain, w_ukv,
                           qn_gain, qr_gain, kn_gain, kr_gain)

    mixed = jnp.concatenate([o_a * jax.nn.silu(z_a), o_b * jax.nn.silu(z_b)], axis=-1)
    out = jnp.matmul(mixed, w_out.astype(jnp.float32))
    return (x.astype(jnp.float32) + out).astype(x.dtype)


def setup_inputs(seed: int = 0) -> dict:
    key = jax.random.key(seed)
    ks = jax.random.split(key, 18)
    f32 = jnp.float32

    def nrm(k, shape, scale):
        return jax.random.normal(k, shape, f32) * scale

    def gain(k, shape):
        return 1.0 + 0.05 * jax.random.normal(k, shape, f32)

    x = jax.random.normal(ks[0], (BATCH, SEQ, D_MODEL), f32)
    offsets = jax.random.randint(ks[1], (BATCH, 1), 0, 1024, dtype=jnp.int32)
    positions = (jnp.arange(SEQ, dtype=jnp.int32)[None, :] + offsets).astype(jnp.int32)
    return {
        "x": x,
        "positions": positions,
        "rel_bias": nrm(ks[2], (REL_BUCKETS, A_HEADS), 0.1),
        "norm_gain": gain(ks[3], (DEPTH, D_MODEL)),
        "w_in": nrm(ks[4], (DEPTH, D_MODEL, IN_COLS), D_MODEL ** -0.5),
        "a_q_gain": gain(ks[5], (DEPTH, A_HEAD_DIM)),
        "a_k_gain": gain(ks[6], (DEPTH, A_HEAD_DIM)),
        "q_c_gain": gain(ks[7], (DEPTH, Q_LORA_RANK)),
        "w_uq": nrm(ks[8], (DEPTH, Q_LORA_RANK, B_HEADS * (QK_NOPE_DIM + QK_ROPE_DIM)), Q_LORA_RANK ** -0.5),
        "kv_c_gain": gain(ks[9], (DEPTH, KV_LORA_RANK)),
        "w_ukv": nrm(ks[10], (DEPTH, KV_LORA_RANK, B_HEADS * (QK_NOPE_DIM + V_HEAD_DIM)), KV_LORA_RANK ** -0.5),
        "qn_gain": gain(ks[11], (DEPTH, QK_NOPE_DIM)),
        "qr_gain": gain(ks[12], (DEPTH, QK_ROPE_DIM)),
        "kn_gain": gain(ks[13], (DEPTH, QK_NOPE_DIM)),
        "kr_gain": gain(ks[14], (DEPTH, QK_ROPE_DIM)),
        "w_out": nrm(ks[15], (DEPTH, D_MIX, D_MODEL), D_MIX ** -0.5),
    }


def reference(x, positions, rel_bias, norm_gain, w_in, a_q_gain, a_k_gain, q_c_gain, w_uq,
              kv_c_gain, w_ukv, qn_gain, qr_gain, kn_gain, kr_gain, w_out):
    for l in range(DEPTH):
        x = hybrid_layer(x, positions, rel_bias, norm_gain[l], w_in[l], a_q_gain[l], a_k_gain[l],
                         q_c_gain[l], w_uq[l], kv_c_gain[l], w_ukv[l], qn_gain[l], qr_gain[l],
                         kn_gain[l], kr_gain[l], w_out[l])
    return x
```

```python
import contextlib
import math
import numpy as np
import concourse.bass as bass
import concourse.mybir as mybir
from concourse.bass_utils import run_bass_kernel_spmd

F32 = mybir.dt.float32
BF16 = mybir.dt.bfloat16
I32 = mybir.dt.int32
AF = mybir.ActivationFunctionType
ALU = mybir.AluOpType

S = 4096
D = 1024
NT = 32
NQ = 8
NCOLS = 2976
MASKV = -30000.0
EPS = 1e-6
PATTERNS = (1, 4, 16)
QSCALE_A = 64 ** -0.5
QSCALE_B = 96 ** -0.5
SB_BASE = 16512
SB_LIMIT = 229344


def _ssl(t0, n, r):
    return slice(t0, t0 + (n - 1) * r + 1, r)


class Sem:
    def __init__(self, k, name):
        self.name = name
        self.sem = k.es.enter_context(k.nc.semaphore(name))
        self.cnt = 0


class Eng(Sem):
    def __init__(self, k, name, eng):
        super().__init__(k, "s_" + name)
        self.eng = eng
        self.seen = {}

    def wait(self, *toks):
        for t in toks:
            if t is None:
                continue
            if isinstance(t, list):
                self.wait(*t)
                continue
            p, c = t
            if self.seen.get(p.name, 0) >= c:
                continue
            self.eng.wait_ge(p.sem, c)
            self.seen[p.name] = c

    def done(self, ins):
        ins.then_inc(self.sem, 1)
        self.cnt += 1
        return (self, self.cnt)


class Buf:
    def __init__(self, name="", excl=False):
        self.name = name
        self.excl = excl
        self.w = None
        self.r = {}

    def add_r(self, tok):
        p, c = tok
        if self.r.get(p.name, (None, 0))[1] < c:
            self.r[p.name] = tok

    def set_w(self, tok):
        self.w = tok
        self.r = {}


class K:
    def __init__(self):
        self.nc = bass.Bass("TRN2", target_bir_lowering=False)
        self.es = contextlib.ExitStack()
        nc = self.nc
        self.PE = Eng(self, "pe", nc.tensor)
        self.ACT = Eng(self, "act", nc.scalar)
        self.DVE = Eng(self, "dve", nc.vector)
        self.POOL = Eng(self, "pool", nc.gpsimd)
        self.SP = Eng(self, "sp", nc.sync)
        self.engines = [self.PE, self.ACT, self.DVE, self.POOL, self.SP]
        self.dsems = []

    def dsem(self, name):
        d = Sem(self, name)
        self.dsems.append(d)
        return d

    def wait_for(self, E, reads, writes):
        for b in reads:
            E.wait(b.w)
            if b.excl:
                E.wait(*[t for n, t in b.r.items() if n != E.name])
        for b in writes:
            E.wait(b.w)
            E.wait(*b.r.values())

    def commit(self, E, ins, reads, writes):
        tok = E.done(ins)
        for b in reads:
            b.add_r(tok)
        for b in writes:
            b.set_w(tok)
        return tok

    def do(self, E, reads, writes, emit):
        self.wait_for(E, reads, writes)
        ins = emit()
        return self.commit(E, ins, reads, writes)

    def dma(self, Q, ds, out, in_, reads=(), writes=()):
        for b in reads:
            Q.wait(b.w)
        for b in writes:
            Q.wait(b.w)
            Q.wait(*b.r.values())
        Q.eng.dma_start(out=out, in_=in_).then_inc(ds.sem, 16)
        ds.cnt += 16
        tok = (ds, ds.cnt)
        for b in reads:
            b.add_r(tok)
        for b in writes:
            b.set_w(tok)
        return tok

    def barrier(self):
        for E in self.engines:
            for F in self.engines:
                if F is not E and F.cnt > 0:
                    E.wait((F, F.cnt))
            for d in self.dsems:
                if d.cnt > 0:
                    E.wait((d, d.cnt))


def build(stage=99, dbg=None):
    k = K()
    nc = k.nc
    PE, ACT, DVE, POOL, SP = k.PE, k.ACT, k.DVE, k.POOL, k.SP
    tn, sc, ve, gp_, sy = nc.tensor, nc.scalar, nc.vector, nc.gpsimd, nc.sync

    def dram(name, shape, dt, kind="ExternalInput"):
        return nc.dram_tensor(name, shape, dt, kind=kind)

    x_d = dram("x", [S, D], F32).ap()
    pos_t = dram("pos", [1, S], I32)
    w_in_d = dram("w_in", [D, NCOLS], F32).ap()
    w_kr_d = dram("w_kr", [D, 128], F32).ap()
    w_uq_d = dram("w_uq", [256, 1024], F32).ap()
    w_ukvk_d = dram("w_ukvk", [128, 576], F32).ap()
    w_ukvv_d = dram("w_ukvv", [128, 512], F32).ap()
    w_out_d = dram("w_out", [D, D], F32).ap()
    cbf_d = dram("cbf", [128, 6, 128], F32).ap()
    cf_d = dram("cf", [128, 2, 128], F32).ap()
    cf2_d = dram("cf2", [128, 2, 128], F32).ap()
    gp_d = dram("gp", [128, 16], F32).ap()
    rc_d = dram("rc", [128, 4], F32).ap()
    bt_d = dram("bt", [4, 128, 6, 256], F32).ap()
    y_d = dram("y", [S, D], F32, kind="ExternalOutput").ap()
    dbg_d = None
    if dbg is not None:
        dbg_d = dram("dbg", list(dbg[1]), dbg[2], kind="ExternalOutput").ap()

    w_in_v = w_in_d.rearrange("(c p) n -> p c n", p=128)
    w_kr_v = w_kr_d.rearrange("(c p) n -> p c n", p=128)
    w_uq_v = w_uq_d.rearrange("(c p) n -> p c n", p=128)
    w_out_v = w_out_d.rearrange("(c p) n -> p c n", p=128)

    def A(name, off, shape, dt):
        sz = int(np.prod(shape[1:])) * (4 if dt in (F32, I32) else 2)
        assert off % 32 == 0 and SB_BASE + off + sz <= SB_LIMIT, (name, off, sz)
        return nc.alloc_sbuf_tensor_at(name, list(shape), dt, offset=SB_BASE + off)

    cbf = A("cbf", 0, [128, 6, 128], BF16)
    cf = A("cf", 1536, [128, 2, 128], F32)
    gp = A("gp", 2560, [128, 16], F32)
    rc = A("rc", 2624, [128, 4], F32)
    st_ss = A("st_ss", 2656, [128, 4], F32)
    st_t = A("st_t", 2688, [128, 4], F32)
    st_rs = A("st_rs", 2720, [128, 4], F32)
    cm05 = A("cm05", 2752, [128, 1], F32)
    epsT = A("epsT", 2784, [128, 1], F32)
    lnhalf = A("lnhalf", 2816, [128, 1], F32)
    cf2 = A("cf2", 2880, [128, 2, 128], F32)
    OFF_MIXA, OFF_MIXB, OFF_HT, OFF_R = 4096, 36864, 69632, 135168
    mixA = A("mixA", OFF_MIXA, [128, 4, S], BF16)
    mixB = A("mixB", OFF_MIXB, [128, 4, S], BF16)
    acc_e = A("acc_e", OFF_MIXB, [128, S], F32)
    acc_o = A("acc_o", OFF_MIXB + 16384, [128, S], F32)
    hT = A("hT", OFF_HT, [128, 8, S], BF16)
    BhT = [Buf() for _ in range(NT)]
    BmixA = [[Buf() for _ in range(NQ)] for _ in range(4)]
    BmixB = [[Buf() for _ in range(NQ)] for _ in range(4)]

    ident = cbf[:, 0, :]
    bo64 = cbf[:, 1, :]
    bo_q = cbf[:, 2, :]
    ones256 = cbf[:, 3, :]
    ones128 = cbf[:, 4, :]
    causal = cbf[:, 5, :]
    sel = cf[:, 0, :]
    msum = cf[:, 1, :]

    ps = [k.es.enter_context(nc.psum_tensor(f"ps{i}", [128, 512], F32)) for i in range(8)]
    Bps = [Buf(excl=True) for _ in range(8)]

    dconst = k.dsem("d_const")
    Bconst = Buf()

    dconst2 = k.dsem("d_const2")
    Bcbf = Buf()
    k.dma(POOL, dconst2, cbf[:], cbf_d, writes=[Bcbf])
    k.dma(SP, dconst, cf[:], cf_d, writes=[Bconst])
    k.dma(SP, dconst, cf2[:], cf2_d, writes=[Bconst])
    k.dma(SP, dconst, gp[:], gp_d, writes=[Bconst])
    k.dma(SP, dconst, rc[:], rc_d, writes=[Bconst])
    Bconst.w = (dconst, dconst.cnt)
    k.do(DVE, [], [Bconst], lambda: ve.tensor_scalar(gp[:, 8:9], gp[:, 8:9], QSCALE_A, None, ALU.mult))
    k.do(DVE, [], [Bconst], lambda: ve.tensor_scalar(gp[:, 13:14], gp[:, 13:14], QSCALE_B, None, ALU.mult))
    k.do(POOL, [], [Bconst], lambda: gp_.memset(cm05[:], -0.5))
    k.do(POOL, [], [Bconst], lambda: gp_.memset(epsT[:], EPS))
    k.do(POOL, [], [Bconst], lambda: gp_.memset(lnhalf[:], math.log(0.5)))
    k.barrier()

    dout = k.dsem("d_out")

    def finish(dump=None):
        if dump is not None:
            k.barrier()
            src, = dump
            k.dma(SP, dout, dbg_d, src)
        k.barrier()
        SP.wait((dout, dout.cnt)) if dout.cnt else None
        k.es.close()
        return nc

    def load_w_scaled(dsm, stage, Bstage, dst, Bdst, src_ap, ncols):
        k.dma(SP, dsm, stage[:, :, 0:ncols], src_ap, writes=[Bstage])
        k.do(POOL, [Bstage, Bconst], [Bdst], lambda: gp_.tensor_tensor(
            dst[:, :, 0:ncols], stage[:, :, 0:ncols],
            gp[:, 0:8].unsqueeze(2).to_broadcast([128, 8, ncols]), ALU.mult))

    R0 = OFF_R
    xbuf = [A(f"xbuf{i}", R0 + i * 4096, [128, D], F32) for i in range(3)]
    xn = [A(f"xn{i}", R0 + 12288 + i * 2048, [128, D], BF16) for i in range(2)]
    junk = A("junk", R0 + 16384, [128, D], BF16)
    Bx = [Buf() for _ in range(3)]
    Bxn = [Buf() for _ in range(2)]
    Bjunk = Buf()
    Bst = [Buf() for _ in range(2)]
    dx = [k.dsem(f"d_x{i}") for i in range(3)]
    x_v = x_d.rearrange("(n p) d -> n p d", p=128)

    def ph1_load(j):
        k.dma(SP, dx[j % 3], xbuf[j % 3][:], x_v[j], writes=[Bx[j % 3]])

    def ph1_a(j):
        s = j % 2
        k.do(ACT, [Bx[j % 3]], [Bjunk, Bst[s]], lambda: sc.activation(
            out=junk[:], in_=xbuf[j % 3][:], func=AF.Square, accum_out=st_ss[:, s:s + 1]))
        k.do(DVE, [], [Bst[s]], lambda: ve.tensor_scalar(
            st_t[:, s:s + 1], st_ss[:, s:s + 1], 1.0 / D, EPS, ALU.mult, ALU.add))
        k.do(POOL, [Bconst], [Bst[s]], lambda: gp_.tensor_tensor(
            st_rs[:, s:s + 1], st_t[:, s:s + 1], cm05[:], ALU.pow))
        k.do(DVE, [Bx[j % 3], Bst[s]], [Bxn[s]], lambda: ve.tensor_scalar(
            xn[s][:], xbuf[j % 3][:], st_rs[:, s:s + 1], None, ALU.mult))

    def ph1_b(j):
        s = j % 2
        pt = ps[s][:].bitcast(BF16)

        def tr():
            for c in range(8):
                ins = tn.transpose(pt[:, c * 128:(c + 1) * 128], xn[s][:, c * 128:(c + 1) * 128], ident)
            return ins
        k.do(PE, [Bxn[s], Bconst], [Bps[s]], tr)
        if j % 2 == 0:
            k.do(ACT, [Bps[s]], [BhT[j]], lambda: sc.activation(
                out=hT[:, :, j * 128:(j + 1) * 128], in_=pt.rearrange("p (c t) -> p c t", c=8), func=AF.Copy))
        else:
            k.do(DVE, [Bps[s]], [BhT[j]], lambda: ve.tensor_copy(
                hT[:, :, j * 128:(j + 1) * 128], pt.rearrange("p (c t) -> p c t", c=8)))

    ph1_load(0)
    ph1_load(1)
    for j in range(NT):
        if j + 2 < NT:
            ph1_load(j + 2)
        ph1_a(j)
        if j >= 1:
            ph1_b(j - 1)
    ph1_b(NT - 1)
    if stage == 1:
        return finish((hT[:],))
    k.barrier()

    wst = A("wst", R0, [128, 8, 128], F32)
    Bwst = Buf()
    dw = k.dsem("d_w")

    def gates(col0, dst, Bdst, wg, Bwg, th, Bth, banks, wst, Bwst, dw):
        for c in range(4):
            w = wg[c % 2]
            load_w_scaled(dw, wst, Bwst, w, Bwg[c % 2], w_in_v[:, :, col0 + c * 128: col0 + (c + 1) * 128], 128)
            for i in range(NQ):
                n = c * NQ + i
                bk = banks[n % len(banks)]

                def mm():
                    for dmc in range(8):
                        ins = tn.matmul(ps[bk][:], w[:, dmc, :], hT[:, dmc, i * 512:(i + 1) * 512],
                                        start=(dmc == 0), stop=(dmc == 7))
                    return ins
                k.do(PE, [Bwg[c % 2]] + BhT[4 * i:4 * i + 4], [Bps[bk]], mm)
                t_ = th[n % 2]
                k.do(ACT, [Bps[bk]], [Bth[n % 2]], lambda: sc.activation(
                    out=t_[:], in_=ps[bk][:], func=AF.Tanh, scale=0.5))
                k.do(DVE, [Bps[bk], Bth[n % 2]], [Bdst[c][i]], lambda: ve.scalar_tensor_tensor(
                    dst[:, c, i * 512:(i + 1) * 512], t_[:], 1.0, ps[bk][:], ALU.add, ALU.mult))

    wg = [A(f"wg{i}", R0 + 4096 + i * 2048, [128, 8, 128], BF16) for i in range(2)]
    Bwg = [Buf(), Buf()]
    th = [A(f"th{i}", R0 + 8192 + i * 2048, [128, 512], F32) for i in range(2)]
    Bth = [Buf(), Buf()]
    gates(1536, mixA, BmixA, wg, Bwg, th, Bth, [0, 1, 2], wst, Bwst, dw)
    if stage == 2:
        return finish((mixA[:],))
    k.barrier()

    o = R0
    Qzz = A("Qzz", o, [128, 2, S], BF16)
    Qz = [Qzz[:, 0, :], Qzz[:, 1, :]]
    o += 16384
    KT = A("KT", o, [128, S], BF16)
    o += 8192
    Vp = A("Vp", o, [128, 32, 192], BF16)
    o += 12288
    btab = A("btab", o, [128, 6, 256], BF16)
    o += 3072
    wq = [A("wq0", o, [128, 8, 128], BF16)] * 2
    o += 2048
    wk = [A("wk0", o, [128, 8, 128], BF16)] * 2
    o += 2048
    wv = [A("wv0", o, [128, 8, 128], BF16)] * 2
    o += 2048
    VT = A("VT", o, [128, S], BF16)
    o += 8192
    stg = [A(f"stg{i}", o + i * 2048, [128, 512], F32) for i in range(2)]
    o += 4096
    wstA = A("wstA", o, [128, 8, 128], F32)
    o += 4096
    PT = [A(f"PT{i}", o + i * 1024, [128, 512], BF16) for i in range(4)]
    o += 4096
    sq = [A(f"sq{i}", o + i * 1024, [128, 512], BF16) for i in range(2)]
    o += 2048
    ssb = [A("ssb0", o, [128, 512], F32)] * 2
    o += 2048
    rstd = [A("rstd0", o, [128, 512], F32)] * 2
    o += 2048
    Rt = A("Rt", o, [128, 512], F32)
    o += 2048
    tfin = [A("tfin0", o, [128, 512], F32), stg[0]]
    o += 2048
    assert SB_BASE + o <= SB_LIMIT, o
    BQz = [[Buf() for _ in range(NQ)] for _ in range(2)]
    BKT = [Buf() for _ in range(NQ)]
    BVp = [Buf() for _ in range(4)]
    Bbt = Buf()
    Bwq, Bwk, Bwv = [Buf()] * 2, [Buf()] * 2, [Buf()] * 2
    BVT = [Buf() for _ in range(NQ)]
    Bstg = [Buf(), Buf()]
    BwstA = Buf()
    BPT = [Buf() for _ in range(4)]
    Bsq, Bssb, Brstd = [Buf(), Buf()], [Buf()] * 2, [Buf()] * 2
    BRt = Buf()
    Btfin = [Buf(), Buf()]
    Bacc = [Buf(), Buf()]
    dwa = k.dsem("d_wa")
    dbt = k.dsem("d_bt")

    BQzAll = Buf()
    k.do(POOL, [], [BQzAll], lambda: gp_.memset(Qz[0][64:128, :], 0.0))
    k.do(POOL, [], [BQzAll], lambda: gp_.memset(Qz[1][0:64, :], 0.0))
    BVpE = Buf()
    k.do(POOL, [], [BVpE], lambda: gp_.memset(Vp[:, :, 64:128], 1.0))
    k.do(POOL, [], [BRt], lambda: gp_.memset(Rt[:], 0.0))
    k.do(POOL, [], [Bacc[0]], lambda: gp_.memset(acc_e[64:128, :], 0.0))
    for b_ in BQz[0] + BQz[1]:
        b_.w = BQzAll.w
    for b_ in BVp:
        b_.w = BVpE.w

    nrm_ctr = [0]

    def blocknorm(bk_proj, bk_ss, bo_list, extra_reads=()):
        n = nrm_ctr[0]
        nrm_ctr[0] += 1
        s = n % 2
        for ci, bk in enumerate(bk_proj):
            k.do(ACT, [Bps[bk]], [Bsq[s]], lambda bk=bk: sc.activation(out=sq[s][:], in_=ps[bk][:], func=AF.Square))
            k.do(PE, [Bsq[s], Bconst], [Bps[bk_ss]], lambda ci=ci: tn.matmul(
                ps[bk_ss][:], bo_list, sq[s][:], start=(ci == 0), stop=(ci == len(bk_proj) - 1)))
        k.do(ACT, [Bps[bk_ss], Bconst], [Bssb[s]], lambda: sc.activation(
            out=ssb[s][:], in_=ps[bk_ss][:], func=AF.Ln, bias=epsT[:, 0:1]))
        k.do(ACT, [Bssb[s]], [Brstd[s]], lambda: sc.activation(out=rstd[s][:], in_=ssb[s][:], func=AF.Exp, scale=-0.5))
        return s

    def load_pair_weights(hp):
        s = hp % 2
        load_w_scaled(dwa, wstA, BwstA, wq[s], Bwq[s], w_in_v[:, :, hp * 128:(hp + 1) * 128], 128)
        load_w_scaled(dwa, wstA, BwstA, wk[s], Bwk[s], w_in_v[:, :, 512 + hp * 128:512 + (hp + 1) * 128], 128)
        load_w_scaled(dwa, wstA, BwstA, wv[s], Bwv[s], w_in_v[:, :, 1024 + hp * 128:1024 + (hp + 1) * 128], 128)

    def proj512(bk, w, Bw, i):
        def mm():
            for dmc in range(8):
                ins = tn.matmul(ps[bk][:], w[:, dmc, :], hT[:, dmc, i * 512:(i + 1) * 512],
                                start=(dmc == 0), stop=(dmc == 7))
            return ins
        k.do(PE, [Bw] + BhT[4 * i:4 * i + 4], [Bps[bk]], mm)

    def phaseA_pair(hp):
        s = hp % 2
        k.dma(POOL, dbt, btab[:], bt_d[hp], writes=[Bbt])
        for i in range(NQ):
            bq = [0, 1][i % 2]
            proj512(bq, wq[s], Bwq[s], i)
            r_ = blocknorm([bq], 2, bo64)
            k.do(DVE, [Bps[bq], Brstd[r_], Bconst], [BQz[0][i]], lambda: ve.scalar_tensor_tensor(
                Qz[0][0:64, i * 512:(i + 1) * 512], ps[bq][0:64, :], gp[0:64, 8:9], rstd[r_][0:64, :], ALU.mult, ALU.mult))
            k.do(DVE, [Bps[bq], Brstd[r_], Bconst], [BQz[1][i]], lambda: ve.scalar_tensor_tensor(
                Qz[1][64:128, i * 512:(i + 1) * 512], ps[bq][64:128, :], gp[64:128, 8:9], rstd[r_][64:128, :], ALU.mult, ALU.mult))
            bk_ = [3, 4][i % 2]
            proj512(bk_, wk[s], Bwk[s], i)
            r_ = blocknorm([bk_], 5, bo64)
            k.do(DVE, [Bps[bk_], Brstd[r_], Bconst], [BKT[i]], lambda: ve.scalar_tensor_tensor(
                KT[:, i * 512:(i + 1) * 512], ps[bk_][:], gp[:, 9:10], rstd[r_][:], ALU.mult, ALU.mult))
        for i in range(NQ):
            bv = [6, 7][i % 2]
            proj512(bv, wv[s], Bwv[s], i)
            k.do(DVE, [Bps[bv]], [BVT[i]], lambda: ve.tensor_copy(VT[:, i * 512:(i + 1) * 512], ps[bv][:]))
        if hp + 1 < npairs_a:
            load_pair_weights(hp + 1)
        for pi, r in enumerate(PATTERNS):
            nb = NT // r
            for g8 in range(4):
                bk = [6, 7][g8 % 2]
                ptv_ = ps[bk][:].bitcast(BF16)

                def mmv():
                    for tt in range(8):
                        n = g8 * 8 + tt
                        b, m = n // nb, n % nb
                        t0 = 128 * m * r + b
                        ins = tn.transpose(ptv_[:, tt * 128:(tt + 1) * 128], VT[:, _ssl(t0, 128, r)], ident)
                    return ins
                k.do(PE, [Bconst] + BVT, [Bps[bk]], mmv)
                k.do(ACT, [Bps[bk]], [BVp[g8]], lambda: sc.activation(
                    out=Vp[:, g8 * 8:(g8 + 1) * 8, :].rearrange("p n (s d) -> p n s d", s=3)[:, :, 0:3:2, :],
                    in_=ptv_.rearrange("p (n s d) -> p n s d", n=8, s=2), func=AF.Copy))
            tiles = [(b, m) for b in range(r) for m in range(nb)]
            oset_ctr = [0]

            def emit_S(n):
                b, m = tiles[n]
                bk = [0, 1, 2][n % 3]
                nq = 256 if m < nb - 1 else 128
                t0 = 128 * m * r + b

                def mm():
                    if nq == 256:
                        o_ = ps[bk][:].rearrange("p (h q) -> p h q", h=2)[:, :, 0:nq]
                        tn.matmul(o_, KT[:, _ssl(t0, 128, r)], Qzz[:, :, _ssl(t0, nq, r)], start=True, stop=False)
                        return tn.matmul(o_, ident, btab[:, pi * 2:pi * 2 + 2, 0:nq], start=False, stop=True)
                    for h in range(2):
                        o_ = ps[bk][:, h * 256:h * 256 + nq]
                        tn.matmul(o_, KT[:, _ssl(t0, 128, r)], Qz[h][:, _ssl(t0, nq, r)], start=True, stop=False)
                        ins = tn.matmul(o_, ident, btab[:, pi * 2 + h, 0:nq], start=False, stop=True)
                    return ins
                qtiles = sorted(set([(t0) // 512, min((t0 + nq * r - 1) // 512, NQ - 1)]))
                ktile = sorted(set([t0 // 512, min((t0 + 128 * r - 1) // 512, NQ - 1)]))
                rd = [Bbt, Bconst] + [BKT[x] for x in range(ktile[0], ktile[-1] + 1)] \
                    + [BQz[h][x] for h in range(2) for x in range(qtiles[0], qtiles[-1] + 1)]
                return rd, [Bps[bk]], mm

            def emit_E(n):
                b, m = tiles[n]
                bk = [0, 1, 2][n % 3]
                nq = 256 if m < nb - 1 else 128
                pv = ps[bk][:].rearrange("p (h q) -> p h q", h=2)[:, :, 0:nq]
                sg = stg[n % 2][:].rearrange("p (h q) -> p h q", h=2)[:, :, 0:nq]
                ptv = PT[n % 4][:].rearrange("p (h q) -> p h q", h=2)[:, :, 0:nq]
                k.do(ACT, [Bps[bk]], [BPT[n % 4]], lambda: sc.activation(out=ptv, in_=pv, func=AF.Exp))

            def emit_PV(n):
                b, m = tiles[n]
                cs = (m % 4) * 128
                st_ = oset_ctr[0] % 2
                be, bo_ = (3, 4) if st_ == 0 else (5, 6)
                vt = b * nb + m

                ne, no_ = (5, 6) if st_ == 0 else (3, 4)
                wr = [Bps[be], Bps[bo_]]
                if m % 4 == 3 and m < nb - 1:
                    wr += [Bps[ne], Bps[no_]]

                def mm():
                    for h in range(2):
                        bank, nbank = (be, ne) if h == 0 else (bo_, no_)
                        lhs = Vp[:, vt, 0:128] if h == 0 else Vp[:, vt, 64:192]
                        p0 = h * 256
                        if m == nb - 1:
                            ins = tn.matmul(ps[bank][:, cs:cs + 128], lhs, PT[n % 4][:, p0:p0 + 128],
                                            start=False, stop=True, skip_group_check=True)
                        elif m % 4 == 3:
                            tn.matmul(ps[bank][:, cs:cs + 128], lhs, PT[n % 4][:, p0:p0 + 128],
                                      start=False, stop=True, skip_group_check=True)
                            ins = tn.matmul(ps[nbank][:, 0:128], lhs, PT[n % 4][:, p0 + 128:p0 + 256],
                                            start=True, stop=True, skip_group_check=True)
                        else:
                            ins = tn.matmul(ps[bank][:, cs:cs + 256], lhs, PT[n % 4][:, p0:p0 + 256],
                                            start=(m == 0), stop=True, skip_group_check=True)
                    return ins
                rd = [BPT[n % 4], BVp[vt // 8]]
                return rd, wr, mm, (lambda: post_PV(n, b, m, cs, be, bo_))

            def post_PV(n, b, m, cs, be, bo_):
                if m % 4 == 3 or m == nb - 1:
                    ncol = cs + 128
                    m0 = m - (m % 4)
                    a0 = 128 * m0 * r + b
                    sl = _ssl(a0, ncol, r)
                    if pi == 0:
                        k.do(DVE, [Bps[be]], [Bacc[0]], lambda: ve.tensor_copy(acc_e[0:65, sl], ps[be][0:65, 0:ncol]))
                        k.do(DVE, [Bps[bo_]], [Bacc[1]], lambda: ve.tensor_copy(acc_o[:, sl], ps[bo_][:, 0:ncol]))
                    else:
                        k.do(DVE, [Bps[be]], [Bacc[0]], lambda: ve.tensor_tensor(
                            acc_e[0:65, sl], ps[be][0:65, 0:ncol], acc_e[0:65, sl], ALU.add))
                        k.do(DVE, [Bps[bo_]], [Bacc[1]], lambda: ve.tensor_tensor(
                            acc_o[:, sl], ps[bo_][:, 0:ncol], acc_o[:, sl], ALU.add))
                    oset_ctr[0] += 1

            ntile = len(tiles)
            for n in range(ntile + 2):
                if n < ntile:
                    sS = emit_S(n)
                    k.do(PE, sS[0], sS[1], sS[2])
                    emit_E(n)
                if n >= 2:
                    sP = emit_PV(n - 2)
                    k.do(PE, sP[0], sP[1], sP[2])
                    sP[3]()
        k.do(ACT, [], [Bacc[0]], lambda: sc.activation(out=acc_e[64:65, :], in_=acc_e[64:65, :], func=AF.Ln))
        k.do(ACT, [], [Bacc[1]], lambda: sc.activation(out=acc_o[0:1, :], in_=acc_o[0:1, :], func=AF.Ln))
        for i in range(NQ):
            cs = slice(i * 512, (i + 1) * 512)
            bkf = [7, 2][i % 2]

            def mmf():
                tn.matmul(ps[bkf][:], cf2[:, 0, :], acc_e[:, cs], start=True, stop=False)
                return tn.matmul(ps[bkf][:], cf2[:, 1, :], acc_o[:, cs], start=False, stop=True)
            k.do(PE, [Bacc[0], Bacc[1], Bconst], [Bps[bkf]], mmf)
            t_ = tfin[i % 2]
            k.do(ACT, [Bps[bkf], Bconst], [Btfin[i % 2]], lambda: sc.activation(
                out=t_[:], in_=ps[bkf][:], func=AF.Exp, scale=-1.0, bias=lnhalf[:, 0:1]))
            k.do(DVE, [Bacc[0]], [Btfin[i % 2]], lambda: ve.tensor_tensor(
                t_[0:64, :], acc_e[0:64, cs], t_[0:64, :], ALU.mult))
            k.do(DVE, [Bacc[1]], [Btfin[i % 2]], lambda: ve.tensor_tensor(
                t_[64:128, :], acc_o[64:128, cs], t_[64:128, :], ALU.mult))
            k.do(POOL, [Btfin[i % 2]], [BmixA[hp][i]], lambda: gp_.tensor_tensor(
                mixA[:, hp, cs], t_[:], mixA[:, hp, cs], ALU.mult))

    npairs_a = 4 if stage != 3 else 1
    load_pair_weights(0)
    for hp in range(npairs_a):
        phaseA_pair(hp)
    if stage in (3, 4):
        return finish((mixA[:],))
    k.barrier()

    o = R0
    TT = A("TT", o, [128, S], F32)
    TTi = A("TTi", OFF_MIXB, [128, S], I32)
    o += 16384
    cqn = A("cqn", o, [128, 2, S], BF16)
    o += 16384
    ckvn = A("ckvn", o, [128, S], BF16)
    o += 8192
    kT = [A(f"kT{i}", o + i * 8192, [128, S], BF16) for i in range(2)]
    o += 16384
    OFF_W = o
    wb = [A(f"wb{i}", o + i * 2048, [128, 8, 128], BF16) for i in range(2)]
    o += 4096
    wstB = A("wstB", o, [128, 8, 128], F32)
    o += 4096
    sq2 = [A(f"sq2{i}", o + i * 1024, [128, 512], BF16) for i in range(2)]
    o += 2048
    ssb2 = A("ssb2", o, [128, 512], F32)
    o += 2048
    rstd2 = A("rstd2", o, [128, 512], F32)
    o += 2048
    tk = A("tk", o, [128, 512], F32)
    o += 2048
    ta = A("ta", OFF_W + 4096, [128, 512], F32)
    tu = A("tu", OFF_W + 6144, [128, 512], F32)
    assert SB_BASE + o <= SB_LIMIT, o
    sq[0], sq[1] = sq2[0], sq2[1]
    ssb[0] = ssb[1] = ssb2
    rstd[0] = rstd[1] = rstd2
    Bsq[0], Bsq[1] = Buf(), Buf()
    Bssb[0] = Bssb[1] = Buf()
    Brstd[0] = Brstd[1] = Buf()
    BTT = Buf()
    Bcqn = [Buf() for _ in range(NQ)]
    Bckvn = [Buf() for _ in range(NQ)]
    BkT = [[Buf() for _ in range(NQ)] for _ in range(2)]
    Bwb = [Buf(), Buf()]
    BwstB = Buf()
    Btk, Bta, Btu = Buf(), Buf(), Buf()
    dwb = k.dsem("d_wb")
    dpos = k.dsem("d_pos")

    k.do(POOL, [], [BTT], lambda: gp_.memset(TT[0:64, :], 1.0))
    k.dma(SP, dpos, TTi[64:128, :], bass.AP(pos_t, 0, [[0, 64], [1, S]]), writes=[BTT])
    k.do(DVE, [], [BTT], lambda: ve.tensor_copy(TT[64:128, :], TTi[64:128, :]))
    MAGIC = 12582912.0
    TWO_PI = 2.0 * math.pi
    for i in range(NQ):
        cs = slice(i * 512, (i + 1) * 512)
        k.do(DVE, [BTT, Bconst], [Bta], lambda: ve.tensor_scalar(
            ta[64:128, :], TT[64:128, cs], rc[64:128, 0:1], rc[64:128, 1:2], ALU.mult, ALU.add))
        k.do(DVE, [Bta], [Btu], lambda: ve.tensor_scalar(tu[64:128, :], ta[64:128, :], 1.0 / TWO_PI, MAGIC, ALU.mult, ALU.add))
        k.do(DVE, [], [Btu], lambda: ve.tensor_scalar(tu[64:128, :], tu[64:128, :], -MAGIC, None, ALU.add))
        k.do(DVE, [Bta], [Btu], lambda: ve.scalar_tensor_tensor(
            tu[64:128, :], tu[64:128, :], -TWO_PI, ta[64:128, :], ALU.mult, ALU.add))
        k.do(DVE, [], [Btu], lambda: ve.tensor_scalar(tu[64:128, :], tu[64:128, :], -math.pi, math.pi, ALU.max, ALU.min))
        k.do(ACT, [Btu, Bconst], [BTT], lambda: sc.activation(
            out=TT[64:128, cs], in_=tu[64:128, :], func=AF.Sin, scale=rc[64:128, 2:3]))
    if stage == 5:
        return finish((TT[:],))
    k.barrier()

    k.do(POOL, [], [Bsq[0]], lambda: gp_.memset(sq2[0][:], 0.0))
    k.do(POOL, [], [Bsq[1]], lambda: gp_.memset(sq2[1][:], 0.0))
    k.do(POOL, [], [Btk], lambda: gp_.memset(tk[:], 0.0))

    wc = [A(f"wc{i}", OFF_HT - 0 + 0, [1, 1], F32) for i in range(0)]

    def prep_chunk_weights(dst_i, src_ap, ncols):
        load_w_scaled(dwb, wstB, BwstB, wb[dst_i], Bwb[dst_i], src_ap, ncols)

    for i in range(NQ):
        cs = slice(i * 512, (i + 1) * 512)
        if i == 0:
            prep_chunk_weights(0, w_in_v[:, :, 2048:2176], 128)
            prep_chunk_weights(1, w_in_v[:, :, 2176:2304], 128)
        proj512(0, wb[0], Bwb[0], i)
        proj512(1, wb[1], Bwb[1], i)
        r_ = blocknorm([0, 1], 2, ones256)
        for c in range(2):
            k.do(DVE, [Bps[c], Brstd[r_], Bconst], [Bcqn[i]], lambda c=c: ve.scalar_tensor_tensor(
                cqn[:, c, cs], ps[c][:], gp[:, 10 + c:11 + c], rstd[r_][:], ALU.mult, ALU.mult))
    if stage == 61:
        return finish((cqn[:],))
    for i in range(NQ):
        cs = slice(i * 512, (i + 1) * 512)
        if i == 0:
            prep_chunk_weights(0, w_in_v[:, :, 2304:2432], 128)
        bk = [0, 1][i % 2]
        proj512(bk, wb[0], Bwb[0], i)
        r_ = blocknorm([bk], 2, ones128)
        k.do(DVE, [Bps[bk], Brstd[r_], Bconst], [Bckvn[i]], lambda: ve.scalar_tensor_tensor(
            ckvn[:, cs], ps[bk][:], gp[:, 12:13], rstd[r_][:], ALU.mult, ALU.mult))
    if stage == 62:
        return finish((cqn[:],))
    for i in range(NQ):
        cs = slice(i * 512, (i + 1) * 512)
        if i == 0:
            prep_chunk_weights(1, w_kr_v, 128)
        bk = [3, 4][i % 2]

        def mm():
            for dmc in range(8):
                ins = tn.matmul(ps[bk][:], wb[1][:, dmc, :], hT[:, dmc, cs],
                                start=(dmc == 0), stop=(dmc == 7))
            return ins
        k.do(PE, [Bwb[1]] + BhT[4 * i:4 * i + 4], [Bps[bk]], mm)
        s_ = i % 2
        k.do(ACT, [Bps[bk]], [Bsq[s_]], lambda: sc.activation(out=sq2[s_][64:128, :], in_=ps[bk][64:128, :], func=AF.Square))
        k.do(PE, [Bsq[s_], Bconst], [Bps[5]], lambda: tn.matmul(ps[5][:], bo_q, sq2[s_][:], start=True, stop=True))
        k.do(ACT, [Bps[5], Bconst], [Bssb[0]], lambda: sc.activation(out=ssb2[:], in_=ps[5][:], func=AF.Ln, bias=epsT[:, 0:1]))
        k.do(ACT, [Bssb[0]], [Brstd[0]], lambda: sc.activation(out=rstd2[:], in_=ssb2[:], func=AF.Exp, scale=-0.5))
        k.do(DVE, [Bps[bk], BTT, Bconst], [Btk], lambda: ve.scalar_tensor_tensor(
            tk[64:128, :], ps[bk][64:128, :], gp[64:128, 14:15], TT[64:128, cs], ALU.mult, ALU.mult))
        k.do(DVE, [Brstd[0]], [Btk], lambda: ve.tensor_tensor(tk[64:128, :], tk[64:128, :], rstd2[64:128, :], ALU.mult))
        k.do(PE, [Btk, Bconst], [Bps[6]], lambda: tn.matmul(ps[6][:], msum, tk[:], start=True, stop=True))
        k.do(ACT, [Bps[6]], [BkT[0][i]], lambda: sc.activation(out=kT[0][64:128, cs], in_=ps[6][64:128, :], func=AF.Copy))
        k.do(ACT, [Bps[6]], [BkT[1][i]], lambda: sc.activation(out=kT[1][64:128, cs], in_=ps[6][64:128, :], func=AF.Copy))
        if stage == 60 and i == 0:
            dd = A("dd", OFF_MIXB, [128, 5, 512], F32)
            Bdd = Buf()
            k.do(POOL, [], [Bdd], lambda: gp_.memset(dd[:], 0.0))
            k.do(DVE, [Bps[bk]], [Bdd], lambda: ve.tensor_copy(dd[64:128, 0, :], ps[bk][64:128, :]))
            k.do(DVE, [Bssb[0]], [Bdd], lambda: ve.tensor_copy(dd[:, 1, :], ssb2[:]))
            k.do(DVE, [Brstd[0]], [Bdd], lambda: ve.tensor_copy(dd[:, 2, :], rstd2[:]))
            k.do(DVE, [Btk], [Bdd], lambda: ve.tensor_copy(dd[:, 3, :], tk[:]))
            k.do(DVE, [Bps[6]], [Bdd], lambda: ve.tensor_copy(dd[:, 4, :], ps[6][:]))
            return finish((dd[:],))
    if stage == 63:
        return finish((cqn[:],))
    gates(2464, mixB, BmixB, wb, Bwb, [ssb2, rstd2], [Bssb[0], Brstd[0]], [0, 1, 2], wstB, BwstB, dwb)
    if stage == 6:
        return finish((kT[0][:],))
    if stage == 7:
        return finish((cqn[:],))
    k.barrier()

    o = OFF_HT
    qT = [A(f"qT{i}", o + i * 8192, [128, S], BF16) for i in range(2)]
    o += 16384
    Vb = [A(f"Vb{i}", o + i * 12288, [128, 32, 192], BF16) for i in range(2)]
    o += 24576
    PTb = [A(f"PTb{i}", o + i * 1024, [128, 512], BF16) for i in range(4)]
    o += 4096
    tq = [A(f"tq{i}", o + i * 2048, [128, 512], F32) for i in range(2)]
    o += 4096
    bcs = [A(f"bcs{i}", o + i * 2048, [128, 512], F32) for i in range(2)]
    o += 4096
    RtB = A("RtB", o, [128, 512], F32)
    o += 2048
    tfb = [A(f"tfb{i}", o + i * 2048, [128, 512], F32) for i in range(2)]
    o += 4096
    assert o <= OFF_R
    wuq = A("wuq", OFF_W, [128, 2, 1024], BF16)
    wukk = A("wukk", OFF_W + 4096, [128, 576], BF16)
    wukv = A("wukv", OFF_W + 5248, [128, 512], BF16)
    BqT = [[Buf() for _ in range(NQ)] for _ in range(2)]
    BVb = [[Buf() for _ in range(8)] for _ in range(2)]
    BPTb = [Buf() for _ in range(4)]
    Btq = [Buf(), Buf()]
    Bbcs = [Buf(), Buf()]
    BRtB = Buf()
    Btfb = [Buf(), Buf()]
    Bwm = Buf()
    dwm = k.dsem("d_wm")
    k.dma(POOL, dwm, wuq[:], w_uq_v, writes=[Bwm])
    k.dma(POOL, dwm, wukk[:], w_ukvk_d, writes=[Bwm])
    k.dma(POOL, dwm, wukv[:], w_ukvv_d, writes=[Bwm])
    Bwm.w = (dwm, dwm.cnt)
    for s_ in range(2):
        k.do(POOL, [], [BVb[s_][0]], lambda: gp_.memset(Vb[s_][:, :, 64:128], 1.0))
        for g4 in range(1, 8):
            BVb[s_][g4].w = BVb[s_][0].w
    k.do(POOL, [], [BRtB], lambda: gp_.memset(RtB[:], 0.0))
    k.do(POOL, [], [Bsq[0]], lambda: gp_.memset(sq2[0][:], 0.0))
    k.do(POOL, [], [Bsq[1]], lambda: gp_.memset(sq2[1][:], 0.0))

    def phaseB_prep(hp):
        vs = hp % 2
        for g4 in range(8):
            bk = [6, 7][g4 % 2]

            def mmv():
                for tt in range(4):
                    n = g4 * 4 + tt
                    ins = tn.matmul(ps[bk][:, tt * 128:(tt + 1) * 128], ckvn[:, n * 128:(n + 1) * 128],
                                    wukv[:, hp * 128:(hp + 1) * 128], start=True, stop=True)
                return ins
            k.do(PE, [Bwm, Bckvn[g4 // 1 * 0 + (g4 * 4) // 4 // 1 if False else (g4 * 512) // 512]], [Bps[bk]], mmv)
            k.do(DVE, [Bps[bk]], [BVb[vs][g4]], lambda: ve.tensor_copy(
                Vb[vs][:, g4 * 4:(g4 + 1) * 4, :].rearrange("p n (s d) -> p n s d", s=3)[:, :, 0:3:2, :],
                ps[bk][:].rearrange("p (n s d) -> p n s d", n=4, s=2)))
        for hh in range(2):
            h = 2 * hp + hh
            for i in range(NQ):
                cs = slice(i * 512, (i + 1) * 512)
                bq = [0, 1][i % 2]

                def mmq():
                    for c in range(2):
                        ins = tn.matmul(ps[bq][:], wuq[:, c, h * 128:(h + 1) * 128], cqn[:, c, cs],
                                        start=(c == 0), stop=(c == 1))
                    return ins
                k.do(PE, [Bwm, Bcqn[i]], [Bps[bq]], mmq)
                s_ = i % 2
                k.do(ACT, [Bps[bq]], [Bsq[s_]], lambda: sc.activation(out=sq2[s_][:], in_=ps[bq][:], func=AF.Square))
                k.do(PE, [Bsq[s_], Bconst], [Bps[2]], lambda: tn.matmul(ps[2][:], bo_q, sq2[s_][:], start=True, stop=True))
                k.do(ACT, [Bps[2], Bconst], [Bssb[0]], lambda: sc.activation(out=ssb2[:], in_=ps[2][:], func=AF.Ln, bias=epsT[:, 0:1]))
                k.do(ACT, [Bssb[0]], [Brstd[0]], lambda: sc.activation(out=rstd2[:], in_=ssb2[:], func=AF.Exp, scale=-0.5))
                t_ = tq[i % 2]
                k.do(DVE, [Bps[bq], BTT, Bconst], [Btq[i % 2]], lambda: ve.scalar_tensor_tensor(
                    t_[:], ps[bq][:], gp[:, 13:14], TT[:, cs], ALU.mult, ALU.mult))
                k.do(POOL, [Btq[i % 2], Brstd[0]], [BqT[hh][i]], lambda: gp_.tensor_tensor(
                    qT[hh][:, cs], t_[:], rstd2[:], ALU.mult))
            k.do(POOL, [], [Bsq[0]], lambda: gp_.memset(sq2[0][64:128, :], 0.0))
            k.do(POOL, [], [Bsq[1]], lambda: gp_.memset(sq2[1][64:128, :], 0.0))
            for i in range(NQ):
                cs = slice(i * 512, (i + 1) * 512)
                bk_ = [3, 4][i % 2]
                k.do(PE, [Bwm, Bckvn[i]], [Bps[bk_]], lambda: tn.matmul(
                    ps[bk_][:], wukk[:, h * 64:h * 64 + 128], ckvn[:, cs], start=True, stop=True))
                s_ = i % 2
                k.do(ACT, [Bps[bk_]], [Bsq[s_]], lambda: sc.activation(out=sq2[s_][0:64, :], in_=ps[bk_][0:64, :], func=AF.Square))
                k.do(PE, [Bsq[s_], Bconst], [Bps[5]], lambda: tn.matmul(ps[5][:], bo_q, sq2[s_][:], start=True, stop=True))
                k.do(ACT, [Bps[5], Bconst], [Bssb[0]], lambda: sc.activation(
                    out=ssb2[0:64, :], in_=ps[5][0:64, :], func=AF.Ln, bias=epsT[0:64, 0:1]))
                k.do(ACT, [Bssb[0]], [Brstd[0]], lambda: sc.activation(out=rstd2[0:64, :], in_=ssb2[0:64, :], func=AF.Exp, scale=-0.5))
                k.do(DVE, [Bps[bk_], Brstd[0], Bconst], [BkT[hh][i]], lambda: ve.scalar_tensor_tensor(
                    kT[hh][0:64, cs], ps[bk_][0:64, :], gp[0:64, 14:15], rstd2[0:64, :], ALU.mult, ALU.mult))

    def phaseB_attn(hp):
        vs = hp % 2
        tiles = []
        for i in range(NQ):
            for hh in range(2):
                for j in range(4 * i + 4):
                    tiles.append((i, hh, j))
        ntile = len(tiles)

        def emit_S(n):
            i, hh, j = tiles[n]
            bk = [0, 1, 2][n % 3]
            jj = j - 4 * i
            c0 = 128 * jj if jj > 0 else 0

            def mm():
                o_ = ps[bk][:, c0:512]
                ins = tn.matmul(o_, kT[hh][:, j * 128:(j + 1) * 128], qT[hh][:, i * 512 + c0:(i + 1) * 512],
                                start=True, stop=(jj < 0))
                if jj >= 0:
                    ins = tn.matmul(ps[bk][:, c0:c0 + 128], ident, causal, start=False, stop=True)
                return ins
            return [Bconst, BkT[hh][j // 4], BqT[hh][i]], [Bps[bk]], mm

        def emit_E(n):
            i, hh, j = tiles[n]
            bk = [0, 1, 2][n % 3]
            jj = j - 4 * i
            c0 = 128 * jj if jj > 0 else 0
            k.do(ACT, [Bps[bk]], [BPTb[n % 4]], lambda: sc.activation(
                out=PTb[n % 4][:, c0:512], in_=ps[bk][:, c0:512], func=AF.Exp))

        def emit_PV(n):
            i, hh, j = tiles[n]
            jj = j - 4 * i
            c0 = 128 * jj if jj > 0 else 0
            bank = [[3, 5], [4, 6]][hh][i % 2]
            lhs = Vb[vs][:, j, 0:128] if hh == 0 else Vb[vs][:, j, 64:192]
            last = (j == 4 * i + 3)
            return [BPTb[n % 4], BVb[vs][j // 4]], [Bps[bank]], (lambda: tn.matmul(
                ps[bank][:, c0:512], lhs, PTb[n % 4][:, c0:512], start=(j == 0), stop=last)), (last and hh == 1), i

        def finalize(i):
            cs = slice(i * 512, (i + 1) * 512)
            be, bo_ = [3, 5][i % 2], [4, 6][i % 2]
            k.do(ACT, [Bps[be]], [BRtB], lambda: sc.activation(out=RtB[64:65, :], in_=ps[be][64:65, :], func=AF.Ln))
            k.do(ACT, [Bps[bo_]], [BRtB], lambda: sc.activation(out=RtB[0:1, :], in_=ps[bo_][0:1, :], func=AF.Ln))
            k.do(PE, [BRtB, Bconst], [Bps[7]], lambda: tn.matmul(ps[7][:], sel, RtB[:], start=True, stop=True))
            b_ = bcs[i % 2]
            k.do(ACT, [Bps[7], Bconst], [Bbcs[i % 2]], lambda: sc.activation(
                out=b_[:], in_=ps[7][:], func=AF.Exp, scale=-1.0, bias=lnhalf[:, 0:1]))
            t_ = tfb[i % 2]
            k.do(DVE, [Bps[be], Bbcs[i % 2]], [Btfb[i % 2]], lambda: ve.tensor_tensor(
                t_[0:64, :], ps[be][0:64, :], b_[0:64, :], ALU.mult))
            k.do(DVE, [Bps[bo_], Bbcs[i % 2]], [Btfb[i % 2]], lambda: ve.tensor_tensor(
                t_[64:128, :], ps[bo_][64:128, :], b_[64:128, :], ALU.mult))
            k.do(POOL, [Btfb[i % 2]], [BmixB[hp][i]], lambda: gp_.tensor_tensor(
                mixB[:, hp, cs], t_[:], mixB[:, hp, cs], ALU.mult))

        for n in range(ntile + 2):
            if n < ntile:
                sS = emit_S(n)
                k.do(PE, sS[0], sS[1], sS[2])
                emit_E(n)
            if n >= 2:
                sP = emit_PV(n - 2)
                k.do(PE, sP[0], sP[1], sP[2])
                if sP[3]:
                    finalize(sP[4])

    npairs_b = 4 if stage != 8 else 1
    for hp in range(npairs_b):
        phaseB_prep(hp)
        phaseB_attn(hp)
    if stage in (8, 9):
        return finish((mixB[:],))
    k.barrier()

    o = OFF_HT
    wo = A("wo", o, [128, 8, 1024], BF16)
    o += 16384
    xo = [A(f"xo{i}", o + i * 4096, [128, D], F32) for i in range(3)]
    o += 12288
    yo = [A(f"yo{i}", o + i * 4096, [128, D], F32) for i in range(3)]
    o += 12288
    Bwo = Buf()
    Bxo = [Buf() for _ in range(3)]
    Byo = [Buf() for _ in range(3)]
    dwo = k.dsem("d_wo")
    dxo = [k.dsem(f"d_xo{i}") for i in range(3)]
    dyo = [k.dsem(f"d_yo{i}") for i in range(3)]
    for c in range(8):
        k.dma(POOL, dwo, wo[:, c, :], w_out_v[:, c, :], writes=[Bwo])
    Bwo.w = (dwo, dwo.cnt)
    y_v = y_d.rearrange("(n p) d -> n p d", p=128)

    def p5_load(j):
        k.dma(SP, dxo[j % 3], xo[j % 3][:], x_v[j], writes=[Bxo[j % 3]])

    p5_load(0)
    p5_load(1)
    for j in range(NT):
        if j + 2 < NT:
            p5_load(j + 2)
        ts = slice(j * 128, (j + 1) * 128)
        for half in range(2):
            bk = [0, 1, 2, 3][(2 * j + half) % 4]

            def mm():
                for c in range(8):
                    src = mixA if c < 4 else mixB
                    ins = tn.matmul(ps[bk][:], src[:, c % 4, ts], wo[:, c, half * 512:(half + 1) * 512],
                                    start=(c == 0), stop=(c == 7))
                return ins
            k.do(PE, [Bwo] + [BmixA[c][j // 4] for c in range(4)] + [BmixB[c][j // 4] for c in range(4)], [Bps[bk]], mm)
            k.do(DVE, [Bps[bk], Bxo[j % 3]], [Byo[j % 3]], lambda: ve.tensor_tensor(
                yo[j % 3][:, half * 512:(half + 1) * 512], ps[bk][:], xo[j % 3][:, half * 512:(half + 1) * 512], ALU.add))
        k.dma(SP, dyo[j % 3], y_v[j], yo[j % 3][:], reads=[Byo[j % 3]])
    k.barrier()
    k.es.close()
    return nc


def _t5_bucket(dist):
    d = np.maximum(dist.astype(np.float32), np.float32(1.0))
    large = 16 + (np.log(d / np.float32(16.0)) / np.float32(math.log(2048 / 16)) * np.float32(16.0)).astype(np.int32)
    large = np.minimum(large, 31)
    return np.where(dist < 16, dist, large)


def _consts():
    cbf = np.zeros((128, 6, 128), np.float32)
    p = np.arange(128)
    cbf[:, 0, :] = np.eye(128, dtype=np.float32)
    cbf[:, 1, :] = ((p[:, None] // 64) == (p[None, :] // 64)) / 64.0
    blk = np.where(p < 64, 0, np.where(p < 96, 1, 2))
    wgt = np.where(p < 64, 1.0 / 64, 1.0 / 32)
    cbf[:, 2, :] = (blk[:, None] == blk[None, :]) * wgt[None, :]
    cbf[:, 3, :] = 1.0 / 256
    cbf[:, 4, :] = 1.0 / 128
    cbf[:, 5, :] = np.where(p[None, :] >= p[:, None], 0.0, MASKV)
    cf = np.zeros((128, 2, 128), np.float32)
    cf[64, 0, 0:64] = 1.0
    cf[0, 0, 64:128] = 1.0
    for j in range(32):
        for a in (64 + j, 96 + j):
            for b in (64 + j, 96 + j):
                cf[a, 1, b] = 1.0
    cf2 = np.zeros((128, 2, 128), np.float32)
    cf2[64, 0, 0:64] = 1.0
    cf2[0, 1, 64:128] = 1.0
    rc = np.zeros((128, 4), np.float32)
    inv_freq = (np.float32(10000.0) ** (-np.arange(0, 32, 2, dtype=np.float32) / np.float32(32))).astype(np.float32)
    for rrow in range(64):
        prt = 64 + rrow
        rc[prt, 0] = inv_freq[rrow % 16]
        rc[prt, 1] = (math.pi / 2) if rrow < 32 else 0.0
        rc[prt, 2] = -1.0 if 32 <= rrow < 48 else 1.0
    return cbf, cf, cf2, rc


def _bias_tables(rel_bias):
    ki = np.arange(128)[:, None]
    c = np.arange(256)[None, :]
    j = c - ki
    valid = (j >= 0) & (j <= 128)
    bt = np.empty((4, 128, 6, 256), np.float32)
    for pi, r in enumerate(PATTERNS):
        bucket = _t5_bucket(np.maximum(j, 0) * r)
        for h in range(8):
            tab = np.where(valid, rel_bias[bucket, h], np.float32(MASKV)).astype(np.float32)
            bt[h // 2, :, pi * 2 + (h % 2), :] = tab
    return bt


_NC_CACHE = {}


def _prep_shared(inputs):
    f = lambda n: np.asarray(inputs[n], np.float32)
    w_in = np.ascontiguousarray(f("w_in")[0])
    kr = w_in[:, 2432:2464]
    w_kr = np.ascontiguousarray(np.concatenate([np.zeros((D, 64), np.float32), kr, kr[:, 16:32], kr[:, 0:16]], axis=1))
    w_uq = f("w_uq")[0]
    w_uq_ext = np.empty((256, 1024), np.float32)
    for h in range(8):
        blk = w_uq[:, h * 96:(h + 1) * 96]
        w_uq_ext[:, h * 128:h * 128 + 96] = blk
        w_uq_ext[:, h * 128 + 96:h * 128 + 112] = blk[:, 80:96]
        w_uq_ext[:, h * 128 + 112:h * 128 + 128] = blk[:, 64:80]
    w_ukv = f("w_ukv")[0].reshape(128, 8, 128)
    w_ukvk = np.ascontiguousarray(np.concatenate([w_ukv[:, :, 0:64].reshape(128, 512), np.zeros((128, 64), np.float32)], axis=1))
    w_ukvv = np.ascontiguousarray(w_ukv[:, :, 64:128].reshape(128, 512))
    w_out = np.ascontiguousarray(f("w_out")[0])
    gp = np.zeros((128, 16), np.float32)
    gp[:, 0:8] = f("norm_gain")[0].reshape(8, 128).T
    gp[:, 8] = np.tile(f("a_q_gain")[0], 2)
    gp[:, 9] = np.tile(f("a_k_gain")[0], 2)
    gp[:, 10:12] = f("q_c_gain")[0].reshape(2, 128).T
    gp[:, 12] = f("kv_c_gain")[0]
    qr = f("qr_gain")[0]
    kr_g = f("kr_gain")[0]
    gp[:, 13] = np.concatenate([f("qn_gain")[0], qr, qr[16:32], qr[0:16]])
    gp[:, 14] = np.concatenate([f("kn_gain")[0], kr_g, kr_g[16:32], kr_g[0:16]])
    cbf, cf, cf2, rc = _consts()
    bt = _bias_tables(f("rel_bias"))
    return dict(w_in=w_in, w_kr=w_kr, w_uq=w_uq_ext, w_ukvk=w_ukvk, w_ukvv=w_ukvv, w_out=w_out,
                cbf=cbf, cf=cf, cf2=cf2, gp=gp, rc=rc, bt=bt)


def kernel(**inputs):
    x = np.asarray(inputs["x"], np.float32)
    pos = np.asarray(inputs["positions"], np.int32)
    shared = _prep_shared(inputs)
    if "nc" not in _NC_CACHE:
        _NC_CACHE["nc"] = build()
    nc = _NC_CACHE["nc"]
    in_maps = []
    for b in range(8):
        m = dict(shared)
        m["x"] = np.ascontiguousarray(x[b])
        m["pos"] = np.ascontiguousarray(pos[b].reshape(1, S))
        in_maps.append(m)
    res = run_bass_kernel_spmd(nc, in_maps, core_ids=list(range(8)))
    return np.stack([np.asarray(r["y"], np.float32) for r in res.results], axis=0)
```

```python
import contextlib
import math
import numpy as np
import concourse.bass as bass
import concourse.mybir as mybir
from concourse.bass_utils import run_bass_kernel_spmd

F32 = mybir.dt.float32
BF16 = mybir.dt.bfloat16
I32 = mybir.dt.int32
AF = mybir.ActivationFunctionType
ALU = mybir.AluOpType

S = 4096
D = 1024
NT = 32
NQ = 8
NCOLS = 2976
MASKV = -30000.0
EPS = 1e-6
PATTERNS = (1, 4, 16)
QSCALE_A = 64 ** -0.5
QSCALE_B = 96 ** -0.5
SB_BASE = 16512
SB_LIMIT = 229344


def _ssl(t0, n, r):
    return slice(t0, t0 + (n - 1) * r + 1, r)


class Sem:
    def __init__(self, k, name):
        self.name = name
        self.sem = k.es.enter_context(k.nc.semaphore(name))
        self.cnt = 0


class Eng(Sem):
    def __init__(self, k, name, eng):
        super().__init__(k, "s_" + name)
        self.eng = eng
        self.seen = {}

    def wait(self, *toks):
        for t in toks:
            if t is None:
                continue
            if isinstance(t, list):
                self.wait(*t)
                continue
            p, c = t
            if self.seen.get(p.name, 0) >= c:
                continue
            self.eng.wait_ge(p.sem, c)
            self.seen[p.name] = c

    def done(self, ins):
        ins.then_inc(self.sem, 1)
        self.cnt += 1
        return (self, self.cnt)


class Buf:
    def __init__(self, name="", excl=False):
        self.name = name
        self.excl = excl
        self.w = None
        self.r = {}

    def add_r(self, tok):
        p, c = tok
        if self.r.get(p.name, (None, 0))[1] < c:
            self.r[p.name] = tok

    def set_w(self, tok):
        self.w = tok
        self.r = {}


class K:
    def __init__(self):
        self.nc = bass.Bass("TRN2", target_bir_lowering=False)
        self.es = contextlib.ExitStack()
        nc = self.nc
        self.PE = Eng(self, "pe", nc.tensor)
        self.ACT = Eng(self, "act", nc.scalar)
        self.DVE = Eng(self, "dve", nc.vector)
        self.POOL = Eng(self, "pool", nc.gpsimd)
        self.SP = Eng(self, "sp", nc.sync)
        self.engines = [self.PE, self.ACT, self.DVE, self.POOL, self.SP]
        self.dsems = []

    def dsem(self, name):
        d = Sem(self, name)
        self.dsems.append(d)
        return d

    def wait_for(self, E, reads, writes):
        for b in reads:
            E.wait(b.w)
            if b.excl:
                E.wait(*[t for n, t in b.r.items() if n != E.name])
        for b in writes:
            E.wait(b.w)
            E.wait(*b.r.values())

    def commit(self, E, ins, reads, writes):
        tok = E.done(ins)
        for b in reads:
            b.add_r(tok)
        for b in writes:
            b.set_w(tok)
        return tok

    def do(self, E, reads, writes, emit):
        self.wait_for(E, reads, writes)
        ins = emit()
        return self.commit(E, ins, reads, writes)

    def dma(self, Q, ds, out, in_, reads=(), writes=()):
        for b in reads:
            Q.wait(b.w)
        for b in writes:
            Q.wait(b.w)
            Q.wait(*b.r.values())
        Q.eng.dma_start(out=out, in_=in_).then_inc(ds.sem, 16)
        ds.cnt += 16
        tok = (ds, ds.cnt)
        for b in reads:
            b.add_r(tok)
        for b in writes:
            b.set_w(tok)
        return tok

    def barrier(self):
        for E in self.engines:
            for F in self.engines:
                if F is not E and F.cnt > 0:
                    E.wait((F, F.cnt))
            for d in self.dsems:
                if d.cnt > 0:
                    E.wait((d, d.cnt))


def build(stage=99, dbg=None):
    k = K()
    nc = k.nc
    PE, ACT, DVE, POOL, SP = k.PE, k.ACT, k.DVE, k.POOL, k.SP
    tn, sc, ve, gp_, sy = nc.tensor, nc.scalar, nc.vector, nc.gpsimd, nc.sync

    def dram(name, shape, dt, kind="ExternalInput"):
        return nc.dram_tensor(name, shape, dt, kind=kind)

    x_d = dram("x", [S, D], F32).ap()
    pos_t = dram("pos", [1, S], I32)
    w_in_d = dram("w_in", [D, NCOLS], F32).ap()
    w_kr_d = dram("w_kr", [D, 128], F32).ap()
    w_uq_d = dram("w_uq", [256, 1024], F32).ap()
    w_ukvk_d = dram("w_ukvk", [128, 576], F32).ap()
    w_ukvv_d = dram("w_ukvv", [128, 512], F32).ap()
    w_out_d = dram("w_out", [D, D], F32).ap()
    cbf_d = dram("cbf", [128, 6, 128], F32).ap()
    cf_d = dram("cf", [128, 2, 128], F32).ap()
    cf2_d = dram("cf2", [128, 2, 128], F32).ap()
    gp_d = dram("gp", [128, 16], F32).ap()
    rc_d = dram("rc", [128, 4], F32).ap()
    bt_d = dram("bt", [4, 128, 6, 256], F32).ap()
    y_d = dram("y", [S, D], F32, kind="ExternalOutput").ap()
    dbg_d = None
    if dbg is not None:
        dbg_d = dram("dbg", list(dbg[1]), dbg[2], kind="ExternalOutput").ap()

    w_in_v = w_in_d.rearrange("(c p) n -> p c n", p=128)
    w_kr_v = w_kr_d.rearrange("(c p) n -> p c n", p=128)
    w_uq_v = w_uq_d.rearrange("(c p) n -> p c n", p=128)
    w_out_v = w_out_d.rearrange("(c p) n -> p c n", p=128)

    def A(name, off, shape, dt):
        sz = int(np.prod(shape[1:])) * (4 if dt in (F32, I32) else 2)
        assert off % 32 == 0 and SB_BASE + off + sz <= SB_LIMIT, (name, off, sz)
        return nc.alloc_sbuf_tensor_at(name, list(shape), dt, offset=SB_BASE + off)

    cbf = A("cbf", 0, [128, 6, 128], BF16)
    cf = A("cf", 1536, [128, 2, 128], F32)
    gp = A("gp", 2560, [128, 16], F32)
    rc = A("rc", 2624, [128, 4], F32)
    st_ss = A("st_ss", 2656, [128, 4], F32)
    st_t = A("st_t", 2688, [128, 4], F32)
    st_rs = A("st_rs", 2720, [128, 4], F32)
    cm05 = A("cm05", 2752, [128, 1], F32)
    epsT = A("epsT", 2784, [128, 1], F32)
    lnhalf = A("lnhalf", 2816, [128, 1], F32)
    cf2 = A("cf2", 2880, [128, 2, 128], F32)
    OFF_MIXA, OFF_MIXB, OFF_HT, OFF_R = 4096, 36864, 69632, 135168
    mixA = A("mixA", OFF_MIXA, [128, 4, S], BF16)
    mixB = A("mixB", OFF_MIXB, [128, 4, S], BF16)
    acc_e = A("acc_e", OFF_MIXB, [128, S], F32)
    acc_o = A("acc_o", OFF_MIXB + 16384, [128, S], F32)
    hT = A("hT", OFF_HT, [128, 8, S], BF16)
    BhT = [Buf() for _ in range(NT)]
    BmixA = [[Buf() for _ in range(NQ)] for _ in range(4)]
    BmixB = [[Buf() for _ in range(NQ)] for _ in range(4)]

    ident = cbf[:, 0, :]
    bo64 = cbf[:, 1, :]
    bo_q = cbf[:, 2, :]
    ones256 = cbf[:, 3, :]
    ones128 = cbf[:, 4, :]
    causal = cbf[:, 5, :]
    sel = cf[:, 0, :]
    msum = cf[:, 1, :]

    ps = [k.es.enter_context(nc.psum_tensor(f"ps{i}", [128, 512], F32)) for i in range(8)]
    Bps = [Buf(excl=True) for _ in range(8)]

    dconst = k.dsem("d_const")
    Bconst = Buf()

    dconst2 = k.dsem("d_const2")
    Bcbf = Buf()
    k.dma(POOL, dconst2, cbf[:], cbf_d, writes=[Bcbf])
    k.dma(SP, dconst, cf[:], cf_d, writes=[Bconst])
    k.dma(SP, dconst, cf2[:], cf2_d, writes=[Bconst])
    k.dma(SP, dconst, gp[:], gp_d, writes=[Bconst])
    k.dma(SP, dconst, rc[:], rc_d, writes=[Bconst])
    Bconst.w = (dconst, dconst.cnt)
    k.do(DVE, [], [Bconst], lambda: ve.tensor_scalar(gp[:, 8:9], gp[:, 8:9], QSCALE_A, None, ALU.mult))
    k.do(DVE, [], [Bconst], lambda: ve.tensor_scalar(gp[:, 13:14], gp[:, 13:14], QSCALE_B, None, ALU.mult))
    k.do(POOL, [], [Bconst], lambda: gp_.memset(cm05[:], -0.5))
    k.do(POOL, [], [Bconst], lambda: gp_.memset(epsT[:], EPS))
    k.do(POOL, [], [Bconst], lambda: gp_.memset(lnhalf[:], math.log(0.5)))
    k.barrier()

    dout = k.dsem("d_out")

    def finish(dump=None):
        if dump is not None:
            k.barrier()
            src, = dump
            k.dma(SP, dout, dbg_d, src)
        k.barrier()
        SP.wait((dout, dout.cnt)) if dout.cnt else None
        k.es.close()
        return nc

    def load_w_scaled(dsm, stage, Bstage, dst, Bdst, src_ap, ncols):
        k.dma(SP, dsm, stage[:, :, 0:ncols], src_ap, writes=[Bstage])
        k.do(POOL, [Bstage, Bconst], [Bdst], lambda: gp_.tensor_tensor(
            dst[:, :, 0:ncols], stage[:, :, 0:ncols],
            gp[:, 0:8].unsqueeze(2).to_broadcast([128, 8, ncols]), ALU.mult))

    R0 = OFF_R
    xbuf = [A(f"xbuf{i}", R0 + i * 4096, [128, D], F32) for i in range(3)]
    xn = [A(f"xn{i}", R0 + 12288 + i * 2048, [128, D], BF16) for i in range(2)]
    junk = A("junk", R0 + 16384, [128, D], BF16)
    Bx = [Buf() for _ in range(3)]
    Bxn = [Buf() for _ in range(2)]
    Bjunk = Buf()
    Bst = [Buf() for _ in range(2)]
    dx = [k.dsem(f"d_x{i}") for i in range(3)]
    x_v = x_d.rearrange("(n p) d -> n p d", p=128)

    def ph1_load(j):
        k.dma(SP, dx[j % 3], xbuf[j % 3][:], x_v[j], writes=[Bx[j % 3]])

    def ph1_a(j):
        s = j % 2
        k.do(ACT, [Bx[j % 3]], [Bjunk, Bst[s]], lambda: sc.activation(
            out=junk[:], in_=xbuf[j % 3][:], func=AF.Square, accum_out=st_ss[:, s:s + 1]))
        k.do(DVE, [], [Bst[s]], lambda: ve.tensor_scalar(
            st_t[:, s:s + 1], st_ss[:, s:s + 1], 1.0 / D, EPS, ALU.mult, ALU.add))
        k.do(POOL, [Bconst], [Bst[s]], lambda: gp_.tensor_tensor(
            st_rs[:, s:s + 1], st_t[:, s:s + 1], cm05[:], ALU.pow))
        k.do(DVE, [Bx[j % 3], Bst[s]], [Bxn[s]], lambda: ve.tensor_scalar(
            xn[s][:], xbuf[j % 3][:], st_rs[:, s:s + 1], None, ALU.mult))

    def ph1_b(j):
        s = j % 2
        pt = ps[s][:].bitcast(BF16)

        def tr():
            for c in range(8):
                ins = tn.transpose(pt[:, c * 128:(c + 1) * 128], xn[s][:, c * 128:(c + 1) * 128], ident)
            return ins
        k.do(PE, [Bxn[s], Bconst], [Bps[s]], tr)
        if j % 2 == 0:
            k.do(ACT, [Bps[s]], [BhT[j]], lambda: sc.activation(
                out=hT[:, :, j * 128:(j + 1) * 128], in_=pt.rearrange("p (c t) -> p c t", c=8), func=AF.Copy))
        else:
            k.do(DVE, [Bps[s]], [BhT[j]], lambda: ve.tensor_copy(
                hT[:, :, j * 128:(j + 1) * 128], pt.rearrange("p (c t) -> p c t", c=8)))

    ph1_load(0)
    ph1_load(1)
    for j in range(NT):
        if j + 2 < NT:
            ph1_load(j + 2)
        ph1_a(j)
        if j >= 1:
            ph1_b(j - 1)
    ph1_b(NT - 1)
    if stage == 1:
        return finish((hT[:],))
    k.barrier()

    wst = A("wst", R0, [128, 8, 128], F32)
    Bwst = Buf()
    dw = k.dsem("d_w")

    def gates(col0, dst, Bdst, wg, Bwg, th, Bth, banks, wst, Bwst, dw):
        for c in range(4):
            w = wg[c % 2]
            load_w_scaled(dw, wst, Bwst, w, Bwg[c % 2], w_in_v[:, :, col0 + c * 128: col0 + (c + 1) * 128], 128)
            for i in range(NQ):
                n = c * NQ + i
                bk = banks[n % len(banks)]

                def mm():
                    for dmc in range(8):
                        ins = tn.matmul(ps[bk][:], w[:, dmc, :], hT[:, dmc, i * 512:(i + 1) * 512],
                                        start=(dmc == 0), stop=(dmc == 7))
                    return ins
                k.do(PE, [Bwg[c % 2]] + BhT[4 * i:4 * i + 4], [Bps[bk]], mm)
                t_ = th[n % 2]
                k.do(ACT, [Bps[bk]], [Bth[n % 2]], lambda: sc.activation(
                    out=t_[:], in_=ps[bk][:], func=AF.Tanh, scale=0.5))
                k.do(DVE, [Bps[bk], Bth[n % 2]], [Bdst[c][i]], lambda: ve.scalar_tensor_tensor(
                    dst[:, c, i * 512:(i + 1) * 512], t_[:], 1.0, ps[bk][:], ALU.add, ALU.mult))

    wg = [A(f"wg{i}", R0 + 4096 + i * 2048, [128, 8, 128], BF16) for i in range(2)]
    Bwg = [Buf(), Buf()]
    th = [A(f"th{i}", R0 + 8192 + i * 2048, [128, 512], F32) for i in range(2)]
    Bth = [Buf(), Buf()]
    gates(1536, mixA, BmixA, wg, Bwg, th, Bth, [0, 1, 2], wst, Bwst, dw)
    if stage == 2:
        return finish((mixA[:],))
    k.barrier()

    o = R0
    Qzz = A("Qzz", o, [128, 2, S], BF16)
    Qz = [Qzz[:, 0, :], Qzz[:, 1, :]]
    o += 16384
    KT = A("KT", o, [128, S], BF16)
    o += 8192
    Vp = A("Vp", o, [128, 32, 192], BF16)
    o += 12288
    btab = A("btab", o, [128, 6, 256], BF16)
    o += 3072
    wq = [A("wq0", o, [128, 8, 128], BF16)] * 2
    o += 2048
    wk = [A("wk0", o, [128, 8, 128], BF16)] * 2
    o += 2048
    wv = [A("wv0", o, [128, 8, 128], BF16)] * 2
    o += 2048
    VT = A("VT", o, [128, S], BF16)
    o += 8192
    stg = [A(f"stg{i}", o + i * 2048, [128, 512], F32) for i in range(2)]
    o += 4096
    wstA = A("wstA", o, [128, 8, 128], F32)
    o += 4096
    PT = [A(f"PT{i}", o + i * 1024, [128, 512], BF16) for i in range(4)]
    o += 4096
    sq = [A(f"sq{i}", o + i * 1024, [128, 512], BF16) for i in range(2)]
    o += 2048
    ssb = [A("ssb0", o, [128, 512], F32)] * 2
    o += 2048
    rstd = [A("rstd0", o, [128, 512], F32)] * 2
    o += 2048
    Rt = A("Rt", o, [128, 512], F32)
    o += 2048
    tfin = [A("tfin0", o, [128, 512], F32), stg[0]]
    o += 2048
    assert SB_BASE + o <= SB_LIMIT, o
    BQz = [[Buf() for _ in range(NQ)] for _ in range(2)]
    BKT = [Buf() for _ in range(NQ)]
    BVp = [Buf() for _ in range(4)]
    Bbt = Buf()
    Bwq, Bwk, Bwv = [Buf()] * 2, [Buf()] * 2, [Buf()] * 2
    BVT = [Buf() for _ in range(NQ)]
    Bstg = [Buf(), Buf()]
    BwstA = Buf()
    BPT = [Buf() for _ in range(4)]
    Bsq, Bssb, Brstd = [Buf(), Buf()], [Buf()] * 2, [Buf()] * 2
    BRt = Buf()
    Btfin = [Buf(), Buf()]
    Bacc = [Buf(), Buf()]
    dwa = k.dsem("d_wa")
    dbt = k.dsem("d_bt")

    BQzAll = Buf()
    k.do(POOL, [], [BQzAll], lambda: gp_.memset(Qz[0][64:128, :], 0.0))
    k.do(POOL, [], [BQzAll], lambda: gp_.memset(Qz[1][0:64, :], 0.0))
    BVpE = Buf()
    k.do(POOL, [], [BVpE], lambda: gp_.memset(Vp[:, :, 64:128], 1.0))
    k.do(POOL, [], [BRt], lambda: gp_.memset(Rt[:], 0.0))
    k.do(POOL, [], [Bacc[0]], lambda: gp_.memset(acc_e[64:128, :], 0.0))
    for b_ in BQz[0] + BQz[1]:
        b_.w = BQzAll.w
    for b_ in BVp:
        b_.w = BVpE.w

    nrm_ctr = [0]

    def blocknorm(bk_proj, bk_ss, bo_list, extra_reads=()):
        n = nrm_ctr[0]
        nrm_ctr[0] += 1
        s = n % 2
        for ci, bk in enumerate(bk_proj):
            k.do(ACT, [Bps[bk]], [Bsq[s]], lambda bk=bk: sc.activation(out=sq[s][:], in_=ps[bk][:], func=AF.Square))
            k.do(PE, [Bsq[s], Bconst], [Bps[bk_ss]], lambda ci=ci: tn.matmul(
                ps[bk_ss][:], bo_list, sq[s][:], start=(ci == 0), stop=(ci == len(bk_proj) - 1)))
        k.do(ACT, [Bps[bk_ss], Bconst], [Bssb[s]], lambda: sc.activation(
            out=ssb[s][:], in_=ps[bk_ss][:], func=AF.Ln, bias=epsT[:, 0:1]))
        k.do(ACT, [Bssb[s]], [Brstd[s]], lambda: sc.activation(out=rstd[s][:], in_=ssb[s][:], func=AF.Exp, scale=-0.5))
        return s

    def load_pair_weights(hp):
        s = hp % 2
        load_w_scaled(dwa, wstA, BwstA, wq[s], Bwq[s], w_in_v[:, :, hp * 128:(hp + 1) * 128], 128)
        load_w_scaled(dwa, wstA, BwstA, wk[s], Bwk[s], w_in_v[:, :, 512 + hp * 128:512 + (hp + 1) * 128], 128)
        load_w_scaled(dwa, wstA, BwstA, wv[s], Bwv[s], w_in_v[:, :, 1024 + hp * 128:1024 + (hp + 1) * 128], 128)

    def proj512(bk, w, Bw, i):
        def mm():
            for dmc in range(8):
                ins = tn.matmul(ps[bk][:], w[:, dmc, :], hT[:, dmc, i * 512:(i + 1) * 512],
                                start=(dmc == 0), stop=(dmc == 7))
            return ins
        k.do(PE, [Bw] + BhT[4 * i:4 * i + 4], [Bps[bk]], mm)

    def phaseA_pair(hp):
        s = hp % 2
        k.dma(POOL, dbt, btab[:], bt_d[hp], writes=[Bbt])
        for i in range(NQ):
            bq = [0, 1][i % 2]
            proj512(bq, wq[s], Bwq[s], i)
            r_ = blocknorm([bq], 2, bo64)
            k.do(DVE, [Bps[bq], Brstd[r_], Bconst], [BQz[0][i]], lambda: ve.scalar_tensor_tensor(
                Qz[0][0:64, i * 512:(i + 1) * 512], ps[bq][0:64, :], gp[0:64, 8:9], rstd[r_][0:64, :], ALU.mult, ALU.mult))
            k.do(DVE, [Bps[bq], Brstd[r_], Bconst], [BQz[1][i]], lambda: ve.scalar_tensor_tensor(
                Qz[1][64:128, i * 512:(i + 1) * 512], ps[bq][64:128, :], gp[64:128, 8:9], rstd[r_][64:128, :], ALU.mult, ALU.mult))
            bk_ = [3, 4][i % 2]
            proj512(bk_, wk[s], Bwk[s], i)
            r_ = blocknorm([bk_], 5, bo64)
            k.do(DVE, [Bps[bk_], Brstd[r_], Bconst], [BKT[i]], lambda: ve.scalar_tensor_tensor(
                KT[:, i * 512:(i + 1) * 512], ps[bk_][:], gp[:, 9:10], rstd[r_][:], ALU.mult, ALU.mult))
        for i in range(NQ):
            bv = [6, 7][i % 2]
            proj512(bv, wv[s], Bwv[s], i)
            k.do(DVE, [Bps[bv]], [BVT[i]], lambda: ve.tensor_copy(VT[:, i * 512:(i + 1) * 512], ps[bv][:]))
        if hp + 1 < npairs_a:
            load_pair_weights(hp + 1)
        for pi, r in enumerate(PATTERNS):
            nb = NT // r
            for g8 in range(4):
                bk = [6, 7][g8 % 2]
                ptv_ = ps[bk][:].bitcast(BF16)

                def mmv():
                    for tt in range(8):
                        n = g8 * 8 + tt
                        b, m = n // nb, n % nb
                        t0 = 128 * m * r + b
                        ins = tn.transpose(ptv_[:, tt * 128:(tt + 1) * 128], VT[:, _ssl(t0, 128, r)], ident)
                    return ins
                k.do(PE, [Bconst] + BVT, [Bps[bk]], mmv)
                k.do(ACT, [Bps[bk]], [BVp[g8]], lambda: sc.activation(
                    out=Vp[:, g8 * 8:(g8 + 1) * 8, :].rearrange("p n (s d) -> p n s d", s=3)[:, :, 0:3:2, :],
                    in_=ptv_.rearrange("p (n s d) -> p n s d", n=8, s=2), func=AF.Copy))
            tiles = [(b, m) for b in range(r) for m in range(nb)]
            oset_ctr = [0]

            def emit_S(n):
                b, m = tiles[n]
                bk = [0, 1, 2][n % 3]
                nq = 256 if m < nb - 1 else 128
                t0 = 128 * m * r + b

                def mm():
                    if nq == 256:
                        o_ = ps[bk][:].rearrange("p (h q) -> p h q", h=2)[:, :, 0:nq]
                        tn.matmul(o_, KT[:, _ssl(t0, 128, r)], Qzz[:, :, _ssl(t0, nq, r)], start=True, stop=False)
                        return tn.matmul(o_, ident, btab[:, pi * 2:pi * 2 + 2, 0:nq], start=False, stop=True)
                    for h in range(2):
                        o_ = ps[bk][:, h * 256:h * 256 + nq]
                        tn.matmul(o_, KT[:, _ssl(t0, 128, r)], Qz[h][:, _ssl(t0, nq, r)], start=True, stop=False)
                        ins = tn.matmul(o_, ident, btab[:, pi * 2 + h, 0:nq], start=False, stop=True)
                    return ins
                qtiles = sorted(set([(t0) // 512, min((t0 + nq * r - 1) // 512, NQ - 1)]))
                ktile = sorted(set([t0 // 512, min((t0 + 128 * r - 1) // 512, NQ - 1)]))
                rd = [Bbt, Bconst] + [BKT[x] for x in range(ktile[0], ktile[-1] + 1)] \
                    + [BQz[h][x] for h in range(2) for x in range(qtiles[0], qtiles[-1] + 1)]
                return rd, [Bps[bk]], mm

            def emit_E(n):
                b, m = tiles[n]
                bk = [0, 1, 2][n % 3]
                nq = 256 if m < nb - 1 else 128
                pv = ps[bk][:].rearrange("p (h q) -> p h q", h=2)[:, :, 0:nq]
                sg = stg[n % 2][:].rearrange("p (h q) -> p h q", h=2)[:, :, 0:nq]
                ptv = PT[n % 4][:].rearrange("p (h q) -> p h q", h=2)[:, :, 0:nq]
                k.do(ACT, [Bps[bk]], [BPT[n % 4]], lambda: sc.activation(out=ptv, in_=pv, func=AF.Exp))

            def emit_PV(n):
                b, m = tiles[n]
                cs = (m % 4) * 128
                st_ = oset_ctr[0] % 2
                be, bo_ = (3, 4) if st_ == 0 else (5, 6)
                vt = b * nb + m

                ne, no_ = (5, 6) if st_ == 0 else (3, 4)
                wr = [Bps[be], Bps[bo_]]
                if m % 4 == 3 and m < nb - 1:
                    wr += [Bps[ne], Bps[no_]]

                def mm():
                    for h in range(2):
                        bank, nbank = (be, ne) if h == 0 else (bo_, no_)
                        lhs = Vp[:, vt, 0:128] if h == 0 else Vp[:, vt, 64:192]
                        p0 = h * 256
                        if m == nb - 1:
                            ins = tn.matmul(ps[bank][:, cs:cs + 128], lhs, PT[n % 4][:, p0:p0 + 128],
                                            start=False, stop=True, skip_group_check=True)
                        elif m % 4 == 3:
                            tn.matmul(ps[bank][:, cs:cs + 128], lhs, PT[n % 4][:, p0:p0 + 128],
                                      start=False, stop=True, skip_group_check=True)
                            ins = tn.matmul(ps[nbank][:, 0:128], lhs, PT[n % 4][:, p0 + 128:p0 + 256],
                                            start=True, stop=True, skip_group_check=True)
                        else:
                            ins = tn.matmul(ps[bank][:, cs:cs + 256], lhs, PT[n % 4][:, p0:p0 + 256],
                                            start=(m == 0), stop=True, skip_group_check=True)
                    return ins
                rd = [BPT[n % 4], BVp[vt // 8]]
                return rd, wr, mm, (lambda: post_PV(n, b, m, cs, be, bo_))

            def post_PV(n, b, m, cs, be, bo_):
                if m % 4 == 3 or m == nb - 1:
                    ncol = cs + 128
                    m0 = m - (m % 4)
                    a0 = 128 * m0 * r + b
                    sl = _ssl(a0, ncol, r)
                    if pi == 0:
                        k.do(DVE, [Bps[be]], [Bacc[0]], lambda: ve.tensor_copy(acc_e[0:65, sl], ps[be][0:65, 0:ncol]))
                        k.do(DVE, [Bps[bo_]], [Bacc[1]], lambda: ve.tensor_copy(acc_o[:, sl], ps[bo_][:, 0:ncol]))
                    else:
                        k.do(DVE, [Bps[be]], [Bacc[0]], lambda: ve.tensor_tensor(
                            acc_e[0:65, sl], ps[be][0:65, 0:ncol], acc_e[0:65, sl], ALU.add))
                        k.do(DVE, [Bps[bo_]], [Bacc[1]], lambda: ve.tensor_tensor(
                            acc_o[:, sl], ps[bo_][:, 0:ncol], acc_o[:, sl], ALU.add))
                    oset_ctr[0] += 1

            ntile = len(tiles)
            for n in range(ntile + 2):
                if n < ntile:
                    sS = emit_S(n)
                    k.do(PE, sS[0], sS[1], sS[2])
                    emit_E(n)
                if n >= 2:
                    sP = emit_PV(n - 2)
                    k.do(PE, sP[0], sP[1], sP[2])
                    sP[3]()
        k.do(ACT, [], [Bacc[0]], lambda: sc.activation(out=acc_e[64:65, :], in_=acc_e[64:65, :], func=AF.Ln))
        k.do(ACT, [], [Bacc[1]], lambda: sc.activation(out=acc_o[0:1, :], in_=acc_o[0:1, :], func=AF.Ln))
        for i in range(NQ):
            cs = slice(i * 512, (i + 1) * 512)
            bkf = [7, 2][i % 2]

            def mmf():
                tn.matmul(ps[bkf][:], cf2[:, 0, :], acc_e[:, cs], start=True, stop=False)
                return tn.matmul(ps[bkf][:], cf2[:, 1, :], acc_o[:, cs], start=False, stop=True)
            k.do(PE, [Bacc[0], Bacc[1], Bconst], [Bps[bkf]], mmf)
            t_ = tfin[i % 2]
            k.do(ACT, [Bps[bkf], Bconst], [Btfin[i % 2]], lambda: sc.activation(
                out=t_[:], in_=ps[bkf][:], func=AF.Exp, scale=-1.0, bias=lnhalf[:, 0:1]))
            k.do(DVE, [Bacc[0]], [Btfin[i % 2]], lambda: ve.tensor_tensor(
                t_[0:64, :], acc_e[0:64, cs], t_[0:64, :], ALU.mult))
            k.do(DVE, [Bacc[1]], [Btfin[i % 2]], lambda: ve.tensor_tensor(
                t_[64:128, :], acc_o[64:128, cs], t_[64:128, :], ALU.mult))
            k.do(POOL, [Btfin[i % 2]], [BmixA[hp][i]], lambda: gp_.tensor_tensor(
                mixA[:, hp, cs], t_[:], mixA[:, hp, cs], ALU.mult))

    npairs_a = 4 if stage != 3 else 1
    load_pair_weights(0)
    for hp in range(npairs_a):
        phaseA_pair(hp)
    if stage in (3, 4):
        return finish((mixA[:],))
    k.barrier()

    o = R0
    TT = A("TT", o, [128, S], F32)
    TTi = A("TTi", OFF_MIXB, [128, S], I32)
    o += 16384
    cqn = A("cqn", o, [128, 2, S], BF16)
    o += 16384
    ckvn = A("ckvn", o, [128, S], BF16)
    o += 8192
    kT = [A(f"kT{i}", o + i * 8192, [128, S], BF16) for i in range(2)]
    o += 16384
    OFF_W = o
    wb = [A(f"wb{i}", o + i * 2048, [128, 8, 128], BF16) for i in range(2)]
    o += 4096
    wstB = A("wstB", o, [128, 8, 128], F32)
    o += 4096
    sq2 = [A(f"sq2{i}", o + i * 1024, [128, 512], BF16) for i in range(2)]
    o += 2048
    ssb2 = A("ssb2", o, [128, 512], F32)
    o += 2048
    rstd2 = A("rstd2", o, [128, 512], F32)
    o += 2048
    tk = A("tk", o, [128, 512], F32)
    o += 2048
    ta = A("ta", OFF_W + 4096, [128, 512], F32)
    tu = A("tu", OFF_W + 6144, [128, 512], F32)
    assert SB_BASE + o <= SB_LIMIT, o
    sq[0], sq[1] = sq2[0], sq2[1]
    ssb[0] = ssb[1] = ssb2
    rstd[0] = rstd[1] = rstd2
    Bsq[0], Bsq[1] = Buf(), Buf()
    Bssb[0] = Bssb[1] = Buf()
    Brstd[0] = Brstd[1] = Buf()
    BTT = Buf()
    Bcqn = [Buf() for _ in range(NQ)]
    Bckvn = [Buf() for _ in range(NQ)]
    BkT = [[Buf() for _ in range(NQ)] for _ in range(2)]
    Bwb = [Buf(), Buf()]
    BwstB = Buf()
    Btk, Bta, Btu = Buf(), Buf(), Buf()
    dwb = k.dsem("d_wb")
    dpos = k.dsem("d_pos")

    k.do(POOL, [], [BTT], lambda: gp_.memset(TT[0:64, :], 1.0))
    k.dma(SP, dpos, TTi[64:128, :], bass.AP(pos_t, 0, [[0, 64], [1, S]]), writes=[BTT])
    k.do(DVE, [], [BTT], lambda: ve.tensor_copy(TT[64:128, :], TTi[64:128, :]))
    MAGIC = 12582912.0
    TWO_PI = 2.0 * math.pi
    for i in range(NQ):
        cs = slice(i * 512, (i + 1) * 512)
        k.do(DVE, [BTT, Bconst], [Bta], lambda: ve.tensor_scalar(
            ta[64:128, :], TT[64:128, cs], rc[64:128, 0:1], rc[64:128, 1:2], ALU.mult, ALU.add))
        k.do(DVE, [Bta], [Btu], lambda: ve.tensor_scalar(tu[64:128, :], ta[64:128, :], 1.0 / TWO_PI, MAGIC, ALU.mult, ALU.add))
        k.do(DVE, [], [Btu], lambda: ve.tensor_scalar(tu[64:128, :], tu[64:128, :], -MAGIC, None, ALU.add))
        k.do(DVE, [Bta], [Btu], lambda: ve.scalar_tensor_tensor(
            tu[64:128, :], tu[64:128, :], -TWO_PI, ta[64:128, :], ALU.mult, ALU.add))
        k.do(DVE, [], [Btu], lambda: ve.tensor_scalar(tu[64:128, :], tu[64:128, :], -math.pi, math.pi, ALU.max, ALU.min))
        k.do(ACT, [Btu, Bconst], [BTT], lambda: sc.activation(
            out=TT[64:128, cs], in_=tu[64:128, :], func=AF.Sin, scale=rc[64:128, 2:3]))
    if stage == 5:
        return finish((TT[:],))
    k.barrier()

    k.do(POOL, [], [Bsq[0]], lambda: gp_.memset(sq2[0][:], 0.0))
    k.do(POOL, [], [Bsq[1]], lambda: gp_.memset(sq2[1][:], 0.0))
    k.do(POOL, [], [Btk], lambda: gp_.memset(tk[:], 0.0))

    wc = [A(f"wc{i}", OFF_HT - 0 + 0, [1, 1], F32) for i in range(0)]

    def prep_chunk_weights(dst_i, src_ap, ncols):
        load_w_scaled(dwb, wstB, BwstB, wb[dst_i], Bwb[dst_i], src_ap, ncols)

    for i in range(NQ):
        cs = slice(i * 512, (i + 1) * 512)
        if i == 0:
            prep_chunk_weights(0, w_in_v[:, :, 2048:2176], 128)
            prep_chunk_weights(1, w_in_v[:, :, 2176:2304], 128)
        proj512(0, wb[0], Bwb[0], i)
        proj512(1, wb[1], Bwb[1], i)
        r_ = blocknorm([0, 1], 2, ones256)
        for c in range(2):
            k.do(DVE, [Bps[c], Brstd[r_], Bconst], [Bcqn[i]], lambda c=c: ve.scalar_tensor_tensor(
                cqn[:, c, cs], ps[c][:], gp[:, 10 + c:11 + c], rstd[r_][:], ALU.mult, ALU.mult))
    if stage == 61:
        return finish((cqn[:],))
    for i in range(NQ):
        cs = slice(i * 512, (i + 1) * 512)
        if i == 0:
            prep_chunk_weights(0, w_in_v[:, :, 2304:2432], 128)
        bk = [0, 1][i % 2]
        proj512(bk, wb[0], Bwb[0], i)
        r_ = blocknorm([bk], 2, ones128)
        k.do(DVE, [Bps[bk], Brstd[r_], Bconst], [Bckvn[i]], lambda: ve.scalar_tensor_tensor(
            ckvn[:, cs], ps[bk][:], gp[:, 12:13], rstd[r_][:], ALU.mult, ALU.mult))
    if stage == 62:
        return finish((cqn[:],))
    for i in range(NQ):
        cs = slice(i * 512, (i + 1) * 512)
        if i == 0:
            prep_chunk_weights(1, w_kr_v, 128)
        bk = [3, 4][i % 2]

        def mm():
            for dmc in range(8):
                ins = tn.matmul(ps[bk][:], wb[1][:, dmc, :], hT[:, dmc, cs],
                                start=(dmc == 0), stop=(dmc == 7))
            return ins
        k.do(PE, [Bwb[1]] + BhT[4 * i:4 * i + 4], [Bps[bk]], mm)
        s_ = i % 2
        k.do(ACT, [Bps[bk]], [Bsq[s_]], lambda: sc.activation(out=sq2[s_][64:128, :], in_=ps[bk][64:128, :], func=AF.Square))
        k.do(PE, [Bsq[s_], Bconst], [Bps[5]], lambda: tn.matmul(ps[5][:], bo_q, sq2[s_][:], start=True, stop=True))
        k.do(ACT, [Bps[5], Bconst], [Bssb[0]], lambda: sc.activation(out=ssb2[:], in_=ps[5][:], func=AF.Ln, bias=epsT[:, 0:1]))
        k.do(ACT, [Bssb[0]], [Brstd[0]], lambda: sc.activation(out=rstd2[:], in_=ssb2[:], func=AF.Exp, scale=-0.5))
        k.do(DVE, [Bps[bk], BTT, Bconst], [Btk], lambda: ve.scalar_tensor_tensor(
            tk[64:128, :], ps[bk][64:128, :], gp[64:128, 14:15], TT[64:128, cs], ALU.mult, ALU.mult))
        k.do(DVE, [Brstd[0]], [Btk], lambda: ve.tensor_tensor(tk[64:128, :], tk[64:128, :], rstd2[64:128, :], ALU.mult))
        k.do(PE, [Btk, Bconst], [Bps[6]], lambda: tn.matmul(ps[6][:], msum, tk[:], start=True, stop=True))
        k.do(ACT, [Bps[6]], [BkT[0][i]], lambda: sc.activation(out=kT[0][64:128, cs], in_=ps[6][64:128, :], func=AF.Copy))
        k.do(ACT, [Bps[6]], [BkT[1][i]], lambda: sc.activation(out=kT[1][64:128, cs], in_=ps[6][64:128, :], func=AF.Copy))
        if stage == 60 and i == 0:
            dd = A("dd", OFF_MIXB, [128, 5, 512], F32)
            Bdd = Buf()
            k.do(POOL, [], [Bdd], lambda: gp_.memset(dd[:], 0.0))
            k.do(DVE, [Bps[bk]], [Bdd], lambda: ve.tensor_copy(dd[64:128, 0, :], ps[bk][64:128, :]))
            k.do(DVE, [Bssb[0]], [Bdd], lambda: ve.tensor_copy(dd[:, 1, :], ssb2[:]))
            k.do(DVE, [Brstd[0]], [Bdd], lambda: ve.tensor_copy(dd[:, 2, :], rstd2[:]))
            k.do(DVE, [Btk], [Bdd], lambda: ve.tensor_copy(dd[:, 3, :], tk[:]))
            k.do(DVE, [Bps[6]], [Bdd], lambda: ve.tensor_copy(dd[:, 4, :], ps[6][:]))
            return finish((dd[:],))
    if stage == 63:
        return finish((cqn[:],))
    gates(2464, mixB, BmixB, wb, Bwb, [ssb2, rstd2], [Bssb[0], Brstd[0]], [0, 1, 2], wstB, BwstB, dwb)
    if stage == 6:
        return finish((kT[0][:],))
    if stage == 7:
        return finish((cqn[:],))
    k.barrier()

    o = OFF_HT
    qT = [A(f"qT{i}", o + i * 8192, [128, S], BF16) for i in range(2)]
    o += 16384
    Vb = [A(f"Vb{i}", o + i * 12288, [128, 32, 192], BF16) for i in range(2)]
    o += 24576
    PTb = [A(f"PTb{i}", o + i * 1024, [128, 512], BF16) for i in range(4)]
    o += 4096
    tq = [A(f"tq{i}", o + i * 2048, [128, 512], F32) for i in range(2)]
    o += 4096
    bcs = [A(f"bcs{i}", o + i * 2048, [128, 512], F32) for i in range(2)]
    o += 4096
    RtB = A("RtB", o, [128, 512], F32)
    o += 2048
    tfb = [A(f"tfb{i}", o + i * 2048, [128, 512], F32) for i in range(2)]
    o += 4096
    assert o <= OFF_R
    wuq = A("wuq", OFF_W, [128, 2, 1024], BF16)
    wukk = A("wukk", OFF_W + 4096, [128, 576], BF16)
    wukv = A("wukv", OFF_W + 5248, [128, 512], BF16)
    BqT = [[Buf() for _ in range(NQ)] for _ in range(2)]
    BVb = [[Buf() for _ in range(8)] for _ in range(2)]
    BPTb = [Buf() for _ in range(4)]
    Btq = [Buf(), Buf()]
    Bbcs = [Buf(), Buf()]
    BRtB = Buf()
    Btfb = [Buf(), Buf()]
    Bwm = Buf()
    dwm = k.dsem("d_wm")
    k.dma(POOL, dwm, wuq[:], w_uq_v, writes=[Bwm])
    k.dma(POOL, dwm, wukk[:], w_ukvk_d, writes=[Bwm])
    k.dma(POOL, dwm, wukv[:], w_ukvv_d, writes=[Bwm])
    Bwm.w = (dwm, dwm.cnt)
    for s_ in range(2):
        k.do(POOL, [], [BVb[s_][0]], lambda: gp_.memset(Vb[s_][:, :, 64:128], 1.0))
        for g4 in range(1, 8):
            BVb[s_][g4].w = BVb[s_][0].w
    k.do(POOL, [], [BRtB], lambda: gp_.memset(RtB[:], 0.0))
    k.do(POOL, [], [Bsq[0]], lambda: gp_.memset(sq2[0][:], 0.0))
    k.do(POOL, [], [Bsq[1]], lambda: gp_.memset(sq2[1][:], 0.0))

    def phaseB_prep(hp):
        vs = hp % 2
        for g4 in range(8):
            bk = [6, 7][g4 % 2]

            def mmv():
                for tt in range(4):
                    n = g4 * 4 + tt
                    ins = tn.matmul(ps[bk][:, tt * 128:(tt + 1) * 128], ckvn[:, n * 128:(n + 1) * 128],
                                    wukv[:, hp * 128:(hp + 1) * 128], start=True, stop=True)
                return ins
            k.do(PE, [Bwm, Bckvn[g4 // 1 * 0 + (g4 * 4) // 4 // 1 if False else (g4 * 512) // 512]], [Bps[bk]], mmv)
            k.do(DVE, [Bps[bk]], [BVb[vs][g4]], lambda: ve.tensor_copy(
                Vb[vs][:, g4 * 4:(g4 + 1) * 4, :].rearrange("p n (s d) -> p n s d", s=3)[:, :, 0:3:2, :],
                ps[bk][:].rearrange("p (n s d) -> p n s d", n=4, s=2)))
        for hh in range(2):
            h = 2 * hp + hh
            def q_s1(i):
                cs = slice(i * 512, (i + 1) * 512)
                bq = [0, 1][i % 2]
                bss = [2, 5][i % 2]

                def mmq():
                    for c in range(2):
                        ins = tn.matmul(ps[bq][:], wuq[:, c, h * 128:(h + 1) * 128], cqn[:, c, cs],
                                        start=(c == 0), stop=(c == 1))
                    return ins
                k.do(PE, [Bwm, Bcqn[i]], [Bps[bq]], mmq)
                s_ = i % 2
                k.do(ACT, [Bps[bq]], [Bsq[s_]], lambda: sc.activation(out=sq2[s_][:], in_=ps[bq][:], func=AF.Square))
                k.do(PE, [Bsq[s_], Bconst], [Bps[bss]], lambda: tn.matmul(ps[bss][:], bo_q, sq2[s_][:], start=True, stop=True))

            def q_s2(i):
                cs = slice(i * 512, (i + 1) * 512)
                bq = [0, 1][i % 2]
                bss = [2, 5][i % 2]
                k.do(ACT, [Bps[bss], Bconst], [Bssb[0]], lambda: sc.activation(out=ssb2[:], in_=ps[bss][:], func=AF.Ln, bias=epsT[:, 0:1]))
                k.do(ACT, [Bssb[0]], [Brstd[0]], lambda: sc.activation(out=rstd2[:], in_=ssb2[:], func=AF.Exp, scale=-0.5))
                t_ = tq[i % 2]
                k.do(DVE, [Bps[bq], BTT, Bconst], [Btq[i % 2]], lambda: ve.scalar_tensor_tensor(
                    t_[:], ps[bq][:], gp[:, 13:14], TT[:, cs], ALU.mult, ALU.mult))
                k.do(POOL, [Btq[i % 2], Brstd[0]], [BqT[hh][i]], lambda: gp_.tensor_tensor(
                    qT[hh][:, cs], t_[:], rstd2[:], ALU.mult))
            for i in range(NQ + 1):
                if i < NQ:
                    q_s1(i)
                if i >= 1:
                    q_s2(i - 1)
            k.do(POOL, [], [Bsq[0]], lambda: gp_.memset(sq2[0][64:128, :], 0.0))
            k.do(POOL, [], [Bsq[1]], lambda: gp_.memset(sq2[1][64:128, :], 0.0))

            def k_s1(i):
                cs = slice(i * 512, (i + 1) * 512)
                bk_ = [3, 4][i % 2]
                bss = [5, 2][i % 2]
                k.do(PE, [Bwm, Bckvn[i]], [Bps[bk_]], lambda: tn.matmul(
                    ps[bk_][:], wukk[:, h * 64:h * 64 + 128], ckvn[:, cs], start=True, stop=True))
                s_ = i % 2
                k.do(ACT, [Bps[bk_]], [Bsq[s_]], lambda: sc.activation(out=sq2[s_][0:64, :], in_=ps[bk_][0:64, :], func=AF.Square))
                k.do(PE, [Bsq[s_], Bconst], [Bps[bss]], lambda: tn.matmul(ps[bss][:], bo_q, sq2[s_][:], start=True, stop=True))

            def k_s2(i):
                cs = slice(i * 512, (i + 1) * 512)
                bk_ = [3, 4][i % 2]
                bss = [5, 2][i % 2]
                k.do(ACT, [Bps[bss], Bconst], [Bssb[0]], lambda: sc.activation(
                    out=ssb2[0:64, :], in_=ps[bss][0:64, :], func=AF.Ln, bias=epsT[0:64, 0:1]))
                k.do(ACT, [Bssb[0]], [Brstd[0]], lambda: sc.activation(out=rstd2[0:64, :], in_=ssb2[0:64, :], func=AF.Exp, scale=-0.5))
                k.do(DVE, [Bps[bk_], Brstd[0], Bconst], [BkT[hh][i]], lambda: ve.scalar_tensor_tensor(
                    kT[hh][0:64, cs], ps[bk_][0:64, :], gp[0:64, 14:15], rstd2[0:64, :], ALU.mult, ALU.mult))
            for i in range(NQ + 1):
                if i < NQ:
                    k_s1(i)
                if i >= 1:
                    k_s2(i - 1)

    def phaseB_attn(hp):
        vs = hp % 2
        tiles = []
        for i in range(NQ):
            for hh in range(2):
                for j in range(4 * i + 4):
                    tiles.append((i, hh, j))
        ntile = len(tiles)

        def emit_S(n):
            i, hh, j = tiles[n]
            bk = [0, 1, 2][n % 3]
            jj = j - 4 * i
            c0 = 128 * jj if jj > 0 else 0

            def mm():
                o_ = ps[bk][:, c0:512]
                ins = tn.matmul(o_, kT[hh][:, j * 128:(j + 1) * 128], qT[hh][:, i * 512 + c0:(i + 1) * 512],
                                start=True, stop=(jj < 0))
                if jj >= 0:
                    ins = tn.matmul(ps[bk][:, c0:c0 + 128], ident, causal, start=False, stop=True)
                return ins
            return [Bconst, BkT[hh][j // 4], BqT[hh][i]], [Bps[bk]], mm

        def emit_E(n):
            i, hh, j = tiles[n]
            bk = [0, 1, 2][n % 3]
            jj = j - 4 * i
            c0 = 128 * jj if jj > 0 else 0
            k.do(ACT, [Bps[bk]], [BPTb[n % 4]], lambda: sc.activation(
                out=PTb[n % 4][:, c0:512], in_=ps[bk][:, c0:512], func=AF.Exp))

        def emit_PV(n):
            i, hh, j = tiles[n]
            jj = j - 4 * i
            c0 = 128 * jj if jj > 0 else 0
            bank = [[3, 5], [4, 6]][hh][i % 2]
            lhs = Vb[vs][:, j, 0:128] if hh == 0 else Vb[vs][:, j, 64:192]
            last = (j == 4 * i + 3)
            return [BPTb[n % 4], BVb[vs][j // 4]], [Bps[bank]], (lambda: tn.matmul(
                ps[bank][:, c0:512], lhs, PTb[n % 4][:, c0:512], start=(j == 0), stop=last)), (last and hh == 1), i

        def finalize(i):
            cs = slice(i * 512, (i + 1) * 512)
            be, bo_ = [3, 5][i % 2], [4, 6][i % 2]
            k.do(ACT, [Bps[be]], [BRtB], lambda: sc.activation(out=RtB[64:65, :], in_=ps[be][64:65, :], func=AF.Ln))
            k.do(ACT, [Bps[bo_]], [BRtB], lambda: sc.activation(out=RtB[0:1, :], in_=ps[bo_][0:1, :], func=AF.Ln))
            k.do(PE, [BRtB, Bconst], [Bps[7]], lambda: tn.matmul(ps[7][:], sel, RtB[:], start=True, stop=True))
            b_ = bcs[i % 2]
            k.do(ACT, [Bps[7], Bconst], [Bbcs[i % 2]], lambda: sc.activation(
                out=b_[:], in_=ps[7][:], func=AF.Exp, scale=-1.0, bias=lnhalf[:, 0:1]))
            t_ = tfb[i % 2]
            k.do(DVE, [Bps[be], Bbcs[i % 2]], [Btfb[i % 2]], lambda: ve.tensor_tensor(
                t_[0:64, :], ps[be][0:64, :], b_[0:64, :], ALU.mult))
            k.do(DVE, [Bps[bo_], Bbcs[i % 2]], [Btfb[i % 2]], lambda: ve.tensor_tensor(
                t_[64:128, :], ps[bo_][64:128, :], b_[64:128, :], ALU.mult))
            k.do(POOL, [Btfb[i % 2]], [BmixB[hp][i]], lambda: gp_.tensor_tensor(
                mixB[:, hp, cs], t_[:], mixB[:, hp, cs], ALU.mult))

        for n in range(ntile + 2):
            if n < ntile:
                sS = emit_S(n)
                k.do(PE, sS[0], sS[1], sS[2])
                emit_E(n)
            if n >= 2:
                sP = emit_PV(n - 2)
                k.do(PE, sP[0], sP[1], sP[2])
                if sP[3]:
                    finalize(sP[4])

    npairs_b = 4 if stage != 8 else 1
    wo = A("wo", R0, [128, 8, 1024], BF16)
    Bwo = Buf()
    dwo = k.dsem("d_wo")
    w_out_pref = [False]
    for hp in range(npairs_b):
        phaseB_prep(hp)
        if hp == 3:
            for c in range(8):
                k.dma(POOL, dwo, wo[:, c, :], w_out_v[:, c, :], writes=([BTT] if c == 0 else []))
            Bwo.w = (dwo, dwo.cnt)
            w_out_pref[0] = True
        phaseB_attn(hp)
    if stage in (8, 9):
        return finish((mixB[:],))
    k.barrier()

    o = OFF_HT
    o += 16384
    xo = [A(f"xo{i}", o + i * 4096, [128, D], F32) for i in range(3)]
    o += 12288
    yo = [A(f"yo{i}", o + i * 4096, [128, D], F32) for i in range(3)]
    o += 12288
    Bxo = [Buf() for _ in range(3)]
    Byo = [Buf() for _ in range(3)]
    dxo = [k.dsem(f"d_xo{i}") for i in range(3)]
    dyo = [k.dsem(f"d_yo{i}") for i in range(3)]
    if not w_out_pref[0]:
        for c in range(8):
            k.dma(POOL, dwo, wo[:, c, :], w_out_v[:, c, :], writes=[Bwo])
        Bwo.w = (dwo, dwo.cnt)
    y_v = y_d.rearrange("(n p) d -> n p d", p=128)

    def p5_load(j):
        k.dma(SP, dxo[j % 3], xo[j % 3][:], x_v[j], writes=[Bxo[j % 3]])

    p5_load(0)
    p5_load(1)
    for j in range(NT):
        if j + 2 < NT:
            p5_load(j + 2)
        ts = slice(j * 128, (j + 1) * 128)
        for half in range(2):
            bk = [0, 1, 2, 3][(2 * j + half) % 4]

            def mm():
                for c in range(8):
                    src = mixA if c < 4 else mixB
                    ins = tn.matmul(ps[bk][:], src[:, c % 4, ts], wo[:, c, half * 512:(half + 1) * 512],
                                    start=(c == 0), stop=(c == 7))
                return ins
            k.do(PE, [Bwo] + [BmixA[c][j // 4] for c in range(4)] + [BmixB[c][j // 4] for c in range(4)], [Bps[bk]], mm)
            k.do(DVE, [Bps[bk], Bxo[j % 3]], [Byo[j % 3]], lambda: ve.tensor_tensor(
                yo[j % 3][:, half * 512:(half + 1) * 512], ps[bk][:], xo[j % 3][:, half * 512:(half + 1) * 512], ALU.add))
        k.dma(SP, dyo[j % 3], y_v[j], yo[j % 3][:], reads=[Byo[j % 3]])
    k.barrier()
    k.es.close()
    return nc


def _t5_bucket(dist):
    d = np.maximum(dist.astype(np.float32), np.float32(1.0))
    large = 16 + (np.log(d / np.float32(16.0)) / np.float32(math.log(2048 / 16)) * np.float32(16.0)).astype(np.int32)
    large = np.minimum(large, 31)
    return np.where(dist < 16, dist, large)


def _consts():
    cbf = np.zeros((128, 6, 128), np.float32)
    p = np.arange(128)
    cbf[:, 0, :] = np.eye(128, dtype=np.float32)
    cbf[:, 1, :] = ((p[:, None] // 64) == (p[None, :] // 64)) / 64.0
    blk = np.where(p < 64, 0, np.where(p < 96, 1, 2))
    wgt = np.where(p < 64, 1.0 / 64, 1.0 / 32)
    cbf[:, 2, :] = (blk[:, None] == blk[None, :]) * wgt[None, :]
    cbf[:, 3, :] = 1.0 / 256
    cbf[:, 4, :] = 1.0 / 128
    cbf[:, 5, :] = np.where(p[None, :] >= p[:, None], 0.0, MASKV)
    cf = np.zeros((128, 2, 128), np.float32)
    cf[64, 0, 0:64] = 1.0
    cf[0, 0, 64:128] = 1.0
    for j in range(32):
        for a in (64 + j, 96 + j):
            for b in (64 + j, 96 + j):
                cf[a, 1, b] = 1.0
    cf2 = np.zeros((128, 2, 128), np.float32)
    cf2[64, 0, 0:64] = 1.0
    cf2[0, 1, 64:128] = 1.0
    rc = np.zeros((128, 4), np.float32)
    inv_freq = (np.float32(10000.0) ** (-np.arange(0, 32, 2, dtype=np.float32) / np.float32(32))).astype(np.float32)
    for rrow in range(64):
        prt = 64 + rrow
        rc[prt, 0] = inv_freq[rrow % 16]
        rc[prt, 1] = (math.pi / 2) if rrow < 32 else 0.0
        rc[prt, 2] = -1.0 if 32 <= rrow < 48 else 1.0
    return cbf, cf, cf2, rc


def _bias_tables(rel_bias):
    ki = np.arange(128)[:, None]
    c = np.arange(256)[None, :]
    j = c - ki
    valid = (j >= 0) & (j <= 128)
    bt = np.empty((4, 128, 6, 256), np.float32)
    for pi, r in enumerate(PATTERNS):
        bucket = _t5_bucket(np.maximum(j, 0) * r)
        for h in range(8):
            tab = np.where(valid, rel_bias[bucket, h], np.float32(MASKV)).astype(np.float32)
            bt[h // 2, :, pi * 2 + (h % 2), :] = tab
    return bt


_NC_CACHE = {}


def _prep_shared(inputs):
    f = lambda n: np.asarray(inputs[n], np.float32)
    w_in = np.ascontiguousarray(f("w_in")[0])
    kr = w_in[:, 2432:2464]
    w_kr = np.ascontiguousarray(np.concatenate([np.zeros((D, 64), np.float32), kr, kr[:, 16:32], kr[:, 0:16]], axis=1))
    w_uq = f("w_uq")[0]
    w_uq_ext = np.empty((256, 1024), np.float32)
    for h in range(8):
        blk = w_uq[:, h * 96:(h + 1) * 96]
        w_uq_ext[:, h * 128:h * 128 + 96] = blk
        w_uq_ext[:, h * 128 + 96:h * 128 + 112] = blk[:, 80:96]
        w_uq_ext[:, h * 128 + 112:h * 128 + 128] = blk[:, 64:80]
    w_ukv = f("w_ukv")[0].reshape(128, 8, 128)
    w_ukvk = np.ascontiguousarray(np.concatenate([w_ukv[:, :, 0:64].reshape(128, 512), np.zeros((128, 64), np.float32)], axis=1))
    w_ukvv = np.ascontiguousarray(w_ukv[:, :, 64:128].reshape(128, 512))
    w_out = np.ascontiguousarray(f("w_out")[0])
    gp = np.zeros((128, 16), np.float32)
    gp[:, 0:8] = f("norm_gain")[0].reshape(8, 128).T
    gp[:, 8] = np.tile(f("a_q_gain")[0], 2)
    gp[:, 9] = np.tile(f("a_k_gain")[0], 2)
    gp[:, 10:12] = f("q_c_gain")[0].reshape(2, 128).T
    gp[:, 12] = f("kv_c_gain")[0]
    qr = f("qr_gain")[0]
    kr_g = f("kr_gain")[0]
    gp[:, 13] = np.concatenate([f("qn_gain")[0], qr, qr[16:32], qr[0:16]])
    gp[:, 14] = np.concatenate([f("kn_gain")[0], kr_g, kr_g[16:32], kr_g[0:16]])
    cbf, cf, cf2, rc = _consts()
    bt = _bias_tables(f("rel_bias"))
    return dict(w_in=w_in, w_kr=w_kr, w_uq=w_uq_ext, w_ukvk=w_ukvk, w_ukvv=w_ukvv, w_out=w_out,
                cbf=cbf, cf=cf, cf2=cf2, gp=gp, rc=rc, bt=bt)


def kernel(**inputs):
    x = np.asarray(inputs["x"], np.float32)
    pos = np.asarray(inputs["positions"], np.int32)
    shared = _prep_shared(inputs)
    if "nc" not in _NC_CACHE:
        _NC_CACHE["nc"] = build()
    nc = _NC_CACHE["nc"]
    in_maps = []
    for b in range(8):
        m = dict(shared)
        m["x"] = np.ascontiguousarray(x[b])
        m["pos"] = np.ascontiguousarray(pos[b].reshape(1, S))
        in_maps.append(m)
    res = run_bass_kernel_spmd(nc, in_maps, core_ids=list(range(8)))
    return np.stack([np.asarray(r["y"], np.float32) for r in res.results], axis=0)
```

```python
import contextlib
import math
import numpy as np
import concourse.bass as bass
import concourse.mybir as mybir
from concourse.bass_utils import run_bass_kernel_spmd

F32 = mybir.dt.float32
BF16 = mybir.dt.bfloat16
I32 = mybir.dt.int32
AF = mybir.ActivationFunctionType
ALU = mybir.AluOpType

S = 4096
D = 1024
NT = 32
NQ = 8
NCOLS = 2976
MASKV = -30000.0
EPS = 1e-6
PATTERNS = (1, 4, 16)
QSCALE_A = 64 ** -0.5
QSCALE_B = 96 ** -0.5
SB_BASE = 16512
SB_LIMIT = 229344


def _ssl(t0, n, r):
    return slice(t0, t0 + (n - 1) * r + 1, r)


class Sem:
    def __init__(self, k, name):
        self.name = name
        self.sem = k.es.enter_context(k.nc.semaphore(name))
        self.cnt = 0


class Eng(Sem):
    def __init__(self, k, name, eng):
        super().__init__(k, "s_" + name)
        self.eng = eng
        self.seen = {}

    def wait(self, *toks):
        for t in toks:
            if t is None:
                continue
            if isinstance(t, list):
                self.wait(*t)
                continue
            p, c = t
            if self.seen.get(p.name, 0) >= c:
                continue
            self.eng.wait_ge(p.sem, c)
            self.seen[p.name] = c

    def done(self, ins):
        ins.then_inc(self.sem, 1)
        self.cnt += 1
        return (self, self.cnt)


class Buf:
    def __init__(self, name="", excl=False):
        self.name = name
        self.excl = excl
        self.w = None
        self.r = {}

    def add_r(self, tok):
        p, c = tok
        if self.r.get(p.name, (None, 0))[1] < c:
            self.r[p.name] = tok

    def set_w(self, tok):
        self.w = tok
        self.r = {}


class K:
    def __init__(self):
        self.nc = bass.Bass("TRN2", target_bir_lowering=False)
        self.es = contextlib.ExitStack()
        nc = self.nc
        self.PE = Eng(self, "pe", nc.tensor)
        self.ACT = Eng(self, "act", nc.scalar)
        self.DVE = Eng(self, "dve", nc.vector)
        self.POOL = Eng(self, "pool", nc.gpsimd)
        self.SP = Eng(self, "sp", nc.sync)
        self.engines = [self.PE, self.ACT, self.DVE, self.POOL, self.SP]
        self.dsems = []

    def dsem(self, name):
        d = Sem(self, name)
        self.dsems.append(d)
        return d

    def wait_for(self, E, reads, writes):
        for b in reads:
            E.wait(b.w)
            if b.excl:
                E.wait(*[t for n, t in b.r.items() if n != E.name])
        for b in writes:
            E.wait(b.w)
            E.wait(*b.r.values())

    def commit(self, E, ins, reads, writes):
        tok = E.done(ins)
        for b in reads:
            b.add_r(tok)
        for b in writes:
            b.set_w(tok)
        return tok

    def do(self, E, reads, writes, emit):
        self.wait_for(E, reads, writes)
        ins = emit()
        return self.commit(E, ins, reads, writes)

    def dma(self, Q, ds, out, in_, reads=(), writes=()):
        for b in reads:
            Q.wait(b.w)
        for b in writes:
            Q.wait(b.w)
            Q.wait(*b.r.values())
        Q.eng.dma_start(out=out, in_=in_).then_inc(ds.sem, 16)
        ds.cnt += 16
        tok = (ds, ds.cnt)
        for b in reads:
            b.add_r(tok)
        for b in writes:
            b.set_w(tok)
        return tok

    def barrier(self):
        for E in self.engines:
            for F in self.engines:
                if F is not E and F.cnt > 0:
                    E.wait((F, F.cnt))
            for d in self.dsems:
                if d.cnt > 0:
                    E.wait((d, d.cnt))


def build(stage=99, dbg=None):
    k = K()
    nc = k.nc
    PE, ACT, DVE, POOL, SP = k.PE, k.ACT, k.DVE, k.POOL, k.SP
    tn, sc, ve, gp_, sy = nc.tensor, nc.scalar, nc.vector, nc.gpsimd, nc.sync

    def dram(name, shape, dt, kind="ExternalInput"):
        return nc.dram_tensor(name, shape, dt, kind=kind)

    x_d = dram("x", [S, D], F32).ap()
    pos_t = dram("pos", [1, S], I32)
    w_in_d = dram("w_in", [D, NCOLS], F32).ap()
    w_kr_d = dram("w_kr", [D, 128], F32).ap()
    w_uq_d = dram("w_uq", [256, 1024], F32).ap()
    w_ukvk_d = dram("w_ukvk", [128, 576], F32).ap()
    w_ukvv_d = dram("w_ukvv", [128, 512], F32).ap()
    w_out_d = dram("w_out", [D, D], F32).ap()
    cbf_d = dram("cbf", [128, 6, 128], F32).ap()
    cf_d = dram("cf", [128, 2, 128], F32).ap()
    cf2_d = dram("cf2", [128, 2, 128], F32).ap()
    gp_d = dram("gp", [128, 16], F32).ap()
    rc_d = dram("rc", [128, 4], F32).ap()
    bt_d = dram("bt", [4, 128, 6, 256], F32).ap()
    y_d = dram("y", [S, D], F32, kind="ExternalOutput").ap()
    dbg_d = None
    if dbg is not None:
        dbg_d = dram("dbg", list(dbg[1]), dbg[2], kind="ExternalOutput").ap()

    w_in_v = w_in_d.rearrange("(c p) n -> p c n", p=128)
    w_kr_v = w_kr_d.rearrange("(c p) n -> p c n", p=128)
    w_uq_v = w_uq_d.rearrange("(c p) n -> p c n", p=128)
    w_out_v = w_out_d.rearrange("(c p) n -> p c n", p=128)

    def A(name, off, shape, dt):
        sz = int(np.prod(shape[1:])) * (4 if dt in (F32, I32) else 2)
        assert off % 32 == 0 and SB_BASE + off + sz <= SB_LIMIT, (name, off, sz)
        return nc.alloc_sbuf_tensor_at(name, list(shape), dt, offset=SB_BASE + off)

    cbf = A("cbf", 0, [128, 6, 128], BF16)
    cf = A("cf", 1536, [128, 2, 128], F32)
    gp = A("gp", 2560, [128, 16], F32)
    rc = A("rc", 2624, [128, 4], F32)
    st_ss = A("st_ss", 2656, [128, 4], F32)
    st_t = A("st_t", 2688, [128, 4], F32)
    st_rs = A("st_rs", 2720, [128, 4], F32)
    cm05 = A("cm05", 2752, [128, 1], F32)
    epsT = A("epsT", 2784, [128, 1], F32)
    lnhalf = A("lnhalf", 2816, [128, 1], F32)
    cf2 = A("cf2", 2880, [128, 2, 128], F32)
    OFF_MIXA, OFF_MIXB, OFF_HT, OFF_R = 4096, 36864, 69632, 135168
    mixA = A("mixA", OFF_MIXA, [128, 4, S], BF16)
    mixB = A("mixB", OFF_MIXB, [128, 4, S], BF16)
    acc_e = A("acc_e", OFF_MIXB, [128, S], F32)
    acc_o = A("acc_o", OFF_MIXB + 16384, [128, S], F32)
    hT = A("hT", OFF_HT, [128, 8, S], BF16)
    BhT = [Buf() for _ in range(NT)]
    BmixA = [[Buf() for _ in range(NQ)] for _ in range(4)]
    BmixB = [[Buf() for _ in range(NQ)] for _ in range(4)]

    ident = cbf[:, 0, :]
    bo64 = cbf[:, 1, :]
    bo_q = cbf[:, 2, :]
    ones256 = cbf[:, 3, :]
    ones128 = cbf[:, 4, :]
    causal = cbf[:, 5, :]
    sel = cf[:, 0, :]
    msum = cf[:, 1, :]

    ps = [k.es.enter_context(nc.psum_tensor(f"ps{i}", [128, 512], F32)) for i in range(8)]
    Bps = [Buf(excl=True) for _ in range(8)]

    dconst = k.dsem("d_const")
    Bconst = Buf()

    dconst2 = k.dsem("d_const2")
    Bcbf = Buf()
    k.dma(POOL, dconst2, cbf[:], cbf_d, writes=[Bcbf])
    k.dma(SP, dconst, cf[:], cf_d, writes=[Bconst])
    k.dma(SP, dconst, cf2[:], cf2_d, writes=[Bconst])
    k.dma(SP, dconst, gp[:], gp_d, writes=[Bconst])
    k.dma(SP, dconst, rc[:], rc_d, writes=[Bconst])
    Bconst.w = (dconst, dconst.cnt)
    k.do(DVE, [], [Bconst], lambda: ve.tensor_scalar(gp[:, 8:9], gp[:, 8:9], QSCALE_A, None, ALU.mult))
    k.do(DVE, [], [Bconst], lambda: ve.tensor_scalar(gp[:, 13:14], gp[:, 13:14], QSCALE_B, None, ALU.mult))
    k.do(POOL, [], [Bconst], lambda: gp_.memset(cm05[:], -0.5))
    k.do(POOL, [], [Bconst], lambda: gp_.memset(epsT[:], EPS))
    k.do(POOL, [], [Bconst], lambda: gp_.memset(lnhalf[:], math.log(0.5)))
    k.barrier()

    dout = k.dsem("d_out")

    def finish(dump=None):
        if dump is not None:
            k.barrier()
            src, = dump
            k.dma(SP, dout, dbg_d, src)
        k.barrier()
        SP.wait((dout, dout.cnt)) if dout.cnt else None
        k.es.close()
        return nc

    def load_w_scaled(dsm, stage, Bstage, dst, Bdst, src_ap, ncols):
        k.dma(SP, dsm, stage[:, :, 0:ncols], src_ap, writes=[Bstage])
        k.do(POOL, [Bstage, Bconst], [Bdst], lambda: gp_.tensor_tensor(
            dst[:, :, 0:ncols], stage[:, :, 0:ncols],
            gp[:, 0:8].unsqueeze(2).to_broadcast([128, 8, ncols]), ALU.mult))

    R0 = OFF_R
    xbuf = [A(f"xbuf{i}", R0 + i * 4096, [128, D], F32) for i in range(3)]
    xn = [A(f"xn{i}", R0 + 12288 + i * 2048, [128, D], BF16) for i in range(2)]
    junk = A("junk", R0 + 16384, [128, D], BF16)
    Bx = [Buf() for _ in range(3)]
    Bxn = [Buf() for _ in range(2)]
    Bjunk = Buf()
    Bst = [Buf() for _ in range(2)]
    dx = [k.dsem(f"d_x{i}") for i in range(3)]
    x_v = x_d.rearrange("(n p) d -> n p d", p=128)

    def ph1_load(j):
        k.dma(SP, dx[j % 3], xbuf[j % 3][:], x_v[j], writes=[Bx[j % 3]])

    def ph1_a(j):
        s = j % 2
        k.do(ACT, [Bx[j % 3]], [Bjunk, Bst[s]], lambda: sc.activation(
            out=junk[:], in_=xbuf[j % 3][:], func=AF.Square, accum_out=st_ss[:, s:s + 1]))
        k.do(DVE, [], [Bst[s]], lambda: ve.tensor_scalar(
            st_t[:, s:s + 1], st_ss[:, s:s + 1], 1.0 / D, EPS, ALU.mult, ALU.add))
        k.do(POOL, [Bconst], [Bst[s]], lambda: gp_.tensor_tensor(
            st_rs[:, s:s + 1], st_t[:, s:s + 1], cm05[:], ALU.pow))
        k.do(DVE, [Bx[j % 3], Bst[s]], [Bxn[s]], lambda: ve.tensor_scalar(
            xn[s][:], xbuf[j % 3][:], st_rs[:, s:s + 1], None, ALU.mult))

    def ph1_b(j):
        s = j % 2
        pt = ps[s][:].bitcast(BF16)

        def tr():
            for c in range(8):
                ins = tn.transpose(pt[:, c * 128:(c + 1) * 128], xn[s][:, c * 128:(c + 1) * 128], ident)
            return ins
        k.do(PE, [Bxn[s], Bconst], [Bps[s]], tr)
        if j % 2 == 0:
            k.do(ACT, [Bps[s]], [BhT[j]], lambda: sc.activation(
                out=hT[:, :, j * 128:(j + 1) * 128], in_=pt.rearrange("p (c t) -> p c t", c=8), func=AF.Copy))
        else:
            k.do(DVE, [Bps[s]], [BhT[j]], lambda: ve.tensor_copy(
                hT[:, :, j * 128:(j + 1) * 128], pt.rearrange("p (c t) -> p c t", c=8)))

    ph1_load(0)
    ph1_load(1)
    for j in range(NT):
        if j + 2 < NT:
            ph1_load(j + 2)
        ph1_a(j)
        if j >= 1:
            ph1_b(j - 1)
    ph1_b(NT - 1)
    if stage == 1:
        return finish((hT[:],))
    k.barrier()

    wst = A("wst", R0, [128, 8, 128], F32)
    Bwst = Buf()
    dw = k.dsem("d_w")

    def gates(col0, dst, Bdst, wg, Bwg, th, Bth, banks, wst, Bwst, dw):
        for c in range(4):
            w = wg[c % 2]
            load_w_scaled(dw, wst, Bwst, w, Bwg[c % 2], w_in_v[:, :, col0 + c * 128: col0 + (c + 1) * 128], 128)
            for i in range(NQ):
                n = c * NQ + i
                bk = banks[n % len(banks)]

                def mm():
                    for dmc in range(8):
                        ins = tn.matmul(ps[bk][:], w[:, dmc, :], hT[:, dmc, i * 512:(i + 1) * 512],
                                        start=(dmc == 0), stop=(dmc == 7))
                    return ins
                k.do(PE, [Bwg[c % 2]] + BhT[4 * i:4 * i + 4], [Bps[bk]], mm)
                t_ = th[n % 2]
                k.do(ACT, [Bps[bk]], [Bth[n % 2]], lambda: sc.activation(
                    out=t_[:], in_=ps[bk][:], func=AF.Tanh, scale=0.5))
                k.do(DVE, [Bps[bk], Bth[n % 2]], [Bdst[c][i]], lambda: ve.scalar_tensor_tensor(
                    dst[:, c, i * 512:(i + 1) * 512], t_[:], 1.0, ps[bk][:], ALU.add, ALU.mult))

    wg = [A(f"wg{i}", R0 + 4096 + i * 2048, [128, 8, 128], BF16) for i in range(2)]
    Bwg = [Buf(), Buf()]
    th = [A(f"th{i}", R0 + 8192 + i * 2048, [128, 512], F32) for i in range(2)]
    Bth = [Buf(), Buf()]
    gates(1536, mixA, BmixA, wg, Bwg, th, Bth, [0, 1, 2], wst, Bwst, dw)
    if stage == 2:
        return finish((mixA[:],))
    k.barrier()

    o = R0
    Qzz = A("Qzz", o, [128, 2, S], BF16)
    Qz = [Qzz[:, 0, :], Qzz[:, 1, :]]
    o += 16384
    KT = A("KT", o, [128, S], BF16)
    o += 8192
    Vp = A("Vp", o, [128, 32, 192], BF16)
    o += 12288
    btab = A("btab", o, [128, 6, 256], BF16)
    o += 3072
    wq = [A("wq0", o, [128, 8, 128], BF16)] * 2
    o += 2048
    wk = [A("wk0", o, [128, 8, 128], BF16)] * 2
    o += 2048
    wv = [A("wv0", o, [128, 8, 128], BF16)] * 2
    o += 2048
    VT = A("VT", o, [128, S], BF16)
    o += 8192
    stg = [A(f"stg{i}", o + i * 2048, [128, 512], F32) for i in range(2)]
    o += 4096
    wstA = A("wstA", o, [128, 8, 128], F32)
    o += 4096
    PT = [A(f"PT{i}", o + i * 1024, [128, 512], BF16) for i in range(4)]
    o += 4096
    sq = [A(f"sq{i}", o + i * 1024, [128, 512], BF16) for i in range(2)]
    o += 2048
    ssb = [A("ssb0", o, [128, 512], F32)] * 2
    o += 2048
    rstd = [A("rstd0", o, [128, 512], F32)] * 2
    o += 2048
    Rt = A("Rt", o, [128, 512], F32)
    o += 2048
    tfin = [A("tfin0", o, [128, 512], F32), stg[0]]
    o += 2048
    assert SB_BASE + o <= SB_LIMIT, o
    BQz = [[Buf() for _ in range(NQ)] for _ in range(2)]
    BKT = [Buf() for _ in range(NQ)]
    BVp = [Buf() for _ in range(4)]
    Bbt = Buf()
    Bwq, Bwk, Bwv = [Buf()] * 2, [Buf()] * 2, [Buf()] * 2
    BVT = [Buf() for _ in range(NQ)]
    Bstg = [Buf(), Buf()]
    BwstA = Buf()
    BPT = [Buf() for _ in range(4)]
    Bsq, Bssb, Brstd = [Buf(), Buf()], [Buf()] * 2, [Buf()] * 2
    BRt = Buf()
    Btfin = [Buf(), Buf()]
    Bacc = [Buf(), Buf()]
    dwa = k.dsem("d_wa")
    dbt = k.dsem("d_bt")

    BQzAll = Buf()
    k.do(POOL, [], [BQzAll], lambda: gp_.memset(Qz[0][64:128, :], 0.0))
    k.do(POOL, [], [BQzAll], lambda: gp_.memset(Qz[1][0:64, :], 0.0))
    BVpE = Buf()
    k.do(POOL, [], [BVpE], lambda: gp_.memset(Vp[:, :, 64:128], 1.0))
    k.do(POOL, [], [BRt], lambda: gp_.memset(Rt[:], 0.0))
    k.do(POOL, [], [Bacc[0]], lambda: gp_.memset(acc_e[64:128, :], 0.0))
    for b_ in BQz[0] + BQz[1]:
        b_.w = BQzAll.w
    for b_ in BVp:
        b_.w = BVpE.w

    nrm_ctr = [0]

    def blocknorm(bk_proj, bk_ss, bo_list, extra_reads=()):
        n = nrm_ctr[0]
        nrm_ctr[0] += 1
        s = n % 2
        for ci, bk in enumerate(bk_proj):
            k.do(ACT, [Bps[bk]], [Bsq[s]], lambda bk=bk: sc.activation(out=sq[s][:], in_=ps[bk][:], func=AF.Square))
            k.do(PE, [Bsq[s], Bconst], [Bps[bk_ss]], lambda ci=ci: tn.matmul(
                ps[bk_ss][:], bo_list, sq[s][:], start=(ci == 0), stop=(ci == len(bk_proj) - 1)))
        k.do(ACT, [Bps[bk_ss], Bconst], [Bssb[s]], lambda: sc.activation(
            out=ssb[s][:], in_=ps[bk_ss][:], func=AF.Ln, bias=epsT[:, 0:1]))
        k.do(ACT, [Bssb[s]], [Brstd[s]], lambda: sc.activation(out=rstd[s][:], in_=ssb[s][:], func=AF.Exp, scale=-0.5))
        return s

    def load_pair_weights(hp):
        s = hp % 2
        load_w_scaled(dwa, wstA, BwstA, wq[s], Bwq[s], w_in_v[:, :, hp * 128:(hp + 1) * 128], 128)
        load_w_scaled(dwa, wstA, BwstA, wk[s], Bwk[s], w_in_v[:, :, 512 + hp * 128:512 + (hp + 1) * 128], 128)
        load_w_scaled(dwa, wstA, BwstA, wv[s], Bwv[s], w_in_v[:, :, 1024 + hp * 128:1024 + (hp + 1) * 128], 128)

    def proj512(bk, w, Bw, i):
        def mm():
            for dmc in range(8):
                ins = tn.matmul(ps[bk][:], w[:, dmc, :], hT[:, dmc, i * 512:(i + 1) * 512],
                                start=(dmc == 0), stop=(dmc == 7))
            return ins
        k.do(PE, [Bw] + BhT[4 * i:4 * i + 4], [Bps[bk]], mm)

    def phaseA_pair(hp):
        s = hp % 2
        k.dma(POOL, dbt, btab[:], bt_d[hp], writes=[Bbt])
        for i in range(NQ):
            bq = [0, 1][i % 2]
            proj512(bq, wq[s], Bwq[s], i)
            r_ = blocknorm([bq], 2, bo64)
            k.do(DVE, [Bps[bq], Brstd[r_], Bconst], [BQz[0][i]], lambda: ve.scalar_tensor_tensor(
                Qz[0][0:64, i * 512:(i + 1) * 512], ps[bq][0:64, :], gp[0:64, 8:9], rstd[r_][0:64, :], ALU.mult, ALU.mult))
            k.do(DVE, [Bps[bq], Brstd[r_], Bconst], [BQz[1][i]], lambda: ve.scalar_tensor_tensor(
                Qz[1][64:128, i * 512:(i + 1) * 512], ps[bq][64:128, :], gp[64:128, 8:9], rstd[r_][64:128, :], ALU.mult, ALU.mult))
            bk_ = [3, 4][i % 2]
            proj512(bk_, wk[s], Bwk[s], i)
            r_ = blocknorm([bk_], 5, bo64)
            k.do(DVE, [Bps[bk_], Brstd[r_], Bconst], [BKT[i]], lambda: ve.scalar_tensor_tensor(
                KT[:, i * 512:(i + 1) * 512], ps[bk_][:], gp[:, 9:10], rstd[r_][:], ALU.mult, ALU.mult))
        for i in range(NQ):
            bv = [6, 7][i % 2]
            proj512(bv, wv[s], Bwv[s], i)
            k.do(DVE, [Bps[bv]], [BVT[i]], lambda: ve.tensor_copy(VT[:, i * 512:(i + 1) * 512], ps[bv][:]))
        if hp + 1 < npairs_a:
            load_pair_weights(hp + 1)
        for pi, r in enumerate(PATTERNS):
            nb = NT // r
            for g8 in range(4):
                bk = [6, 7][g8 % 2]
                ptv_ = ps[bk][:].bitcast(BF16)

                def mmv():
                    for tt in range(8):
                        n = g8 * 8 + tt
                        b, m = n // nb, n % nb
                        t0 = 128 * m * r + b
                        ins = tn.transpose(ptv_[:, tt * 128:(tt + 1) * 128], VT[:, _ssl(t0, 128, r)], ident)
                    return ins
                k.do(PE, [Bconst] + BVT, [Bps[bk]], mmv)
                k.do(ACT, [Bps[bk]], [BVp[g8]], lambda: sc.activation(
                    out=Vp[:, g8 * 8:(g8 + 1) * 8, :].rearrange("p n (s d) -> p n s d", s=3)[:, :, 0:3:2, :],
                    in_=ptv_.rearrange("p (n s d) -> p n s d", n=8, s=2), func=AF.Copy))
            tiles = [(b, m) for b in range(r) for m in range(nb)]
            oset_ctr = [0]

            def emit_S(n):
                b, m = tiles[n]
                bk = [0, 1, 2][n % 3]
                nq = 256 if m < nb - 1 else 128
                t0 = 128 * m * r + b

                def mm():
                    if nq == 256:
                        o_ = ps[bk][:].rearrange("p (h q) -> p h q", h=2)[:, :, 0:nq]
                        tn.matmul(o_, KT[:, _ssl(t0, 128, r)], Qzz[:, :, _ssl(t0, nq, r)], start=True, stop=False)
                        return tn.matmul(o_, ident, btab[:, pi * 2:pi * 2 + 2, 0:nq], start=False, stop=True)
                    for h in range(2):
                        o_ = ps[bk][:, h * 256:h * 256 + nq]
                        tn.matmul(o_, KT[:, _ssl(t0, 128, r)], Qz[h][:, _ssl(t0, nq, r)], start=True, stop=False)
                        ins = tn.matmul(o_, ident, btab[:, pi * 2 + h, 0:nq], start=False, stop=True)
                    return ins
                qtiles = sorted(set([(t0) // 512, min((t0 + nq * r - 1) // 512, NQ - 1)]))
                ktile = sorted(set([t0 // 512, min((t0 + 128 * r - 1) // 512, NQ - 1)]))
                rd = [Bbt, Bconst] + [BKT[x] for x in range(ktile[0], ktile[-1] + 1)] \
                    + [BQz[h][x] for h in range(2) for x in range(qtiles[0], qtiles[-1] + 1)]
                return rd, [Bps[bk]], mm

            def emit_E(n):
                b, m = tiles[n]
                bk = [0, 1, 2][n % 3]
                nq = 256 if m < nb - 1 else 128
                pv = ps[bk][:].rearrange("p (h q) -> p h q", h=2)[:, :, 0:nq]
                sg = stg[n % 2][:].rearrange("p (h q) -> p h q", h=2)[:, :, 0:nq]
                ptv = PT[n % 4][:].rearrange("p (h q) -> p h q", h=2)[:, :, 0:nq]
                k.do(ACT, [Bps[bk]], [BPT[n % 4]], lambda: sc.activation(out=ptv, in_=pv, func=AF.Exp))

            def emit_PV(n):
                b, m = tiles[n]
                cs = (m % 4) * 128
                st_ = oset_ctr[0] % 2
                be, bo_ = (3, 4) if st_ == 0 else (5, 6)
                vt = b * nb + m

                ne, no_ = (5, 6) if st_ == 0 else (3, 4)
                wr = [Bps[be], Bps[bo_]]
                if m % 4 == 3 and m < nb - 1:
                    wr += [Bps[ne], Bps[no_]]

                def mm():
                    for h in range(2):
                        bank, nbank = (be, ne) if h == 0 else (bo_, no_)
                        lhs = Vp[:, vt, 0:128] if h == 0 else Vp[:, vt, 64:192]
                        p0 = h * 256
                        if m == nb - 1:
                            ins = tn.matmul(ps[bank][:, cs:cs + 128], lhs, PT[n % 4][:, p0:p0 + 128],
                                            start=False, stop=True, skip_group_check=True)
                        elif m % 4 == 3:
                            tn.matmul(ps[bank][:, cs:cs + 128], lhs, PT[n % 4][:, p0:p0 + 128],
                                      start=False, stop=True, skip_group_check=True)
                            ins = tn.matmul(ps[nbank][:, 0:128], lhs, PT[n % 4][:, p0 + 128:p0 + 256],
                                            start=True, stop=True, skip_group_check=True)
                        else:
                            ins = tn.matmul(ps[bank][:, cs:cs + 256], lhs, PT[n % 4][:, p0:p0 + 256],
                                            start=(m == 0), stop=True, skip_group_check=True)
                    return ins
                rd = [BPT[n % 4], BVp[vt // 8]]
                return rd, wr, mm, (lambda: post_PV(n, b, m, cs, be, bo_))

            def post_PV(n, b, m, cs, be, bo_):
                if m % 4 == 3 or m == nb - 1:
                    ncol = cs + 128
                    m0 = m - (m % 4)
                    a0 = 128 * m0 * r + b
                    sl = _ssl(a0, ncol, r)
                    if pi == 0:
                        k.do(DVE, [Bps[be]], [Bacc[0]], lambda: ve.tensor_copy(acc_e[0:65, sl], ps[be][0:65, 0:ncol]))
                        k.do(DVE, [Bps[bo_]], [Bacc[1]], lambda: ve.tensor_copy(acc_o[:, sl], ps[bo_][:, 0:ncol]))
                    else:
                        k.do(DVE, [Bps[be]], [Bacc[0]], lambda: ve.tensor_tensor(
                            acc_e[0:65, sl], ps[be][0:65, 0:ncol], acc_e[0:65, sl], ALU.add))
                        k.do(DVE, [Bps[bo_]], [Bacc[1]], lambda: ve.tensor_tensor(
                            acc_o[:, sl], ps[bo_][:, 0:ncol], acc_o[:, sl], ALU.add))
                    oset_ctr[0] += 1

            ntile = len(tiles)
            for n in range(ntile + 2):
                if n < ntile:
                    sS = emit_S(n)
                    k.do(PE, sS[0], sS[1], sS[2])
                    emit_E(n)
                if n >= 2:
                    sP = emit_PV(n - 2)
                    k.do(PE, sP[0], sP[1], sP[2])
                    sP[3]()
        k.do(ACT, [], [Bacc[0]], lambda: sc.activation(out=acc_e[64:65, :], in_=acc_e[64:65, :], func=AF.Ln))
        k.do(ACT, [], [Bacc[1]], lambda: sc.activation(out=acc_o[0:1, :], in_=acc_o[0:1, :], func=AF.Ln))
        for i in range(NQ):
            cs = slice(i * 512, (i + 1) * 512)
            bkf = [7, 2][i % 2]

            def mmf():
                tn.matmul(ps[bkf][:], cf2[:, 0, :], acc_e[:, cs], start=True, stop=False)
                return tn.matmul(ps[bkf][:], cf2[:, 1, :], acc_o[:, cs], start=False, stop=True)
            k.do(PE, [Bacc[0], Bacc[1], Bconst], [Bps[bkf]], mmf)
            t_ = tfin[i % 2]
            k.do(ACT, [Bps[bkf], Bconst], [Btfin[i % 2]], lambda: sc.activation(
                out=t_[:], in_=ps[bkf][:], func=AF.Exp, scale=-1.0, bias=lnhalf[:, 0:1]))
            k.do(DVE, [Bacc[0]], [Btfin[i % 2]], lambda: ve.tensor_tensor(
                t_[0:64, :], acc_e[0:64, cs], t_[0:64, :], ALU.mult))
            k.do(DVE, [Bacc[1]], [Btfin[i % 2]], lambda: ve.tensor_tensor(
                t_[64:128, :], acc_o[64:128, cs], t_[64:128, :], ALU.mult))
            k.do(POOL, [Btfin[i % 2]], [BmixA[hp][i]], lambda: gp_.tensor_tensor(
                mixA[:, hp, cs], t_[:], mixA[:, hp, cs], ALU.mult))

    npairs_a = 4 if stage != 3 else 1
    load_pair_weights(0)
    for hp in range(npairs_a):
        phaseA_pair(hp)
    if stage in (3, 4):
        return finish((mixA[:],))
    k.barrier()

    o = R0
    TT = A("TT", o, [128, S], F32)
    TTi = A("TTi", OFF_MIXB, [128, S], I32)
    o += 16384
    cqn = A("cqn", o, [128, 2, S], BF16)
    o += 16384
    ckvn = A("ckvn", o, [128, S], BF16)
    o += 8192
    kT = [A(f"kT{i}", o + i * 8192, [128, S], BF16) for i in range(2)]
    o += 16384
    OFF_W = o
    wb = [A(f"wb{i}", o + i * 2048, [128, 8, 128], BF16) for i in range(2)]
    o += 4096
    wstB = A("wstB", o, [128, 8, 128], F32)
    o += 4096
    sq2 = [A(f"sq2{i}", o + i * 1024, [128, 512], BF16) for i in range(2)]
    o += 2048
    ssb2 = A("ssb2", o, [128, 512], F32)
    o += 2048
    rstd2 = A("rstd2", o, [128, 512], F32)
    o += 2048
    tk = A("tk", o, [128, 512], F32)
    o += 2048
    rstd2b = A("rstd2b", o, [128, 512], F32)
    Brstd2b = Buf()
    ta = A("ta", OFF_W + 4096, [128, 512], F32)
    tu = A("tu", OFF_W + 6144, [128, 512], F32)
    assert SB_BASE + o <= SB_LIMIT, o
    sq[0], sq[1] = sq2[0], sq2[1]
    ssb[0] = ssb[1] = ssb2
    rstd[0] = rstd[1] = rstd2
    Bsq[0], Bsq[1] = Buf(), Buf()
    Bssb[0] = Bssb[1] = Buf()
    Brstd[0] = Brstd[1] = Buf()
    BTT = Buf()
    Bcqn = [Buf() for _ in range(NQ)]
    Bckvn = [Buf() for _ in range(NQ)]
    BkT = [[Buf() for _ in range(NQ)] for _ in range(2)]
    Bwb = [Buf(), Buf()]
    BwstB = Buf()
    Btk, Bta, Btu = Buf(), Buf(), Buf()
    dwb = k.dsem("d_wb")
    dpos = k.dsem("d_pos")

    k.do(POOL, [], [BTT], lambda: gp_.memset(TT[0:64, :], 1.0))
    k.dma(SP, dpos, TTi[64:128, :], bass.AP(pos_t, 0, [[0, 64], [1, S]]), writes=[BTT])
    k.do(DVE, [], [BTT], lambda: ve.tensor_copy(TT[64:128, :], TTi[64:128, :]))
    MAGIC = 12582912.0
    TWO_PI = 2.0 * math.pi
    for i in range(NQ):
        cs = slice(i * 512, (i + 1) * 512)
        k.do(DVE, [BTT, Bconst], [Bta], lambda: ve.tensor_scalar(
            ta[64:128, :], TT[64:128, cs], rc[64:128, 0:1], rc[64:128, 1:2], ALU.mult, ALU.add))
        k.do(DVE, [Bta], [Btu], lambda: ve.tensor_scalar(tu[64:128, :], ta[64:128, :], 1.0 / TWO_PI, MAGIC, ALU.mult, ALU.add))
        k.do(DVE, [], [Btu], lambda: ve.tensor_scalar(tu[64:128, :], tu[64:128, :], -MAGIC, None, ALU.add))
        k.do(DVE, [Bta], [Btu], lambda: ve.scalar_tensor_tensor(
            tu[64:128, :], tu[64:128, :], -TWO_PI, ta[64:128, :], ALU.mult, ALU.add))
        k.do(DVE, [], [Btu], lambda: ve.tensor_scalar(tu[64:128, :], tu[64:128, :], -math.pi, math.pi, ALU.max, ALU.min))
        k.do(ACT, [Btu, Bconst], [BTT], lambda: sc.activation(
            out=TT[64:128, cs], in_=tu[64:128, :], func=AF.Sin, scale=rc[64:128, 2:3]))
    if stage == 5:
        return finish((TT[:],))
    k.barrier()

    k.do(POOL, [], [Bsq[0]], lambda: gp_.memset(sq2[0][:], 0.0))
    k.do(POOL, [], [Bsq[1]], lambda: gp_.memset(sq2[1][:], 0.0))
    k.do(POOL, [], [Btk], lambda: gp_.memset(tk[:], 0.0))

    wc = [A(f"wc{i}", OFF_HT - 0 + 0, [1, 1], F32) for i in range(0)]

    def prep_chunk_weights(dst_i, src_ap, ncols):
        load_w_scaled(dwb, wstB, BwstB, wb[dst_i], Bwb[dst_i], src_ap, ncols)

    for i in range(NQ):
        cs = slice(i * 512, (i + 1) * 512)
        if i == 0:
            prep_chunk_weights(0, w_in_v[:, :, 2048:2176], 128)
            prep_chunk_weights(1, w_in_v[:, :, 2176:2304], 128)
        proj512(0, wb[0], Bwb[0], i)
        proj512(1, wb[1], Bwb[1], i)
        r_ = blocknorm([0, 1], 2, ones256)
        for c in range(2):
            k.do(DVE, [Bps[c], Brstd[r_], Bconst], [Bcqn[i]], lambda c=c: ve.scalar_tensor_tensor(
                cqn[:, c, cs], ps[c][:], gp[:, 10 + c:11 + c], rstd[r_][:], ALU.mult, ALU.mult))
    if stage == 61:
        return finish((cqn[:],))
    for i in range(NQ):
        cs = slice(i * 512, (i + 1) * 512)
        if i == 0:
            prep_chunk_weights(0, w_in_v[:, :, 2304:2432], 128)
        bk = [0, 1][i % 2]
        proj512(bk, wb[0], Bwb[0], i)
        r_ = blocknorm([bk], 2, ones128)
        k.do(DVE, [Bps[bk], Brstd[r_], Bconst], [Bckvn[i]], lambda: ve.scalar_tensor_tensor(
            ckvn[:, cs], ps[bk][:], gp[:, 12:13], rstd[r_][:], ALU.mult, ALU.mult))
    if stage == 62:
        return finish((cqn[:],))
    for i in range(NQ):
        cs = slice(i * 512, (i + 1) * 512)
        if i == 0:
            prep_chunk_weights(1, w_kr_v, 128)
        bk = [3, 4][i % 2]

        def mm():
            for dmc in range(8):
                ins = tn.matmul(ps[bk][:], wb[1][:, dmc, :], hT[:, dmc, cs],
                                start=(dmc == 0), stop=(dmc == 7))
            return ins
        k.do(PE, [Bwb[1]] + BhT[4 * i:4 * i + 4], [Bps[bk]], mm)
        s_ = i % 2
        k.do(ACT, [Bps[bk]], [Bsq[s_]], lambda: sc.activation(out=sq2[s_][64:128, :], in_=ps[bk][64:128, :], func=AF.Square))
        k.do(PE, [Bsq[s_], Bconst], [Bps[5]], lambda: tn.matmul(ps[5][:], bo_q, sq2[s_][:], start=True, stop=True))
        k.do(ACT, [Bps[5], Bconst], [Bssb[0]], lambda: sc.activation(out=ssb2[:], in_=ps[5][:], func=AF.Ln, bias=epsT[:, 0:1]))
        k.do(ACT, [Bssb[0]], [Brstd[0]], lambda: sc.activation(out=rstd2[:], in_=ssb2[:], func=AF.Exp, scale=-0.5))
        k.do(DVE, [Bps[bk], BTT, Bconst], [Btk], lambda: ve.scalar_tensor_tensor(
            tk[64:128, :], ps[bk][64:128, :], gp[64:128, 14:15], TT[64:128, cs], ALU.mult, ALU.mult))
        k.do(DVE, [Brstd[0]], [Btk], lambda: ve.tensor_tensor(tk[64:128, :], tk[64:128, :], rstd2[64:128, :], ALU.mult))
        k.do(PE, [Btk, Bconst], [Bps[6]], lambda: tn.matmul(ps[6][:], msum, tk[:], start=True, stop=True))
        k.do(ACT, [Bps[6]], [BkT[0][i]], lambda: sc.activation(out=kT[0][64:128, cs], in_=ps[6][64:128, :], func=AF.Copy))
        k.do(ACT, [Bps[6]], [BkT[1][i]], lambda: sc.activation(out=kT[1][64:128, cs], in_=ps[6][64:128, :], func=AF.Copy))
        if stage == 60 and i == 0:
            dd = A("dd", OFF_MIXB, [128, 5, 512], F32)
            Bdd = Buf()
            k.do(POOL, [], [Bdd], lambda: gp_.memset(dd[:], 0.0))
            k.do(DVE, [Bps[bk]], [Bdd], lambda: ve.tensor_copy(dd[64:128, 0, :], ps[bk][64:128, :]))
            k.do(DVE, [Bssb[0]], [Bdd], lambda: ve.tensor_copy(dd[:, 1, :], ssb2[:]))
            k.do(DVE, [Brstd[0]], [Bdd], lambda: ve.tensor_copy(dd[:, 2, :], rstd2[:]))
            k.do(DVE, [Btk], [Bdd], lambda: ve.tensor_copy(dd[:, 3, :], tk[:]))
            k.do(DVE, [Bps[6]], [Bdd], lambda: ve.tensor_copy(dd[:, 4, :], ps[6][:]))
            return finish((dd[:],))
    if stage == 63:
        return finish((cqn[:],))
    gates(2464, mixB, BmixB, wb, Bwb, [ssb2, rstd2], [Bssb[0], Brstd[0]], [0, 1, 2], wstB, BwstB, dwb)
    if stage == 6:
        return finish((kT[0][:],))
    if stage == 7:
        return finish((cqn[:],))
    k.barrier()

    o = OFF_HT
    qT = [A(f"qT{i}", o + i * 8192, [128, S], BF16) for i in range(2)]
    o += 16384
    Vb = [A(f"Vb{i}", o + i * 12288, [128, 32, 192], BF16) for i in range(2)]
    o += 24576
    PTb = [A(f"PTb{i}", o + i * 1024, [128, 512], BF16) for i in range(4)]
    o += 4096
    tq = [A(f"tq{i}", o + i * 2048, [128, 512], F32) for i in range(2)]
    o += 4096
    bcs = [A(f"bcs{i}", o + i * 2048, [128, 512], F32) for i in range(2)]
    o += 4096
    RtB = A("RtB", o, [128, 512], F32)
    o += 2048
    tfb = [A(f"tfb{i}", o + i * 2048, [128, 512], F32) for i in range(2)]
    o += 4096
    assert o <= OFF_R
    wuq = A("wuq", OFF_W, [128, 2, 1024], BF16)
    wukk = A("wukk", OFF_W + 4096, [128, 576], BF16)
    wukv = A("wukv", OFF_W + 5248, [128, 512], BF16)
    BqT = [[Buf() for _ in range(NQ)] for _ in range(2)]
    BVb = [[Buf() for _ in range(8)] for _ in range(2)]
    BPTb = [Buf() for _ in range(4)]
    Btq = [Buf(), Buf()]
    Bbcs = [Buf(), Buf()]
    BRtB = Buf()
    Btfb = [Buf(), Buf()]
    Bwm = Buf()
    dwm = k.dsem("d_wm")
    k.dma(POOL, dwm, wuq[:], w_uq_v, writes=[Bwm])
    k.dma(POOL, dwm, wukk[:], w_ukvk_d, writes=[Bwm])
    k.dma(POOL, dwm, wukv[:], w_ukvv_d, writes=[Bwm])
    Bwm.w = (dwm, dwm.cnt)
    for s_ in range(2):
        k.do(POOL, [], [BVb[s_][0]], lambda: gp_.memset(Vb[s_][:, :, 64:128], 1.0))
        for g4 in range(1, 8):
            BVb[s_][g4].w = BVb[s_][0].w
    k.do(POOL, [], [BRtB], lambda: gp_.memset(RtB[:], 0.0))
    k.do(POOL, [], [Bsq[0]], lambda: gp_.memset(sq2[0][:], 0.0))
    k.do(POOL, [], [Bsq[1]], lambda: gp_.memset(sq2[1][:], 0.0))

    def phaseB_prep(hp):
        vs = hp % 2
        for g4 in range(8):
            bk = [6, 7][g4 % 2]

            def mmv():
                for tt in range(4):
                    n = g4 * 4 + tt
                    ins = tn.matmul(ps[bk][:, tt * 128:(tt + 1) * 128], ckvn[:, n * 128:(n + 1) * 128],
                                    wukv[:, hp * 128:(hp + 1) * 128], start=True, stop=True)
                return ins
            k.do(PE, [Bwm, Bckvn[g4 // 1 * 0 + (g4 * 4) // 4 // 1 if False else (g4 * 512) // 512]], [Bps[bk]], mmv)
            k.do(DVE, [Bps[bk]], [BVb[vs][g4]], lambda: ve.tensor_copy(
                Vb[vs][:, g4 * 4:(g4 + 1) * 4, :].rearrange("p n (s d) -> p n s d", s=3)[:, :, 0:3:2, :],
                ps[bk][:].rearrange("p (n s d) -> p n s d", n=4, s=2)))
        for hh in range(2):
            h = 2 * hp + hh
            def q_s1(i):
                cs = slice(i * 512, (i + 1) * 512)
                bq = [0, 1, 3, 4][i % 4]
                bss = [2, 5][i % 2]

                def mmq():
                    for c in range(2):
                        ins = tn.matmul(ps[bq][:], wuq[:, c, h * 128:(h + 1) * 128], cqn[:, c, cs],
                                        start=(c == 0), stop=(c == 1))
                    return ins
                k.do(PE, [Bwm, Bcqn[i]], [Bps[bq]], mmq)
                s_ = i % 2
                k.do(ACT, [Bps[bq]], [Bsq[s_]], lambda: sc.activation(out=sq2[s_][:], in_=ps[bq][:], func=AF.Square))
                k.do(PE, [Bsq[s_], Bconst], [Bps[bss]], lambda: tn.matmul(ps[bss][:], bo_q, sq2[s_][:], start=True, stop=True))

            def q_s2(i):
                cs = slice(i * 512, (i + 1) * 512)
                bq = [0, 1, 3, 4][i % 4]
                bss = [2, 5][i % 2]
                rs, Brs = [(rstd2, Brstd[0]), (rstd2b, Brstd2b)][i % 2]
                k.do(ACT, [Bps[bss], Bconst], [Bssb[0]], lambda: sc.activation(out=ssb2[:], in_=ps[bss][:], func=AF.Ln, bias=epsT[:, 0:1]))
                k.do(ACT, [Bssb[0]], [Brs], lambda: sc.activation(out=rs[:], in_=ssb2[:], func=AF.Exp, scale=-0.5))
                t_ = tq[i % 2]
                k.do(DVE, [Bps[bq], BTT, Bconst], [Btq[i % 2]], lambda: ve.scalar_tensor_tensor(
                    t_[:], ps[bq][:], gp[:, 13:14], TT[:, cs], ALU.mult, ALU.mult))
                k.do(POOL, [Btq[i % 2], Brs], [BqT[hh][i]], lambda: gp_.tensor_tensor(
                    qT[hh][:, cs], t_[:], rs[:], ALU.mult))
            for i in range(NQ + 1):
                if i < NQ:
                    q_s1(i)
                if i >= 1:
                    q_s2(i - 1)
            k.do(POOL, [], [Bsq[0]], lambda: gp_.memset(sq2[0][64:128, :], 0.0))
            k.do(POOL, [], [Bsq[1]], lambda: gp_.memset(sq2[1][64:128, :], 0.0))

            def k_s1(i):
                cs = slice(i * 512, (i + 1) * 512)
                bk_ = [3, 4, 0, 1][i % 4]
                bss = [5, 2][i % 2]
                k.do(PE, [Bwm, Bckvn[i]], [Bps[bk_]], lambda: tn.matmul(
                    ps[bk_][:], wukk[:, h * 64:h * 64 + 128], ckvn[:, cs], start=True, stop=True))
                s_ = i % 2
                k.do(ACT, [Bps[bk_]], [Bsq[s_]], lambda: sc.activation(out=sq2[s_][0:64, :], in_=ps[bk_][0:64, :], func=AF.Square))
                k.do(PE, [Bsq[s_], Bconst], [Bps[bss]], lambda: tn.matmul(ps[bss][:], bo_q, sq2[s_][:], start=True, stop=True))

            def k_s2(i):
                cs = slice(i * 512, (i + 1) * 512)
                bk_ = [3, 4, 0, 1][i % 4]
                bss = [5, 2][i % 2]
                rs, Brs = [(rstd2, Brstd[0]), (rstd2b, Brstd2b)][i % 2]
                k.do(ACT, [Bps[bss], Bconst], [Bssb[0]], lambda: sc.activation(
                    out=ssb2[0:64, :], in_=ps[bss][0:64, :], func=AF.Ln, bias=epsT[0:64, 0:1]))
                k.do(ACT, [Bssb[0]], [Brs], lambda: sc.activation(out=rs[0:64, :], in_=ssb2[0:64, :], func=AF.Exp, scale=-0.5))
                k.do(DVE, [Bps[bk_], Brs, Bconst], [BkT[hh][i]], lambda: ve.scalar_tensor_tensor(
                    kT[hh][0:64, cs], ps[bk_][0:64, :], gp[0:64, 14:15], rs[0:64, :], ALU.mult, ALU.mult))
            for i in range(NQ + 1):
                if i < NQ:
                    k_s1(i)
                if i >= 1:
                    k_s2(i - 1)

    def phaseB_attn(hp):
        vs = hp % 2
        tiles = []
        for i in range(NQ):
            for hh in range(2):
                for j in range(4 * i + 4):
                    tiles.append((i, hh, j))
        ntile = len(tiles)

        def emit_S(n):
            i, hh, j = tiles[n]
            bk = [0, 1, 2][n % 3]
            jj = j - 4 * i
            c0 = 128 * jj if jj > 0 else 0

            def mm():
                o_ = ps[bk][:, c0:512]
                ins = tn.matmul(o_, kT[hh][:, j * 128:(j + 1) * 128], qT[hh][:, i * 512 + c0:(i + 1) * 512],
                                start=True, stop=(jj < 0))
                if jj >= 0:
                    ins = tn.matmul(ps[bk][:, c0:c0 + 128], ident, causal, start=False, stop=True)
                return ins
            return [Bconst, BkT[hh][j // 4], BqT[hh][i]], [Bps[bk]], mm

        def emit_E(n):
            i, hh, j = tiles[n]
            bk = [0, 1, 2][n % 3]
            jj = j - 4 * i
            c0 = 128 * jj if jj > 0 else 0
            k.do(ACT, [Bps[bk]], [BPTb[n % 4]], lambda: sc.activation(
                out=PTb[n % 4][:, c0:512], in_=ps[bk][:, c0:512], func=AF.Exp))

        def emit_PV(n):
            i, hh, j = tiles[n]
            jj = j - 4 * i
            c0 = 128 * jj if jj > 0 else 0
            bank = [[3, 5], [4, 6]][hh][i % 2]
            lhs = Vb[vs][:, j, 0:128] if hh == 0 else Vb[vs][:, j, 64:192]
            last = (j == 4 * i + 3)
            return [BPTb[n % 4], BVb[vs][j // 4]], [Bps[bank]], (lambda: tn.matmul(
                ps[bank][:, c0:512], lhs, PTb[n % 4][:, c0:512], start=(j == 0), stop=last)), (last and hh == 1), i

        def finalize(i):
            cs = slice(i * 512, (i + 1) * 512)
            be, bo_ = [3, 5][i % 2], [4, 6][i % 2]
            k.do(ACT, [Bps[be]], [BRtB], lambda: sc.activation(out=RtB[64:65, :], in_=ps[be][64:65, :], func=AF.Ln))
            k.do(ACT, [Bps[bo_]], [BRtB], lambda: sc.activation(out=RtB[0:1, :], in_=ps[bo_][0:1, :], func=AF.Ln))
            k.do(PE, [BRtB, Bconst], [Bps[7]], lambda: tn.matmul(ps[7][:], sel, RtB[:], start=True, stop=True))
            b_ = bcs[i % 2]
            k.do(ACT, [Bps[7], Bconst], [Bbcs[i % 2]], lambda: sc.activation(
                out=b_[:], in_=ps[7][:], func=AF.Exp, scale=-1.0, bias=lnhalf[:, 0:1]))
            t_ = tfb[i % 2]
            k.do(DVE, [Bps[be], Bbcs[i % 2]], [Btfb[i % 2]], lambda: ve.tensor_tensor(
                t_[0:64, :], ps[be][0:64, :], b_[0:64, :], ALU.mult))
            k.do(DVE, [Bps[bo_], Bbcs[i % 2]], [Btfb[i % 2]], lambda: ve.tensor_tensor(
                t_[64:128, :], ps[bo_][64:128, :], b_[64:128, :], ALU.mult))
            k.do(POOL, [Btfb[i % 2]], [BmixB[hp][i]], lambda: gp_.tensor_tensor(
                mixB[:, hp, cs], t_[:], mixB[:, hp, cs], ALU.mult))

        for n in range(ntile + 2):
            if n < ntile:
                sS = emit_S(n)
                k.do(PE, sS[0], sS[1], sS[2])
                emit_E(n)
            if n >= 2:
                sP = emit_PV(n - 2)
                k.do(PE, sP[0], sP[1], sP[2])
                if sP[3]:
                    finalize(sP[4])

    npairs_b = 4 if stage != 8 else 1
    wo = A("wo", R0, [128, 8, 1024], BF16)
    Bwo = Buf()
    dwo = k.dsem("d_wo")
    w_out_pref = [False]
    for hp in range(npairs_b):
        phaseB_prep(hp)
        if hp == 3:
            for c in range(8):
                k.dma(POOL, dwo, wo[:, c, :], w_out_v[:, c, :], writes=([BTT] if c == 0 else []))
            Bwo.w = (dwo, dwo.cnt)
            w_out_pref[0] = True
        phaseB_attn(hp)
    if stage in (8, 9):
        return finish((mixB[:],))
    k.barrier()

    o = OFF_HT
    o += 16384
    xo = [A(f"xo{i}", o + i * 4096, [128, D], F32) for i in range(3)]
    o += 12288
    yo = [A(f"yo{i}", o + i * 4096, [128, D], F32) for i in range(3)]
    o += 12288
    Bxo = [Buf() for _ in range(3)]
    Byo = [Buf() for _ in range(3)]
    dxo = [k.dsem(f"d_xo{i}") for i in range(3)]
    dyo = [k.dsem(f"d_yo{i}") for i in range(3)]
    if not w_out_pref[0]:
        for c in range(8):
            k.dma(POOL, dwo, wo[:, c, :], w_out_v[:, c, :], writes=[Bwo])
        Bwo.w = (dwo, dwo.cnt)
    y_v = y_d.rearrange("(n p) d -> n p d", p=128)

    def p5_load(j):
        k.dma(SP, dxo[j % 3], xo[j % 3][:], x_v[j], writes=[Bxo[j % 3]])

    p5_load(0)
    p5_load(1)
    for j in range(NT):
        if j + 2 < NT:
            p5_load(j + 2)
        ts = slice(j * 128, (j + 1) * 128)
        for half in range(2):
            bk = [0, 1, 2, 3][(2 * j + half) % 4]

            def mm():
                for c in range(8):
                    src = mixA if c < 4 else mixB
                    ins = tn.matmul(ps[bk][:], src[:, c % 4, ts], wo[:, c, half * 512:(half + 1) * 512],
                                    start=(c == 0), stop=(c == 7))
                return ins
            k.do(PE, [Bwo] + [BmixA[c][j // 4] for c in range(4)] + [BmixB[c][j // 4] for c in range(4)], [Bps[bk]], mm)
            k.do(DVE, [Bps[bk], Bxo[j % 3]], [Byo[j % 3]], lambda: ve.tensor_tensor(
                yo[j % 3][:, half * 512:(half + 1) * 512], ps[bk][:], xo[j % 3][:, half * 512:(half + 1) * 512], ALU.add))
        k.dma(SP, dyo[j % 3], y_v[j], yo[j % 3][:], reads=[Byo[j % 3]])
    k.barrier()
    k.es.close()
    return nc


def _t5_bucket(dist):
    d = np.maximum(dist.astype(np.float32), np.float32(1.0))
    large = 16 + (np.log(d / np.float32(16.0)) / np.float32(math.log(2048 / 16)) * np.float32(16.0)).astype(np.int32)
    large = np.minimum(large, 31)
    return np.where(dist < 16, dist, large)


def _consts():
    cbf = np.zeros((128, 6, 128), np.float32)
    p = np.arange(128)
    cbf[:, 0, :] = np.eye(128, dtype=np.float32)
    cbf[:, 1, :] = ((p[:, None] // 64) == (p[None, :] // 64)) / 64.0
    blk = np.where(p < 64, 0, np.where(p < 96, 1, 2))
    wgt = np.where(p < 64, 1.0 / 64, 1.0 / 32)
    cbf[:, 2, :] = (blk[:, None] == blk[None, :]) * wgt[None, :]
    cbf[:, 3, :] = 1.0 / 256
    cbf[:, 4, :] = 1.0 / 128
    cbf[:, 5, :] = np.where(p[None, :] >= p[:, None], 0.0, MASKV)
    cf = np.zeros((128, 2, 128), np.float32)
    cf[64, 0, 0:64] = 1.0
    cf[0, 0, 64:128] = 1.0
    for j in range(32):
        for a in (64 + j, 96 + j):
            for b in (64 + j, 96 + j):
                cf[a, 1, b] = 1.0
    cf2 = np.zeros((128, 2, 128), np.float32)
    cf2[64, 0, 0:64] = 1.0
    cf2[0, 1, 64:128] = 1.0
    rc = np.zeros((128, 4), np.float32)
    inv_freq = (np.float32(10000.0) ** (-np.arange(0, 32, 2, dtype=np.float32) / np.float32(32))).astype(np.float32)
    for rrow in range(64):
        prt = 64 + rrow
        rc[prt, 0] = inv_freq[rrow % 16]
        rc[prt, 1] = (math.pi / 2) if rrow < 32 else 0.0
        rc[prt, 2] = -1.0 if 32 <= rrow < 48 else 1.0
    return cbf, cf, cf2, rc


def _bias_tables(rel_bias):
    ki = np.arange(128)[:, None]
    c = np.arange(256)[None, :]
    j = c - ki
    valid = (j >= 0) & (j <= 128)
    bt = np.empty((4, 128, 6, 256), np.float32)
    for pi, r in enumerate(PATTERNS):
        bucket = _t5_bucket(np.maximum(j, 0) * r)
        for h in range(8):
            tab = np.where(valid, rel_bias[bucket, h], np.float32(MASKV)).astype(np.float32)
            bt[h // 2, :, pi * 2 + (h % 2), :] = tab
    return bt


_NC_CACHE = {}


def _prep_shared(inputs):
    f = lambda n: np.asarray(inputs[n], np.float32)
    w_in = np.ascontiguousarray(f("w_in")[0])
    kr = w_in[:, 2432:2464]
    w_kr = np.ascontiguousarray(np.concatenate([np.zeros((D, 64), np.float32), kr, kr[:, 16:32], kr[:, 0:16]], axis=1))
    w_uq = f("w_uq")[0]
    w_uq_ext = np.empty((256, 1024), np.float32)
    for h in range(8):
        blk = w_uq[:, h * 96:(h + 1) * 96]
        w_uq_ext[:, h * 128:h * 128 + 96] = blk
        w_uq_ext[:, h * 128 + 96:h * 128 + 112] = blk[:, 80:96]
        w_uq_ext[:, h * 128 + 112:h * 128 + 128] = blk[:, 64:80]
    w_ukv = f("w_ukv")[0].reshape(128, 8, 128)
    w_ukvk = np.ascontiguousarray(np.concatenate([w_ukv[:, :, 0:64].reshape(128, 512), np.zeros((128, 64), np.float32)], axis=1))
    w_ukvv = np.ascontiguousarray(w_ukv[:, :, 64:128].reshape(128, 512))
    w_out = np.ascontiguousarray(f("w_out")[0])
    gp = np.zeros((128, 16), np.float32)
    gp[:, 0:8] = f("norm_gain")[0].reshape(8, 128).T
    gp[:, 8] = np.tile(f("a_q_gain")[0], 2)
    gp[:, 9] = np.tile(f("a_k_gain")[0], 2)
    gp[:, 10:12] = f("q_c_gain")[0].reshape(2, 128).T
    gp[:, 12] = f("kv_c_gain")[0]
    qr = f("qr_gain")[0]
    kr_g = f("kr_gain")[0]
    gp[:, 13] = np.concatenate([f("qn_gain")[0], qr, qr[16:32], qr[0:16]])
    gp[:, 14] = np.concatenate([f("kn_gain")[0], kr_g, kr_g[16:32], kr_g[0:16]])
    cbf, cf, cf2, rc = _consts()
    bt = _bias_tables(f("rel_bias"))
    return dict(w_in=w_in, w_kr=w_kr, w_uq=w_uq_ext, w_ukvk=w_ukvk, w_ukvv=w_ukvv, w_out=w_out,
                cbf=cbf, cf=cf, cf2=cf2, gp=gp, rc=rc, bt=bt)


def kernel(**inputs):
    x = np.asarray(inputs["x"], np.float32)
    pos = np.asarray(inputs["positions"], np.int32)
    shared = _prep_shared(inputs)
    if "nc" not in _NC_CACHE:
        _NC_CACHE["nc"] = build()
    nc = _NC_CACHE["nc"]
    in_maps = []
    for b in range(8):
        m = dict(shared)
        m["x"] = np.ascontiguousarray(x[b])
        m["pos"] = np.ascontiguousarray(pos[b].reshape(1, S))
        in_maps.append(m)
    res = run_bass_kernel_spmd(nc, in_maps, core_ids=list(range(8)))
    return np.stack([np.asarray(r["y"], np.float32) for r in res.results], axis=0)
```

```python
import contextlib
import math
import numpy as np
import concourse.bass as bass
import concourse.mybir as mybir
from concourse.bass_utils import run_bass_kernel_spmd

F32 = mybir.dt.float32
BF16 = mybir.dt.bfloat16
I32 = mybir.dt.int32
AF = mybir.ActivationFunctionType
ALU = mybir.AluOpType

S = 4096
D = 1024
NT = 32
NQ = 8
NCOLS = 2976
MASKV = -30000.0
EPS = 1e-6
PATTERNS = (1, 4, 16)
QSCALE_A = 64 ** -0.5
QSCALE_B = 96 ** -0.5
SB_BASE = 16512
SB_LIMIT = 229344


def _ssl(t0, n, r):
    return slice(t0, t0 + (n - 1) * r + 1, r)


class Sem:
    def __init__(self, k, name):
        self.name = name
        self.sem = k.es.enter_context(k.nc.semaphore(name))
        self.cnt = 0


class Eng(Sem):
    def __init__(self, k, name, eng):
        super().__init__(k, "s_" + name)
        self.eng = eng
        self.seen = {}

    def wait(self, *toks):
        for t in toks:
            if t is None:
                continue
            if isinstance(t, list):
                self.wait(*t)
                continue
            p, c = t
            if self.seen.get(p.name, 0) >= c:
                continue
            self.eng.wait_ge(p.sem, c)
            self.seen[p.name] = c

    def done(self, ins):
        ins.then_inc(self.sem, 1)
        self.cnt += 1
        return (self, self.cnt)


class Buf:
    def __init__(self, name="", excl=False):
        self.name = name
        self.excl = excl
        self.w = None
        self.r = {}

    def add_r(self, tok):
        p, c = tok
        if self.r.get(p.name, (None, 0))[1] < c:
            self.r[p.name] = tok

    def set_w(self, tok):
        self.w = tok
        self.r = {}


class K:
    def __init__(self):
        self.nc = bass.Bass("TRN2", target_bir_lowering=False)
        self.es = contextlib.ExitStack()
        nc = self.nc
        self.PE = Eng(self, "pe", nc.tensor)
        self.ACT = Eng(self, "act", nc.scalar)
        self.DVE = Eng(self, "dve", nc.vector)
        self.POOL = Eng(self, "pool", nc.gpsimd)
        self.SP = Eng(self, "sp", nc.sync)
        self.engines = [self.PE, self.ACT, self.DVE, self.POOL, self.SP]
        self.dsems = []

    def dsem(self, name):
        d = Sem(self, name)
        self.dsems.append(d)
        return d

    def wait_for(self, E, reads, writes):
        for b in reads:
            E.wait(b.w)
            if b.excl:
                E.wait(*[t for n, t in b.r.items() if n != E.name])
        for b in writes:
            E.wait(b.w)
            E.wait(*b.r.values())

    def commit(self, E, ins, reads, writes):
        tok = E.done(ins)
        for b in reads:
            b.add_r(tok)
        for b in writes:
            b.set_w(tok)
        return tok

    def do(self, E, reads, writes, emit):
        self.wait_for(E, reads, writes)
        ins = emit()
        return self.commit(E, ins, reads, writes)

    def dma(self, Q, ds, out, in_, reads=(), writes=()):
        for b in reads:
            Q.wait(b.w)
        for b in writes:
            Q.wait(b.w)
            Q.wait(*b.r.values())
        Q.eng.dma_start(out=out, in_=in_).then_inc(ds.sem, 16)
        ds.cnt += 16
        tok = (ds, ds.cnt)
        for b in reads:
            b.add_r(tok)
        for b in writes:
            b.set_w(tok)
        return tok

    def barrier(self):
        for E in self.engines:
            for F in self.engines:
                if F is not E and F.cnt > 0:
                    E.wait((F, F.cnt))
            for d in self.dsems:
                if d.cnt > 0:
                    E.wait((d, d.cnt))


def build(stage=99, dbg=None):
    k = K()
    nc = k.nc
    PE, ACT, DVE, POOL, SP = k.PE, k.ACT, k.DVE, k.POOL, k.SP
    tn, sc, ve, gp_, sy = nc.tensor, nc.scalar, nc.vector, nc.gpsimd, nc.sync

    def dram(name, shape, dt, kind="ExternalInput"):
        return nc.dram_tensor(name, shape, dt, kind=kind)

    x_d = dram("x", [S, D], F32).ap()
    pos_t = dram("pos", [1, S], I32)
    w_in_d = dram("w_in", [D, NCOLS], F32).ap()
    w_kr_d = dram("w_kr", [D, 128], F32).ap()
    w_uq_d = dram("w_uq", [256, 1024], F32).ap()
    w_ukvk_d = dram("w_ukvk", [128, 576], F32).ap()
    w_ukvv_d = dram("w_ukvv", [128, 512], F32).ap()
    w_out_d = dram("w_out", [D, D], F32).ap()
    cbf_d = dram("cbf", [128, 6, 128], F32).ap()
    cf_d = dram("cf", [128, 2, 128], F32).ap()
    cf2_d = dram("cf2", [128, 2, 128], F32).ap()
    gp_d = dram("gp", [128, 16], F32).ap()
    rc_d = dram("rc", [128, 4], F32).ap()
    bt_d = dram("bt", [4, 128, 6, 256], F32).ap()
    y_d = dram("y", [S, D], F32, kind="ExternalOutput").ap()
    dbg_d = None
    if dbg is not None:
        dbg_d = dram("dbg", list(dbg[1]), dbg[2], kind="ExternalOutput").ap()

    w_in_v = w_in_d.rearrange("(c p) n -> p c n", p=128)
    w_kr_v = w_kr_d.rearrange("(c p) n -> p c n", p=128)
    w_uq_v = w_uq_d.rearrange("(c p) n -> p c n", p=128)
    w_out_v = w_out_d.rearrange("(c p) n -> p c n", p=128)

    def A(name, off, shape, dt):
        sz = int(np.prod(shape[1:])) * (4 if dt in (F32, I32) else 2)
        assert off % 32 == 0 and SB_BASE + off + sz <= SB_LIMIT, (name, off, sz)
        return nc.alloc_sbuf_tensor_at(name, list(shape), dt, offset=SB_BASE + off)

    cbf = A("cbf", 0, [128, 6, 128], BF16)
    cf = A("cf", 1536, [128, 2, 128], F32)
    gp = A("gp", 2560, [128, 16], F32)
    rc = A("rc", 2624, [128, 4], F32)
    st_ss = A("st_ss", 2656, [128, 4], F32)
    st_t = A("st_t", 2688, [128, 4], F32)
    st_rs = A("st_rs", 2720, [128, 4], F32)
    cm05 = A("cm05", 2752, [128, 1], F32)
    epsT = A("epsT", 2784, [128, 1], F32)
    lnhalf = A("lnhalf", 2816, [128, 1], F32)
    cf2 = A("cf2", 2880, [128, 2, 128], F32)
    OFF_MIXA, OFF_MIXB, OFF_HT, OFF_R = 4096, 36864, 69632, 135168
    mixA = A("mixA", OFF_MIXA, [128, 4, S], BF16)
    mixB = A("mixB", OFF_MIXB, [128, 4, S], BF16)
    acc_e = A("acc_e", OFF_MIXB, [128, S], F32)
    acc_o = A("acc_o", OFF_MIXB + 16384, [128, S], F32)
    hT = A("hT", OFF_HT, [128, 8, S], BF16)
    BhT = [Buf() for _ in range(NT)]
    BmixA = [[Buf() for _ in range(NQ)] for _ in range(4)]
    BmixB = [[Buf() for _ in range(NQ)] for _ in range(4)]

    ident = cbf[:, 0, :]
    bo64 = cbf[:, 1, :]
    bo_q = cbf[:, 2, :]
    ones256 = cbf[:, 3, :]
    ones128 = cbf[:, 4, :]
    causal = cbf[:, 5, :]
    sel = cf[:, 0, :]
    msum = cf[:, 1, :]

    ps = [k.es.enter_context(nc.psum_tensor(f"ps{i}", [128, 512], F32)) for i in range(8)]
    Bps = [Buf(excl=True) for _ in range(8)]

    dconst = k.dsem("d_const")
    Bconst = Buf()

    dconst2 = k.dsem("d_const2")
    Bcbf = Buf()
    k.dma(POOL, dconst2, cbf[:], cbf_d, writes=[Bcbf])
    k.dma(SP, dconst, cf[:], cf_d, writes=[Bconst])
    k.dma(SP, dconst, cf2[:], cf2_d, writes=[Bconst])
    k.dma(SP, dconst, gp[:], gp_d, writes=[Bconst])
    k.dma(SP, dconst, rc[:], rc_d, writes=[Bconst])
    Bconst.w = (dconst, dconst.cnt)
    k.do(DVE, [], [Bconst], lambda: ve.tensor_scalar(gp[:, 8:9], gp[:, 8:9], QSCALE_A, None, ALU.mult))
    k.do(DVE, [], [Bconst], lambda: ve.tensor_scalar(gp[:, 13:14], gp[:, 13:14], QSCALE_B, None, ALU.mult))
    k.do(POOL, [], [Bconst], lambda: gp_.memset(cm05[:], -0.5))
    k.do(POOL, [], [Bconst], lambda: gp_.memset(epsT[:], EPS))
    k.do(POOL, [], [Bconst], lambda: gp_.memset(lnhalf[:], math.log(0.5)))
    k.barrier()

    dout = k.dsem("d_out")

    def finish(dump=None):
        if dump is not None:
            k.barrier()
            src, = dump
            k.dma(SP, dout, dbg_d, src)
        k.barrier()
        SP.wait((dout, dout.cnt)) if dout.cnt else None
        k.es.close()
        return nc

    def load_w_scaled(dsm, stage, Bstage, dst, Bdst, src_ap, ncols):
        k.dma(SP, dsm, stage[:, :, 0:ncols], src_ap, writes=[Bstage])
        k.do(POOL, [Bstage, Bconst], [Bdst], lambda: gp_.tensor_tensor(
            dst[:, :, 0:ncols], stage[:, :, 0:ncols],
            gp[:, 0:8].unsqueeze(2).to_broadcast([128, 8, ncols]), ALU.mult))

    R0 = OFF_R
    xbuf = [A(f"xbuf{i}", R0 + i * 4096, [128, D], F32) for i in range(3)]
    xn = [A(f"xn{i}", R0 + 12288 + i * 2048, [128, D], BF16) for i in range(2)]
    junk = A("junk", R0 + 16384, [128, D], BF16)
    Bx = [Buf() for _ in range(3)]
    Bxn = [Buf() for _ in range(2)]
    Bjunk = Buf()
    Bst = [Buf() for _ in range(2)]
    dx = [k.dsem(f"d_x{i}") for i in range(3)]
    x_v = x_d.rearrange("(n p) d -> n p d", p=128)

    def ph1_load(j):
        k.dma(SP, dx[j % 3], xbuf[j % 3][:], x_v[j], writes=[Bx[j % 3]])

    def ph1_a(j):
        s = j % 2
        k.do(ACT, [Bx[j % 3]], [Bjunk, Bst[s]], lambda: sc.activation(
            out=junk[:], in_=xbuf[j % 3][:], func=AF.Square, accum_out=st_ss[:, s:s + 1]))
        k.do(DVE, [], [Bst[s]], lambda: ve.tensor_scalar(
            st_t[:, s:s + 1], st_ss[:, s:s + 1], 1.0 / D, EPS, ALU.mult, ALU.add))
        k.do(POOL, [Bconst], [Bst[s]], lambda: gp_.tensor_tensor(
            st_rs[:, s:s + 1], st_t[:, s:s + 1], cm05[:], ALU.pow))
        k.do(DVE, [Bx[j % 3], Bst[s]], [Bxn[s]], lambda: ve.tensor_scalar(
            xn[s][:], xbuf[j % 3][:], st_rs[:, s:s + 1], None, ALU.mult))

    def ph1_b(j):
        s = j % 2
        pt = ps[s][:].bitcast(BF16)

        def tr():
            for c in range(8):
                ins = tn.transpose(pt[:, c * 128:(c + 1) * 128], xn[s][:, c * 128:(c + 1) * 128], ident)
            return ins
        k.do(PE, [Bxn[s], Bconst], [Bps[s]], tr)
        if j % 2 == 0:
            k.do(ACT, [Bps[s]], [BhT[j]], lambda: sc.activation(
                out=hT[:, :, j * 128:(j + 1) * 128], in_=pt.rearrange("p (c t) -> p c t", c=8), func=AF.Copy))
        else:
            k.do(DVE, [Bps[s]], [BhT[j]], lambda: ve.tensor_copy(
                hT[:, :, j * 128:(j + 1) * 128], pt.rearrange("p (c t) -> p c t", c=8)))

    ph1_load(0)
    ph1_load(1)
    for j in range(NT):
        if j + 2 < NT:
            ph1_load(j + 2)
        ph1_a(j)
        if j >= 1:
            ph1_b(j - 1)
    ph1_b(NT - 1)
    if stage == 1:
        return finish((hT[:],))
    k.barrier()

    wst = A("wst", R0, [128, 8, 128], F32)
    Bwst = Buf()
    dw = k.dsem("d_w")

    def gates(col0, dst, Bdst, wg, Bwg, th, Bth, banks, wst, Bwst, dw):
        for c in range(4):
            w = wg[c % 2]
            load_w_scaled(dw, wst, Bwst, w, Bwg[c % 2], w_in_v[:, :, col0 + c * 128: col0 + (c + 1) * 128], 128)
            for i in range(NQ):
                n = c * NQ + i
                bk = banks[n % len(banks)]

                def mm():
                    for dmc in range(8):
                        ins = tn.matmul(ps[bk][:], w[:, dmc, :], hT[:, dmc, i * 512:(i + 1) * 512],
                                        start=(dmc == 0), stop=(dmc == 7))
                    return ins
                k.do(PE, [Bwg[c % 2]] + BhT[4 * i:4 * i + 4], [Bps[bk]], mm)
                t_ = th[n % 2]
                k.do(ACT, [Bps[bk]], [Bth[n % 2]], lambda: sc.activation(
                    out=t_[:], in_=ps[bk][:], func=AF.Tanh, scale=0.5))
                k.do(DVE, [Bps[bk], Bth[n % 2]], [Bdst[c][i]], lambda: ve.scalar_tensor_tensor(
                    dst[:, c, i * 512:(i + 1) * 512], t_[:], 1.0, ps[bk][:], ALU.add, ALU.mult))

    wg = [A(f"wg{i}", R0 + 4096 + i * 2048, [128, 8, 128], BF16) for i in range(2)]
    Bwg = [Buf(), Buf()]
    th = [A(f"th{i}", R0 + 8192 + i * 2048, [128, 512], F32) for i in range(2)]
    Bth = [Buf(), Buf()]
    gates(1536, mixA, BmixA, wg, Bwg, th, Bth, [0, 1, 2], wst, Bwst, dw)
    if stage == 2:
        return finish((mixA[:],))
    k.barrier()

    o = R0
    Qzz = A("Qzz", o, [128, 2, S], BF16)
    Qz = [Qzz[:, 0, :], Qzz[:, 1, :]]
    o += 16384
    KT = A("KT", o, [128, S], BF16)
    o += 8192
    Vp = A("Vp", o, [128, 32, 192], BF16)
    o += 12288
    btab = A("btab", o, [128, 6, 256], BF16)
    o += 3072
    wq = [A("wq0", o, [128, 8, 128], BF16)] * 2
    o += 2048
    wk = [A("wk0", o, [128, 8, 128], BF16)] * 2
    o += 2048
    wv = [A("wv0", o, [128, 8, 128], BF16)] * 2
    o += 2048
    VT = A("VT", o, [128, S], BF16)
    o += 8192
    stg = [A(f"stg{i}", o + i * 2048, [128, 512], F32) for i in range(2)]
    o += 4096
    wstA = A("wstA", o, [128, 8, 128], F32)
    o += 4096
    PT = [A(f"PT{i}", o + i * 1024, [128, 512], BF16) for i in range(4)]
    o += 4096
    sq = [A(f"sq{i}", o + i * 1024, [128, 512], BF16) for i in range(2)]
    o += 2048
    ssb = [A("ssb0", o, [128, 512], F32)] * 2
    o += 2048
    rstd = [A("rstd0", o, [128, 512], F32)] * 2
    o += 2048
    Rt = A("Rt", o, [128, 512], F32)
    o += 2048
    tfin = [A("tfin0", o, [128, 512], F32), stg[0]]
    o += 2048
    assert SB_BASE + o <= SB_LIMIT, o
    BQz = [[Buf() for _ in range(NQ)] for _ in range(2)]
    BKT = [Buf() for _ in range(NQ)]
    BVp = [Buf() for _ in range(4)]
    Bbt = Buf()
    Bwq, Bwk, Bwv = [Buf()] * 2, [Buf()] * 2, [Buf()] * 2
    BVT = [Buf() for _ in range(NQ)]
    Bstg = [Buf(), Buf()]
    BwstA = Buf()
    BPT = [Buf() for _ in range(4)]
    Bsq, Bssb, Brstd = [Buf(), Buf()], [Buf()] * 2, [Buf()] * 2
    BRt = Buf()
    Btfin = [Buf(), Buf()]
    Bacc = [Buf(), Buf()]
    dwa = k.dsem("d_wa")
    dbt = k.dsem("d_bt")

    BQzAll = Buf()
    k.do(POOL, [], [BQzAll], lambda: gp_.memset(Qz[0][64:128, :], 0.0))
    k.do(POOL, [], [BQzAll], lambda: gp_.memset(Qz[1][0:64, :], 0.0))
    BVpE = Buf()
    k.do(POOL, [], [BVpE], lambda: gp_.memset(Vp[:, :, 64:128], 1.0))
    k.do(POOL, [], [BRt], lambda: gp_.memset(Rt[:], 0.0))
    k.do(POOL, [], [Bacc[0]], lambda: gp_.memset(acc_e[64:128, :], 0.0))
    for b_ in BQz[0] + BQz[1]:
        b_.w = BQzAll.w
    for b_ in BVp:
        b_.w = BVpE.w

    nrm_ctr = [0]

    def blocknorm(bk_proj, bk_ss, bo_list, extra_reads=()):
        n = nrm_ctr[0]
        nrm_ctr[0] += 1
        s = n % 2
        for ci, bk in enumerate(bk_proj):
            k.do(ACT, [Bps[bk]], [Bsq[s]], lambda bk=bk: sc.activation(out=sq[s][:], in_=ps[bk][:], func=AF.Square))
            k.do(PE, [Bsq[s], Bconst], [Bps[bk_ss]], lambda ci=ci: tn.matmul(
                ps[bk_ss][:], bo_list, sq[s][:], start=(ci == 0), stop=(ci == len(bk_proj) - 1)))
        k.do(ACT, [Bps[bk_ss], Bconst], [Bssb[s]], lambda: sc.activation(
            out=ssb[s][:], in_=ps[bk_ss][:], func=AF.Ln, bias=epsT[:, 0:1]))
        k.do(ACT, [Bssb[s]], [Brstd[s]], lambda: sc.activation(out=rstd[s][:], in_=ssb[s][:], func=AF.Exp, scale=-0.5))
        return s

    def load_pair_weights(hp):
        s = hp % 2
        load_w_scaled(dwa, wstA, BwstA, wq[s], Bwq[s], w_in_v[:, :, hp * 128:(hp + 1) * 128], 128)
        load_w_scaled(dwa, wstA, BwstA, wk[s], Bwk[s], w_in_v[:, :, 512 + hp * 128:512 + (hp + 1) * 128], 128)
        load_w_scaled(dwa, wstA, BwstA, wv[s], Bwv[s], w_in_v[:, :, 1024 + hp * 128:1024 + (hp + 1) * 128], 128)

    def proj512(bk, w, Bw, i):
        def mm():
            for dmc in range(8):
                ins = tn.matmul(ps[bk][:], w[:, dmc, :], hT[:, dmc, i * 512:(i + 1) * 512],
                                start=(dmc == 0), stop=(dmc == 7))
            return ins
        k.do(PE, [Bw] + BhT[4 * i:4 * i + 4], [Bps[bk]], mm)

    def phaseA_pair(hp):
        s = hp % 2
        k.dma(POOL, dbt, btab[:], bt_d[hp], writes=[Bbt])
        for i in range(NQ):
            bq = [0, 1][i % 2]
            proj512(bq, wq[s], Bwq[s], i)
            r_ = blocknorm([bq], 2, bo64)
            k.do(DVE, [Bps[bq], Brstd[r_], Bconst], [BQz[0][i]], lambda: ve.scalar_tensor_tensor(
                Qz[0][0:64, i * 512:(i + 1) * 512], ps[bq][0:64, :], gp[0:64, 8:9], rstd[r_][0:64, :], ALU.mult, ALU.mult))
            k.do(DVE, [Bps[bq], Brstd[r_], Bconst], [BQz[1][i]], lambda: ve.scalar_tensor_tensor(
                Qz[1][64:128, i * 512:(i + 1) * 512], ps[bq][64:128, :], gp[64:128, 8:9], rstd[r_][64:128, :], ALU.mult, ALU.mult))
            bk_ = [3, 4][i % 2]
            proj512(bk_, wk[s], Bwk[s], i)
            r_ = blocknorm([bk_], 5, bo64)
            k.do(DVE, [Bps[bk_], Brstd[r_], Bconst], [BKT[i]], lambda: ve.scalar_tensor_tensor(
                KT[:, i * 512:(i + 1) * 512], ps[bk_][:], gp[:, 9:10], rstd[r_][:], ALU.mult, ALU.mult))
        for i in range(NQ):
            bv = [6, 7][i % 2]
            proj512(bv, wv[s], Bwv[s], i)
            k.do(DVE, [Bps[bv]], [BVT[i]], lambda: ve.tensor_copy(VT[:, i * 512:(i + 1) * 512], ps[bv][:]))
        if hp + 1 < npairs_a:
            load_pair_weights(hp + 1)
        for pi, r in enumerate(PATTERNS):
            nb = NT // r
            for g8 in range(4):
                bk = [6, 7][g8 % 2]
                ptv_ = ps[bk][:].bitcast(BF16)

                def mmv():
                    for tt in range(8):
                        n = g8 * 8 + tt
                        b, m = n // nb, n % nb
                        t0 = 128 * m * r + b
                        ins = tn.transpose(ptv_[:, tt * 128:(tt + 1) * 128], VT[:, _ssl(t0, 128, r)], ident)
                    return ins
                k.do(PE, [Bconst] + BVT, [Bps[bk]], mmv)
                k.do(ACT, [Bps[bk]], [BVp[g8]], lambda: sc.activation(
                    out=Vp[:, g8 * 8:(g8 + 1) * 8, :].rearrange("p n (s d) -> p n s d", s=3)[:, :, 0:3:2, :],
                    in_=ptv_.rearrange("p (n s d) -> p n s d", n=8, s=2), func=AF.Copy))
            tiles = [(b, m) for b in range(r) for m in range(nb)]
            oset_ctr = [0]

            def emit_S(n):
                b, m = tiles[n]
                bk = [0, 1, 2][n % 3]
                nq = 256 if m < nb - 1 else 128
                t0 = 128 * m * r + b

                def mm():
                    if nq == 256:
                        o_ = ps[bk][:].rearrange("p (h q) -> p h q", h=2)[:, :, 0:nq]
                        tn.matmul(o_, KT[:, _ssl(t0, 128, r)], Qzz[:, :, _ssl(t0, nq, r)], start=True, stop=False)
                        return tn.matmul(o_, ident, btab[:, pi * 2:pi * 2 + 2, 0:nq], start=False, stop=True)
                    for h in range(2):
                        o_ = ps[bk][:, h * 256:h * 256 + nq]
                        tn.matmul(o_, KT[:, _ssl(t0, 128, r)], Qz[h][:, _ssl(t0, nq, r)], start=True, stop=False)
                        ins = tn.matmul(o_, ident, btab[:, pi * 2 + h, 0:nq], start=False, stop=True)
                    return ins
                qtiles = sorted(set([(t0) // 512, min((t0 + nq * r - 1) // 512, NQ - 1)]))
                ktile = sorted(set([t0 // 512, min((t0 + 128 * r - 1) // 512, NQ - 1)]))
                rd = [Bbt, Bconst] + [BKT[x] for x in range(ktile[0], ktile[-1] + 1)] \
                    + [BQz[h][x] for h in range(2) for x in range(qtiles[0], qtiles[-1] + 1)]
                return rd, [Bps[bk]], mm

            def emit_E(n):
                b, m = tiles[n]
                bk = [0, 1, 2][n % 3]
                nq = 256 if m < nb - 1 else 128
                pv = ps[bk][:].rearrange("p (h q) -> p h q", h=2)[:, :, 0:nq]
                sg = stg[n % 2][:].rearrange("p (h q) -> p h q", h=2)[:, :, 0:nq]
                ptv = PT[n % 4][:].rearrange("p (h q) -> p h q", h=2)[:, :, 0:nq]
                k.do(ACT, [Bps[bk]], [BPT[n % 4]], lambda: sc.activation(out=ptv, in_=pv, func=AF.Exp))

            def emit_PV(n):
                b, m = tiles[n]
                cs = (m % 4) * 128
                st_ = oset_ctr[0] % 2
                be, bo_ = (3, 4) if st_ == 0 else (5, 6)
                vt = b * nb + m

                ne, no_ = (5, 6) if st_ == 0 else (3, 4)
                wr = [Bps[be], Bps[bo_]]
                if m % 4 == 3 and m < nb - 1:
                    wr += [Bps[ne], Bps[no_]]

                def mm():
                    for h in range(2):
                        bank, nbank = (be, ne) if h == 0 else (bo_, no_)
                        lhs = Vp[:, vt, 0:128] if h == 0 else Vp[:, vt, 64:192]
                        p0 = h * 256
                        if m == nb - 1:
                            ins = tn.matmul(ps[bank][:, cs:cs + 128], lhs, PT[n % 4][:, p0:p0 + 128],
                                            start=False, stop=True, skip_group_check=True)
                        elif m % 4 == 3:
                            tn.matmul(ps[bank][:, cs:cs + 128], lhs, PT[n % 4][:, p0:p0 + 128],
                                      start=False, stop=True, skip_group_check=True)
                            ins = tn.matmul(ps[nbank][:, 0:128], lhs, PT[n % 4][:, p0 + 128:p0 + 256],
                                            start=True, stop=True, skip_group_check=True)
                        else:
                            ins = tn.matmul(ps[bank][:, cs:cs + 256], lhs, PT[n % 4][:, p0:p0 + 256],
                                            start=(m == 0), stop=True, skip_group_check=True)
                    return ins
                rd = [BPT[n % 4], BVp[vt // 8]]
                return rd, wr, mm, (lambda: post_PV(n, b, m, cs, be, bo_))

            def post_PV(n, b, m, cs, be, bo_):
                if m % 4 == 3 or m == nb - 1:
                    ncol = cs + 128
                    m0 = m - (m % 4)
                    a0 = 128 * m0 * r + b
                    sl = _ssl(a0, ncol, r)
                    if pi == 0:
                        k.do(DVE, [Bps[be]], [Bacc[0]], lambda: ve.tensor_copy(acc_e[0:65, sl], ps[be][0:65, 0:ncol]))
                        k.do(DVE, [Bps[bo_]], [Bacc[1]], lambda: ve.tensor_copy(acc_o[:, sl], ps[bo_][:, 0:ncol]))
                    else:
                        k.do(DVE, [Bps[be]], [Bacc[0]], lambda: ve.tensor_tensor(
                            acc_e[0:65, sl], ps[be][0:65, 0:ncol], acc_e[0:65, sl], ALU.add))
                        k.do(DVE, [Bps[bo_]], [Bacc[1]], lambda: ve.tensor_tensor(
                            acc_o[:, sl], ps[bo_][:, 0:ncol], acc_o[:, sl], ALU.add))
                    oset_ctr[0] += 1

            ntile = len(tiles)
            for n in range(ntile + 2):
                if n < ntile:
                    sS = emit_S(n)
                    k.do(PE, sS[0], sS[1], sS[2])
                    emit_E(n)
                if n >= 2:
                    sP = emit_PV(n - 2)
                    k.do(PE, sP[0], sP[1], sP[2])
                    sP[3]()
        k.do(ACT, [], [Bacc[0]], lambda: sc.activation(out=acc_e[64:65, :], in_=acc_e[64:65, :], func=AF.Ln))
        k.do(ACT, [], [Bacc[1]], lambda: sc.activation(out=acc_o[0:1, :], in_=acc_o[0:1, :], func=AF.Ln))
        for i in range(NQ):
            cs = slice(i * 512, (i + 1) * 512)
            bkf = [7, 2][i % 2]

            def mmf():
                tn.matmul(ps[bkf][:], cf2[:, 0, :], acc_e[:, cs], start=True, stop=False)
                return tn.matmul(ps[bkf][:], cf2[:, 1, :], acc_o[:, cs], start=False, stop=True)
            k.do(PE, [Bacc[0], Bacc[1], Bconst], [Bps[bkf]], mmf)
            t_ = tfin[i % 2]
            k.do(ACT, [Bps[bkf], Bconst], [Btfin[i % 2]], lambda: sc.activation(
                out=t_[:], in_=ps[bkf][:], func=AF.Exp, scale=-1.0, bias=lnhalf[:, 0:1]))
            k.do(DVE, [Bacc[0]], [Btfin[i % 2]], lambda: ve.tensor_tensor(
                t_[0:64, :], acc_e[0:64, cs], t_[0:64, :], ALU.mult))
            k.do(DVE, [Bacc[1]], [Btfin[i % 2]], lambda: ve.tensor_tensor(
                t_[64:128, :], acc_o[64:128, cs], t_[64:128, :], ALU.mult))
            k.do(POOL, [Btfin[i % 2]], [BmixA[hp][i]], lambda: gp_.tensor_tensor(
                mixA[:, hp, cs], t_[:], mixA[:, hp, cs], ALU.mult))

    npairs_a = 4 if stage != 3 else 1
    load_pair_weights(0)
    for hp in range(npairs_a):
        phaseA_pair(hp)
    if stage in (3, 4):
        return finish((mixA[:],))
    k.barrier()

    o = R0
    TT = A("TT", o, [128, S], F32)
    TTi = A("TTi", OFF_MIXB, [128, S], I32)
    o += 16384
    cqn = A("cqn", o, [128, 2, S], BF16)
    o += 16384
    ckvn = A("ckvn", o, [128, S], BF16)
    o += 8192
    kT = [A(f"kT{i}", o + i * 8192, [128, S], BF16) for i in range(2)]
    o += 16384
    OFF_W = o
    wb = [A(f"wb{i}", o + i * 2048, [128, 8, 128], BF16) for i in range(2)]
    o += 4096
    wstB = A("wstB", o, [128, 8, 128], F32)
    o += 4096
    sq2 = [A(f"sq2{i}", o + i * 1024, [128, 512], BF16) for i in range(2)]
    o += 2048
    ssb2 = A("ssb2", o, [128, 512], F32)
    o += 2048
    rstd2 = A("rstd2", o, [128, 512], F32)
    o += 2048
    tk = A("tk", o, [128, 512], F32)
    o += 2048
    rstd2b = A("rstd2b", o, [128, 512], F32)
    Brstd2b = Buf()
    ta = A("ta", OFF_W + 4096, [128, 512], F32)
    tu = A("tu", OFF_W + 6144, [128, 512], F32)
    assert SB_BASE + o <= SB_LIMIT, o
    sq[0], sq[1] = sq2[0], sq2[1]
    ssb[0] = ssb[1] = ssb2
    rstd[0] = rstd[1] = rstd2
    Bsq[0], Bsq[1] = Buf(), Buf()
    Bssb[0] = Bssb[1] = Buf()
    Brstd[0] = Brstd[1] = Buf()
    BTT = Buf()
    Bcqn = [Buf() for _ in range(NQ)]
    Bckvn = [Buf() for _ in range(NQ)]
    BkT = [[Buf() for _ in range(NQ)] for _ in range(2)]
    Bwb = [Buf(), Buf()]
    BwstB = Buf()
    Btk, Bta, Btu = Buf(), Buf(), Buf()
    dwb = k.dsem("d_wb")
    dpos = k.dsem("d_pos")

    k.do(POOL, [], [BTT], lambda: gp_.memset(TT[0:64, :], 1.0))
    k.dma(SP, dpos, TTi[64:128, :], bass.AP(pos_t, 0, [[0, 64], [1, S]]), writes=[BTT])
    k.do(DVE, [], [BTT], lambda: ve.tensor_copy(TT[64:128, :], TTi[64:128, :]))
    MAGIC = 12582912.0
    TWO_PI = 2.0 * math.pi
    for i in range(NQ):
        cs = slice(i * 512, (i + 1) * 512)
        k.do(DVE, [BTT, Bconst], [Bta], lambda: ve.tensor_scalar(
            ta[64:128, :], TT[64:128, cs], rc[64:128, 0:1], rc[64:128, 1:2], ALU.mult, ALU.add))
        k.do(DVE, [Bta], [Btu], lambda: ve.tensor_scalar(tu[64:128, :], ta[64:128, :], 1.0 / TWO_PI, MAGIC, ALU.mult, ALU.add))
        k.do(DVE, [], [Btu], lambda: ve.tensor_scalar(tu[64:128, :], tu[64:128, :], -MAGIC, None, ALU.add))
        k.do(DVE, [Bta], [Btu], lambda: ve.scalar_tensor_tensor(
            tu[64:128, :], tu[64:128, :], -TWO_PI, ta[64:128, :], ALU.mult, ALU.add))
        k.do(DVE, [], [Btu], lambda: ve.tensor_scalar(tu[64:128, :], tu[64:128, :], -math.pi, math.pi, ALU.max, ALU.min))
        k.do(ACT, [Btu, Bconst], [BTT], lambda: sc.activation(
            out=TT[64:128, cs], in_=tu[64:128, :], func=AF.Sin, scale=rc[64:128, 2:3]))
    if stage == 5:
        return finish((TT[:],))
    k.barrier()

    k.do(POOL, [], [Bsq[0]], lambda: gp_.memset(sq2[0][:], 0.0))
    k.do(POOL, [], [Bsq[1]], lambda: gp_.memset(sq2[1][:], 0.0))
    k.do(POOL, [], [Btk], lambda: gp_.memset(tk[:], 0.0))

    wc = [A(f"wc{i}", OFF_HT - 0 + 0, [1, 1], F32) for i in range(0)]

    def prep_chunk_weights(dst_i, src_ap, ncols):
        load_w_scaled(dwb, wstB, BwstB, wb[dst_i], Bwb[dst_i], src_ap, ncols)

    for i in range(NQ):
        cs = slice(i * 512, (i + 1) * 512)
        if i == 0:
            prep_chunk_weights(0, w_in_v[:, :, 2048:2176], 128)
            prep_chunk_weights(1, w_in_v[:, :, 2176:2304], 128)
        proj512(0, wb[0], Bwb[0], i)
        proj512(1, wb[1], Bwb[1], i)
        r_ = blocknorm([0, 1], 2, ones256)
        for c in range(2):
            k.do(DVE, [Bps[c], Brstd[r_], Bconst], [Bcqn[i]], lambda c=c: ve.scalar_tensor_tensor(
                cqn[:, c, cs], ps[c][:], gp[:, 10 + c:11 + c], rstd[r_][:], ALU.mult, ALU.mult))
    if stage == 61:
        return finish((cqn[:],))
    for i in range(NQ):
        cs = slice(i * 512, (i + 1) * 512)
        if i == 0:
            prep_chunk_weights(0, w_in_v[:, :, 2304:2432], 128)
        bk = [0, 1][i % 2]
        proj512(bk, wb[0], Bwb[0], i)
        r_ = blocknorm([bk], 2, ones128)
        k.do(DVE, [Bps[bk], Brstd[r_], Bconst], [Bckvn[i]], lambda: ve.scalar_tensor_tensor(
            ckvn[:, cs], ps[bk][:], gp[:, 12:13], rstd[r_][:], ALU.mult, ALU.mult))
    if stage == 62:
        return finish((cqn[:],))
    for i in range(NQ):
        cs = slice(i * 512, (i + 1) * 512)
        if i == 0:
            prep_chunk_weights(1, w_kr_v, 128)
        bk = [3, 4][i % 2]

        def mm():
            for dmc in range(8):
                ins = tn.matmul(ps[bk][:], wb[1][:, dmc, :], hT[:, dmc, cs],
                                start=(dmc == 0), stop=(dmc == 7))
            return ins
        k.do(PE, [Bwb[1]] + BhT[4 * i:4 * i + 4], [Bps[bk]], mm)
        s_ = i % 2
        k.do(ACT, [Bps[bk]], [Bsq[s_]], lambda: sc.activation(out=sq2[s_][64:128, :], in_=ps[bk][64:128, :], func=AF.Square))
        k.do(PE, [Bsq[s_], Bconst], [Bps[5]], lambda: tn.matmul(ps[5][:], bo_q, sq2[s_][:], start=True, stop=True))
        k.do(ACT, [Bps[5], Bconst], [Bssb[0]], lambda: sc.activation(out=ssb2[:], in_=ps[5][:], func=AF.Ln, bias=epsT[:, 0:1]))
        k.do(ACT, [Bssb[0]], [Brstd[0]], lambda: sc.activation(out=rstd2[:], in_=ssb2[:], func=AF.Exp, scale=-0.5))
        k.do(DVE, [Bps[bk], BTT, Bconst], [Btk], lambda: ve.scalar_tensor_tensor(
            tk[64:128, :], ps[bk][64:128, :], gp[64:128, 14:15], TT[64:128, cs], ALU.mult, ALU.mult))
        k.do(DVE, [Brstd[0]], [Btk], lambda: ve.tensor_tensor(tk[64:128, :], tk[64:128, :], rstd2[64:128, :], ALU.mult))
        k.do(PE, [Btk, Bconst], [Bps[6]], lambda: tn.matmul(ps[6][:], msum, tk[:], start=True, stop=True))
        k.do(ACT, [Bps[6]], [BkT[0][i]], lambda: sc.activation(out=kT[0][64:128, cs], in_=ps[6][64:128, :], func=AF.Copy))
        k.do(ACT, [Bps[6]], [BkT[1][i]], lambda: sc.activation(out=kT[1][64:128, cs], in_=ps[6][64:128, :], func=AF.Copy))
        if stage == 60 and i == 0:
            dd = A("dd", OFF_MIXB, [128, 5, 512], F32)
            Bdd = Buf()
            k.do(POOL, [], [Bdd], lambda: gp_.memset(dd[:], 0.0))
            k.do(DVE, [Bps[bk]], [Bdd], lambda: ve.tensor_copy(dd[64:128, 0, :], ps[bk][64:128, :]))
            k.do(DVE, [Bssb[0]], [Bdd], lambda: ve.tensor_copy(dd[:, 1, :], ssb2[:]))
            k.do(DVE, [Brstd[0]], [Bdd], lambda: ve.tensor_copy(dd[:, 2, :], rstd2[:]))
            k.do(DVE, [Btk], [Bdd], lambda: ve.tensor_copy(dd[:, 3, :], tk[:]))
            k.do(DVE, [Bps[6]], [Bdd], lambda: ve.tensor_copy(dd[:, 4, :], ps[6][:]))
            return finish((dd[:],))
    if stage == 63:
        return finish((cqn[:],))
    gates(2464, mixB, BmixB, wb, Bwb, [ssb2, rstd2], [Bssb[0], Brstd[0]], [0, 1, 2], wstB, BwstB, dwb)
    if stage == 6:
        return finish((kT[0][:],))
    if stage == 7:
        return finish((cqn[:],))
    k.barrier()

    o = OFF_HT
    qT = [A(f"qT{i}", o + i * 8192, [128, S], BF16) for i in range(2)]
    o += 16384
    Vb = [A(f"Vb{i}", o + i * 12288, [128, 32, 192], BF16) for i in range(2)]
    o += 24576
    PTb = [A(f"PTb{i}", o + i * 1024, [128, 512], BF16) for i in range(4)]
    o += 4096
    tq = [A(f"tq{i}", o + i * 2048, [128, 512], F32) for i in range(2)]
    o += 4096
    bcs = [A(f"bcs{i}", o + i * 2048, [128, 512], F32) for i in range(2)]
    o += 4096
    RtB = A("RtB", o, [128, 512], F32)
    o += 2048
    tfb = [A(f"tfb{i}", o + i * 2048, [128, 512], F32) for i in range(2)]
    o += 4096
    assert o <= OFF_R
    wuq = A("wuq", OFF_W, [128, 2, 1024], BF16)
    wukk = A("wukk", OFF_W + 4096, [128, 576], BF16)
    wukv = A("wukv", OFF_W + 5248, [128, 512], BF16)
    BqT = [[Buf() for _ in range(NQ)] for _ in range(2)]
    BVb = [[Buf() for _ in range(8)] for _ in range(2)]
    BPTb = [Buf() for _ in range(4)]
    Btq = [Buf(), Buf()]
    Bbcs = [Buf(), Buf()]
    BRtB = Buf()
    Btfb = [Buf(), Buf()]
    Bwm = Buf()
    dwm = k.dsem("d_wm")
    k.dma(POOL, dwm, wuq[:], w_uq_v, writes=[Bwm])
    k.dma(POOL, dwm, wukk[:], w_ukvk_d, writes=[Bwm])
    k.dma(POOL, dwm, wukv[:], w_ukvv_d, writes=[Bwm])
    Bwm.w = (dwm, dwm.cnt)
    for s_ in range(2):
        k.do(POOL, [], [BVb[s_][0]], lambda: gp_.memset(Vb[s_][:, :, 64:128], 1.0))
        for g4 in range(1, 8):
            BVb[s_][g4].w = BVb[s_][0].w
    k.do(POOL, [], [BRtB], lambda: gp_.memset(RtB[:], 0.0))
    k.do(POOL, [], [Bsq[0]], lambda: gp_.memset(sq2[0][:], 0.0))
    k.do(POOL, [], [Bsq[1]], lambda: gp_.memset(sq2[1][:], 0.0))

    def phaseB_prep(hp):
        vs = hp % 2
        for g4 in range(8):
            bk = [6, 7][g4 % 2]

            def mmv():
                for tt in range(4):
                    n = g4 * 4 + tt
                    ins = tn.matmul(ps[bk][:, tt * 128:(tt + 1) * 128], ckvn[:, n * 128:(n + 1) * 128],
                                    wukv[:, hp * 128:(hp + 1) * 128], start=True, stop=True)
                return ins
            k.do(PE, [Bwm, Bckvn[g4 // 1 * 0 + (g4 * 4) // 4 // 1 if False else (g4 * 512) // 512]], [Bps[bk]], mmv)
            k.do(DVE, [Bps[bk]], [BVb[vs][g4]], lambda: ve.tensor_copy(
                Vb[vs][:, g4 * 4:(g4 + 1) * 4, :].rearrange("p n (s d) -> p n s d", s=3)[:, :, 0:3:2, :],
                ps[bk][:].rearrange("p (n s d) -> p n s d", n=4, s=2)))
        for hh in range(2):
            h = 2 * hp + hh
            def q_s1(i):
                cs = slice(i * 512, (i + 1) * 512)
                bq = [0, 1, 3, 4][i % 4]
                bss = [2, 5][i % 2]

                def mmq():
                    for c in range(2):
                        ins = tn.matmul(ps[bq][:], wuq[:, c, h * 128:(h + 1) * 128], cqn[:, c, cs],
                                        start=(c == 0), stop=(c == 1))
                    return ins
                k.do(PE, [Bwm, Bcqn[i]], [Bps[bq]], mmq)

            def q_s1b(i):
                bq = [0, 1, 3, 4][i % 4]
                bss = [2, 5][i % 2]
                s_ = i % 2
                k.do(ACT, [Bps[bq]], [Bsq[s_]], lambda: sc.activation(out=sq2[s_][:], in_=ps[bq][:], func=AF.Square))
                k.do(PE, [Bsq[s_], Bconst], [Bps[bss]], lambda: tn.matmul(ps[bss][:], bo_q, sq2[s_][:], start=True, stop=True))

            def q_s2(i):
                cs = slice(i * 512, (i + 1) * 512)
                bq = [0, 1, 3, 4][i % 4]
                bss = [2, 5][i % 2]
                rs, Brs = [(rstd2, Brstd[0]), (rstd2b, Brstd2b)][i % 2]
                k.do(ACT, [Bps[bss], Bconst], [Bssb[0]], lambda: sc.activation(out=ssb2[:], in_=ps[bss][:], func=AF.Ln, bias=epsT[:, 0:1]))
                k.do(ACT, [Bssb[0]], [Brs], lambda: sc.activation(out=rs[:], in_=ssb2[:], func=AF.Exp, scale=-0.5))
                t_ = tq[i % 2]
                k.do(DVE, [Bps[bq], BTT, Bconst], [Btq[i % 2]], lambda: ve.scalar_tensor_tensor(
                    t_[:], ps[bq][:], gp[:, 13:14], TT[:, cs], ALU.mult, ALU.mult))
                k.do(POOL, [Btq[i % 2], Brs], [BqT[hh][i]], lambda: gp_.tensor_tensor(
                    qT[hh][:, cs], t_[:], rs[:], ALU.mult))
            for i in range(NQ + 2):
                if i < NQ:
                    q_s1(i)
                if 1 <= i <= NQ:
                    q_s1b(i - 1)
                if i >= 2:
                    q_s2(i - 2)
            k.do(POOL, [], [Bsq[0]], lambda: gp_.memset(sq2[0][64:128, :], 0.0))
            k.do(POOL, [], [Bsq[1]], lambda: gp_.memset(sq2[1][64:128, :], 0.0))

            def k_s1(i):
                cs = slice(i * 512, (i + 1) * 512)
                bk_ = [3, 4, 0, 1][i % 4]
                bss = [5, 2][i % 2]
                k.do(PE, [Bwm, Bckvn[i]], [Bps[bk_]], lambda: tn.matmul(
                    ps[bk_][:], wukk[:, h * 64:h * 64 + 128], ckvn[:, cs], start=True, stop=True))

            def k_s1b(i):
                bk_ = [3, 4, 0, 1][i % 4]
                bss = [5, 2][i % 2]
                s_ = i % 2
                k.do(ACT, [Bps[bk_]], [Bsq[s_]], lambda: sc.activation(out=sq2[s_][0:64, :], in_=ps[bk_][0:64, :], func=AF.Square))
                k.do(PE, [Bsq[s_], Bconst], [Bps[bss]], lambda: tn.matmul(ps[bss][:], bo_q, sq2[s_][:], start=True, stop=True))

            def k_s2(i):
                cs = slice(i * 512, (i + 1) * 512)
                bk_ = [3, 4, 0, 1][i % 4]
                bss = [5, 2][i % 2]
                rs, Brs = [(rstd2, Brstd[0]), (rstd2b, Brstd2b)][i % 2]
                k.do(ACT, [Bps[bss], Bconst], [Bssb[0]], lambda: sc.activation(
                    out=ssb2[0:64, :], in_=ps[bss][0:64, :], func=AF.Ln, bias=epsT[0:64, 0:1]))
                k.do(ACT, [Bssb[0]], [Brs], lambda: sc.activation(out=rs[0:64, :], in_=ssb2[0:64, :], func=AF.Exp, scale=-0.5))
                k.do(DVE, [Bps[bk_], Brs, Bconst], [BkT[hh][i]], lambda: ve.scalar_tensor_tensor(
                    kT[hh][0:64, cs], ps[bk_][0:64, :], gp[0:64, 14:15], rs[0:64, :], ALU.mult, ALU.mult))
            for i in range(NQ + 2):
                if i < NQ:
                    k_s1(i)
                if 1 <= i <= NQ:
                    k_s1b(i - 1)
                if i >= 2:
                    k_s2(i - 2)

    def phaseB_attn(hp):
        vs = hp % 2
        tiles = []
        for i in range(NQ):
            for hh in range(2):
                for j in range(4 * i + 4):
                    tiles.append((i, hh, j))
        ntile = len(tiles)

        def emit_S(n):
            i, hh, j = tiles[n]
            bk = [0, 1, 2][n % 3]
            jj = j - 4 * i
            c0 = 128 * jj if jj > 0 else 0

            def mm():
                o_ = ps[bk][:, c0:512]
                ins = tn.matmul(o_, kT[hh][:, j * 128:(j + 1) * 128], qT[hh][:, i * 512 + c0:(i + 1) * 512],
                                start=True, stop=(jj < 0))
                if jj >= 0:
                    ins = tn.matmul(ps[bk][:, c0:c0 + 128], ident, causal, start=False, stop=True)
                return ins
            return [Bconst, BkT[hh][j // 4], BqT[hh][i]], [Bps[bk]], mm

        def emit_E(n):
            i, hh, j = tiles[n]
            bk = [0, 1, 2][n % 3]
            jj = j - 4 * i
            c0 = 128 * jj if jj > 0 else 0
            k.do(ACT, [Bps[bk]], [BPTb[n % 4]], lambda: sc.activation(
                out=PTb[n % 4][:, c0:512], in_=ps[bk][:, c0:512], func=AF.Exp))

        def emit_PV(n):
            i, hh, j = tiles[n]
            jj = j - 4 * i
            c0 = 128 * jj if jj > 0 else 0
            bank = [[3, 5], [4, 6]][hh][i % 2]
            lhs = Vb[vs][:, j, 0:128] if hh == 0 else Vb[vs][:, j, 64:192]
            last = (j == 4 * i + 3)
            return [BPTb[n % 4], BVb[vs][j // 4]], [Bps[bank]], (lambda: tn.matmul(
                ps[bank][:, c0:512], lhs, PTb[n % 4][:, c0:512], start=(j == 0), stop=last)), (last and hh == 1), i

        def finalize(i):
            cs = slice(i * 512, (i + 1) * 512)
            be, bo_ = [3, 5][i % 2], [4, 6][i % 2]
            k.do(ACT, [Bps[be]], [BRtB], lambda: sc.activation(out=RtB[64:65, :], in_=ps[be][64:65, :], func=AF.Ln))
            k.do(ACT, [Bps[bo_]], [BRtB], lambda: sc.activation(out=RtB[0:1, :], in_=ps[bo_][0:1, :], func=AF.Ln))
            k.do(PE, [BRtB, Bconst], [Bps[7]], lambda: tn.matmul(ps[7][:], sel, RtB[:], start=True, stop=True))
            b_ = bcs[i % 2]
            k.do(ACT, [Bps[7], Bconst], [Bbcs[i % 2]], lambda: sc.activation(
                out=b_[:], in_=ps[7][:], func=AF.Exp, scale=-1.0, bias=lnhalf[:, 0:1]))
            t_ = tfb[i % 2]
            k.do(DVE, [Bps[be], Bbcs[i % 2]], [Btfb[i % 2]], lambda: ve.tensor_tensor(
                t_[0:64, :], ps[be][0:64, :], b_[0:64, :], ALU.mult))
            k.do(DVE, [Bps[bo_], Bbcs[i % 2]], [Btfb[i % 2]], lambda: ve.tensor_tensor(
                t_[64:128, :], ps[bo_][64:128, :], b_[64:128, :], ALU.mult))
            k.do(POOL, [Btfb[i % 2]], [BmixB[hp][i]], lambda: gp_.tensor_tensor(
                mixB[:, hp, cs], t_[:], mixB[:, hp, cs], ALU.mult))

        for n in range(ntile + 2):
            if n < ntile:
                sS = emit_S(n)
                k.do(PE, sS[0], sS[1], sS[2])
                emit_E(n)
            if n >= 2:
                sP = emit_PV(n - 2)
                k.do(PE, sP[0], sP[1], sP[2])
                if sP[3]:
                    finalize(sP[4])

    npairs_b = 4 if stage != 8 else 1
    wo = A("wo", R0, [128, 8, 1024], BF16)
    Bwo = Buf()
    dwo = k.dsem("d_wo")
    w_out_pref = [False]
    for hp in range(npairs_b):
        phaseB_prep(hp)
        if hp == 3:
            for c in range(8):
                k.dma(POOL, dwo, wo[:, c, :], w_out_v[:, c, :], writes=([BTT] if c == 0 else []))
            Bwo.w = (dwo, dwo.cnt)
            w_out_pref[0] = True
        phaseB_attn(hp)
    if stage in (8, 9):
        return finish((mixB[:],))
    k.barrier()

    o = OFF_HT
    o += 16384
    xo = [A(f"xo{i}", o + i * 4096, [128, D], F32) for i in range(3)]
    o += 12288
    yo = [A(f"yo{i}", o + i * 4096, [128, D], F32) for i in range(3)]
    o += 12288
    Bxo = [Buf() for _ in range(3)]
    Byo = [Buf() for _ in range(3)]
    dxo = [k.dsem(f"d_xo{i}") for i in range(3)]
    dyo = [k.dsem(f"d_yo{i}") for i in range(3)]
    if not w_out_pref[0]:
        for c in range(8):
            k.dma(POOL, dwo, wo[:, c, :], w_out_v[:, c, :], writes=[Bwo])
        Bwo.w = (dwo, dwo.cnt)
    y_v = y_d.rearrange("(n p) d -> n p d", p=128)

    def p5_load(j):
        k.dma(SP, dxo[j % 3], xo[j % 3][:], x_v[j], writes=[Bxo[j % 3]])

    p5_load(0)
    p5_load(1)
    for j in range(NT):
        if j + 2 < NT:
            p5_load(j + 2)
        ts = slice(j * 128, (j + 1) * 128)
        for half in range(2):
            bk = [0, 1, 2, 3][(2 * j + half) % 4]

            def mm():
                for c in range(8):
                    src = mixA if c < 4 else mixB
                    ins = tn.matmul(ps[bk][:], src[:, c % 4, ts], wo[:, c, half * 512:(half + 1) * 512],
                                    start=(c == 0), stop=(c == 7))
                return ins
            k.do(PE, [Bwo] + [BmixA[c][j // 4] for c in range(4)] + [BmixB[c][j // 4] for c in range(4)], [Bps[bk]], mm)
            k.do(DVE, [Bps[bk], Bxo[j % 3]], [Byo[j % 3]], lambda: ve.tensor_tensor(
                yo[j % 3][:, half * 512:(half + 1) * 512], ps[bk][:], xo[j % 3][:, half * 512:(half + 1) * 512], ALU.add))
        k.dma(SP, dyo[j % 3], y_v[j], yo[j % 3][:], reads=[Byo[j % 3]])
    k.barrier()
    k.es.close()
    return nc


def _t5_bucket(dist):
    d = np.maximum(dist.astype(np.float32), np.float32(1.0))
    large = 16 + (np.log(d / np.float32(16.0)) / np.float32(math.log(2048 / 16)) * np.float32(16.0)).astype(np.int32)
    large = np.minimum(large, 31)
    return np.where(dist < 16, dist, large)


def _consts():
    cbf = np.zeros((128, 6, 128), np.float32)
    p = np.arange(128)
    cbf[:, 0, :] = np.eye(128, dtype=np.float32)
    cbf[:, 1, :] = ((p[:, None] // 64) == (p[None, :] // 64)) / 64.0
    blk = np.where(p < 64, 0, np.where(p < 96, 1, 2))
    wgt = np.where(p < 64, 1.0 / 64, 1.0 / 32)
    cbf[:, 2, :] = (blk[:, None] == blk[None, :]) * wgt[None, :]
    cbf[:, 3, :] = 1.0 / 256
    cbf[:, 4, :] = 1.0 / 128
    cbf[:, 5, :] = np.where(p[None, :] >= p[:, None], 0.0, MASKV)
    cf = np.zeros((128, 2, 128), np.float32)
    cf[64, 0, 0:64] = 1.0
    cf[0, 0, 64:128] = 1.0
    for j in range(32):
        for a in (64 + j, 96 + j):
            for b in (64 + j, 96 + j):
                cf[a, 1, b] = 1.0
    cf2 = np.zeros((128, 2, 128), np.float32)
    cf2[64, 0, 0:64] = 1.0
    cf2[0, 1, 64:128] = 1.0
    rc = np.zeros((128, 4), np.float32)
    inv_freq = (np.float32(10000.0) ** (-np.arange(0, 32, 2, dtype=np.float32) / np.float32(32))).astype(np.float32)
    for rrow in range(64):
        prt = 64 + rrow
        rc[prt, 0] = inv_freq[rrow % 16]
        rc[prt, 1] = (math.pi / 2) if rrow < 32 else 0.0
        rc[prt, 2] = -1.0 if 32 <= rrow < 48 else 1.0
    return cbf, cf, cf2, rc


def _bias_tables(rel_bias):
    ki = np.arange(128)[:, None]
    c = np.arange(256)[None, :]
    j = c - ki
    valid = (j >= 0) & (j <= 128)
    bt = np.empty((4, 128, 6, 256), np.float32)
    for pi, r in enumerate(PATTERNS):
        bucket = _t5_bucket(np.maximum(j, 0) * r)
        for h in range(8):
            tab = np.where(valid, rel_bias[bucket, h], np.float32(MASKV)).astype(np.float32)
            bt[h // 2, :, pi * 2 + (h % 2), :] = tab
    return bt


_NC_CACHE = {}


def _prep_shared(inputs):
    f = lambda n: np.asarray(inputs[n], np.float32)
    w_in = np.ascontiguousarray(f("w_in")[0])
    kr = w_in[:, 2432:2464]
    w_kr = np.ascontiguousarray(np.concatenate([np.zeros((D, 64), np.float32), kr, kr[:, 16:32], kr[:, 0:16]], axis=1))
    w_uq = f("w_uq")[0]
    w_uq_ext = np.empty((256, 1024), np.float32)
    for h in range(8):
        blk = w_uq[:, h * 96:(h + 1) * 96]
        w_uq_ext[:, h * 128:h * 128 + 96] = blk
        w_uq_ext[:, h * 128 + 96:h * 128 + 112] = blk[:, 80:96]
        w_uq_ext[:, h * 128 + 112:h * 128 + 128] = blk[:, 64:80]
    w_ukv = f("w_ukv")[0].reshape(128, 8, 128)
    w_ukvk = np.ascontiguousarray(np.concatenate([w_ukv[:, :, 0:64].reshape(128, 512), np.zeros((128, 64), np.float32)], axis=1))
    w_ukvv = np.ascontiguousarray(w_ukv[:, :, 64:128].reshape(128, 512))
    w_out = np.ascontiguousarray(f("w_out")[0])
    gp = np.zeros((128, 16), np.float32)
    gp[:, 0:8] = f("norm_gain")[0].reshape(8, 128).T
    gp[:, 8] = np.tile(f("a_q_gain")[0], 2)
    gp[:, 9] = np.tile(f("a_k_gain")[0], 2)
    gp[:, 10:12] = f("q_c_gain")[0].reshape(2, 128).T
    gp[:, 12] = f("kv_c_gain")[0]
    qr = f("qr_gain")[0]
    kr_g = f("kr_gain")[0]
    gp[:, 13] = np.concatenate([f("qn_gain")[0], qr, qr[16:32], qr[0:16]])
    gp[:, 14] = np.concatenate([f("kn_gain")[0], kr_g, kr_g[16:32], kr_g[0:16]])
    cbf, cf, cf2, rc = _consts()
    bt = _bias_tables(f("rel_bias"))
    return dict(w_in=w_in, w_kr=w_kr, w_uq=w_uq_ext, w_ukvk=w_ukvk, w_ukvv=w_ukvv, w_out=w_out,
                cbf=cbf, cf=cf, cf2=cf2, gp=gp, rc=rc, bt=bt)


def kernel(**inputs):
    x = np.asarray(inputs["x"], np.float32)
    pos = np.asarray(inputs["positions"], np.int32)
    shared = _prep_shared(inputs)
    if "nc" not in _NC_CACHE:
        _NC_CACHE["nc"] = build()
    nc = _NC_CACHE["nc"]
    in_maps = []
    for b in range(8):
        m = dict(shared)
        m["x"] = np.ascontiguousarray(x[b])
        m["pos"] = np.ascontiguousarray(pos[b].reshape(1, S))
        in_maps.append(m)
    res = run_bass_kernel_spmd(nc, in_maps, core_ids=list(range(8)))
    return np.stack([np.asarray(r["y"], np.float32) for r in res.results], axis=0)
```

```python
import contextlib
import math
import numpy as np
import concourse.bass as bass
import concourse.mybir as mybir
from concourse.bass_utils import run_bass_kernel_spmd

F32 = mybir.dt.float32
BF16 = mybir.dt.bfloat16
I32 = mybir.dt.int32
AF = mybir.ActivationFunctionType
ALU = mybir.AluOpType

S = 4096
D = 1024
NT = 32
NQ = 8
NCOLS = 2976
MASKV = -30000.0
EPS = 1e-6
PATTERNS = (1, 4, 16)
QSCALE_A = 64 ** -0.5
QSCALE_B = 96 ** -0.5
SB_BASE = 16512
SB_LIMIT = 229344


def _ssl(t0, n, r):
    return slice(t0, t0 + (n - 1) * r + 1, r)


class Sem:
    def __init__(self, k, name):
        self.name = name
        self.sem = k.es.enter_context(k.nc.semaphore(name))
        self.cnt = 0


class Eng(Sem):
    def __init__(self, k, name, eng):
        super().__init__(k, "s_" + name)
        self.eng = eng
        self.seen = {}

    def wait(self, *toks):
        for t in toks:
            if t is None:
                continue
            if isinstance(t, list):
                self.wait(*t)
                continue
            p, c = t
            if self.seen.get(p.name, 0) >= c:
                continue
            self.eng.wait_ge(p.sem, c)
            self.seen[p.name] = c

    def done(self, ins):
        ins.then_inc(self.sem, 1)
        self.cnt += 1
        return (self, self.cnt)


class Buf:
    def __init__(self, name="", excl=False):
        self.name = name
        self.excl = excl
        self.w = None
        self.r = {}

    def add_r(self, tok):
        p, c = tok
        if self.r.get(p.name, (None, 0))[1] < c:
            self.r[p.name] = tok

    def set_w(self, tok):
        self.w = tok
        self.r = {}


class K:
    def __init__(self):
        self.nc = bass.Bass("TRN2", target_bir_lowering=False)
        self.es = contextlib.ExitStack()
        nc = self.nc
        self.PE = Eng(self, "pe", nc.tensor)
        self.ACT = Eng(self, "act", nc.scalar)
        self.DVE = Eng(self, "dve", nc.vector)
        self.POOL = Eng(self, "pool", nc.gpsimd)
        self.SP = Eng(self, "sp", nc.sync)
        self.engines = [self.PE, self.ACT, self.DVE, self.POOL, self.SP]
        self.dsems = []

    def dsem(self, name):
        d = Sem(self, name)
        self.dsems.append(d)
        return d

    def wait_for(self, E, reads, writes):
        for b in reads:
            E.wait(b.w)
            if b.excl:
                E.wait(*[t for n, t in b.r.items() if n != E.name])
        for b in writes:
            E.wait(b.w)
            E.wait(*b.r.values())

    def commit(self, E, ins, reads, writes):
        tok = E.done(ins)
        for b in reads:
            b.add_r(tok)
        for b in writes:
            b.set_w(tok)
        return tok

    def do(self, E, reads, writes, emit):
        self.wait_for(E, reads, writes)
        ins = emit()
        return self.commit(E, ins, reads, writes)

    def dma(self, Q, ds, out, in_, reads=(), writes=()):
        for b in reads:
            Q.wait(b.w)
        for b in writes:
            Q.wait(b.w)
            Q.wait(*b.r.values())
        Q.eng.dma_start(out=out, in_=in_).then_inc(ds.sem, 16)
        ds.cnt += 16
        tok = (ds, ds.cnt)
        for b in reads:
            b.add_r(tok)
        for b in writes:
            b.set_w(tok)
        return tok

    def barrier(self):
        for E in self.engines:
            for F in self.engines:
                if F is not E and F.cnt > 0:
                    E.wait((F, F.cnt))
            for d in self.dsems:
                if d.cnt > 0:
                    E.wait((d, d.cnt))


def build(stage=99, dbg=None):
    k = K()
    nc = k.nc
    PE, ACT, DVE, POOL, SP = k.PE, k.ACT, k.DVE, k.POOL, k.SP
    tn, sc, ve, gp_, sy = nc.tensor, nc.scalar, nc.vector, nc.gpsimd, nc.sync

    def dram(name, shape, dt, kind="ExternalInput"):
        return nc.dram_tensor(name, shape, dt, kind=kind)

    x_d = dram("x", [S, D], F32).ap()
    pos_t = dram("pos", [1, S], I32)
    w_in_d = dram("w_in", [D, NCOLS], F32).ap()
    w_kr_d = dram("w_kr", [D, 128], F32).ap()
    w_uq_d = dram("w_uq", [256, 1024], F32).ap()
    w_ukvk_d = dram("w_ukvk", [128, 576], F32).ap()
    w_ukvv_d = dram("w_ukvv", [128, 512], F32).ap()
    w_out_d = dram("w_out", [D, D], F32).ap()
    cbf_d = dram("cbf", [128, 6, 128], F32).ap()
    cf_d = dram("cf", [128, 2, 128], F32).ap()
    cf2_d = dram("cf2", [128, 2, 128], F32).ap()
    gp_d = dram("gp", [128, 16], F32).ap()
    rc_d = dram("rc", [128, 4], F32).ap()
    bt_d = dram("bt", [4, 128, 6, 256], F32).ap()
    y_d = dram("y", [S, D], F32, kind="ExternalOutput").ap()
    dbg_d = None
    if dbg is not None:
        dbg_d = dram("dbg", list(dbg[1]), dbg[2], kind="ExternalOutput").ap()

    w_in_v = w_in_d.rearrange("(c p) n -> p c n", p=128)
    w_kr_v = w_kr_d.rearrange("(c p) n -> p c n", p=128)
    w_uq_v = w_uq_d.rearrange("(c p) n -> p c n", p=128)
    w_out_v = w_out_d.rearrange("(c p) n -> p c n", p=128)

    def A(name, off, shape, dt):
        sz = int(np.prod(shape[1:])) * (4 if dt in (F32, I32) else 2)
        assert off % 32 == 0 and SB_BASE + off + sz <= SB_LIMIT, (name, off, sz)
        return nc.alloc_sbuf_tensor_at(name, list(shape), dt, offset=SB_BASE + off)

    cbf = A("cbf", 0, [128, 6, 128], BF16)
    cf = A("cf", 1536, [128, 2, 128], F32)
    gp = A("gp", 2560, [128, 16], F32)
    rc = A("rc", 2624, [128, 4], F32)
    st_ss = A("st_ss", 2656, [128, 4], F32)
    st_t = A("st_t", 2688, [128, 4], F32)
    st_rs = A("st_rs", 2720, [128, 4], F32)
    cm05 = A("cm05", 2752, [128, 1], F32)
    epsT = A("epsT", 2784, [128, 1], F32)
    lnhalf = A("lnhalf", 2816, [128, 1], F32)
    cf2 = A("cf2", 2880, [128, 2, 128], F32)
    OFF_MIXA, OFF_MIXB, OFF_HT, OFF_R = 4096, 36864, 69632, 135168
    mixA = A("mixA", OFF_MIXA, [128, 4, S], BF16)
    mixB = A("mixB", OFF_MIXB, [128, 4, S], BF16)
    acc_e = A("acc_e", OFF_MIXB, [128, S], F32)
    acc_o = A("acc_o", OFF_MIXB + 16384, [128, S], F32)
    hT = A("hT", OFF_HT, [128, 8, S], BF16)
    BhT = [Buf() for _ in range(NT)]
    BmixA = [[Buf() for _ in range(NQ)] for _ in range(4)]
    BmixB = [[Buf() for _ in range(NQ)] for _ in range(4)]

    ident = cbf[:, 0, :]
    bo64 = cbf[:, 1, :]
    bo_q = cbf[:, 2, :]
    ones256 = cbf[:, 3, :]
    ones128 = cbf[:, 4, :]
    causal = cbf[:, 5, :]
    sel = cf[:, 0, :]
    msum = cf[:, 1, :]

    ps = [k.es.enter_context(nc.psum_tensor(f"ps{i}", [128, 512], F32)) for i in range(8)]
    Bps = [Buf(excl=True) for _ in range(8)]

    dconst = k.dsem("d_const")
    Bconst = Buf()

    dconst2 = k.dsem("d_const2")
    Bcbf = Buf()
    k.dma(POOL, dconst2, cbf[:], cbf_d, writes=[Bcbf])
    k.dma(SP, dconst, cf[:], cf_d, writes=[Bconst])
    k.dma(SP, dconst, cf2[:], cf2_d, writes=[Bconst])
    k.dma(SP, dconst, gp[:], gp_d, writes=[Bconst])
    k.dma(SP, dconst, rc[:], rc_d, writes=[Bconst])
    Bconst.w = (dconst, dconst.cnt)
    k.do(DVE, [], [Bconst], lambda: ve.tensor_scalar(gp[:, 8:9], gp[:, 8:9], QSCALE_A, None, ALU.mult))
    k.do(DVE, [], [Bconst], lambda: ve.tensor_scalar(gp[:, 13:14], gp[:, 13:14], QSCALE_B, None, ALU.mult))
    k.do(POOL, [], [Bconst], lambda: gp_.memset(cm05[:], -0.5))
    k.do(POOL, [], [Bconst], lambda: gp_.memset(epsT[:], EPS))
    k.do(POOL, [], [Bconst], lambda: gp_.memset(lnhalf[:], math.log(0.5)))
    k.barrier()

    dout = k.dsem("d_out")

    def finish(dump=None):
        if dump is not None:
            k.barrier()
            src, = dump
            k.dma(SP, dout, dbg_d, src)
        k.barrier()
        SP.wait((dout, dout.cnt)) if dout.cnt else None
        k.es.close()
        return nc

    def load_w_scaled(dsm, stage, Bstage, dst, Bdst, src_ap, ncols):
        k.dma(SP, dsm, stage[:, :, 0:ncols], src_ap, writes=[Bstage])
        k.do(POOL, [Bstage, Bconst], [Bdst], lambda: gp_.tensor_tensor(
            dst[:, :, 0:ncols], stage[:, :, 0:ncols],
            gp[:, 0:8].unsqueeze(2).to_broadcast([128, 8, ncols]), ALU.mult))

    R0 = OFF_R
    xbuf = [A(f"xbuf{i}", R0 + i * 4096, [128, D], F32) for i in range(3)]
    xn = [A(f"xn{i}", R0 + 12288 + i * 2048, [128, D], BF16) for i in range(2)]
    junk = A("junk", R0 + 16384, [128, D], BF16)
    Bx = [Buf() for _ in range(3)]
    Bxn = [Buf() for _ in range(2)]
    Bjunk = Buf()
    Bst = [Buf() for _ in range(2)]
    dx = [k.dsem(f"d_x{i}") for i in range(3)]
    x_v = x_d.rearrange("(n p) d -> n p d", p=128)

    def ph1_load(j):
        k.dma(SP, dx[j % 3], xbuf[j % 3][:], x_v[j], writes=[Bx[j % 3]])

    def ph1_a(j):
        s = j % 2
        k.do(ACT, [Bx[j % 3]], [Bjunk, Bst[s]], lambda: sc.activation(
            out=junk[:], in_=xbuf[j % 3][:], func=AF.Square, accum_out=st_ss[:, s:s + 1]))
        k.do(DVE, [], [Bst[s]], lambda: ve.tensor_scalar(
            st_t[:, s:s + 1], st_ss[:, s:s + 1], 1.0 / D, EPS, ALU.mult, ALU.add))
        k.do(POOL, [Bconst], [Bst[s]], lambda: gp_.tensor_tensor(
            st_rs[:, s:s + 1], st_t[:, s:s + 1], cm05[:], ALU.pow))
        k.do(DVE, [Bx[j % 3], Bst[s]], [Bxn[s]], lambda: ve.tensor_scalar(
            xn[s][:], xbuf[j % 3][:], st_rs[:, s:s + 1], None, ALU.mult))

    def ph1_b(j):
        s = j % 2
        pt = ps[s][:].bitcast(BF16)

        def tr():
            for c in range(8):
                ins = tn.transpose(pt[:, c * 128:(c + 1) * 128], xn[s][:, c * 128:(c + 1) * 128], ident)
            return ins
        k.do(PE, [Bxn[s], Bconst], [Bps[s]], tr)
        if j % 2 == 0:
            k.do(ACT, [Bps[s]], [BhT[j]], lambda: sc.activation(
                out=hT[:, :, j * 128:(j + 1) * 128], in_=pt.rearrange("p (c t) -> p c t", c=8), func=AF.Copy))
        else:
            k.do(DVE, [Bps[s]], [BhT[j]], lambda: ve.tensor_copy(
                hT[:, :, j * 128:(j + 1) * 128], pt.rearrange("p (c t) -> p c t", c=8)))

    ph1_load(0)
    ph1_load(1)
    for j in range(NT):
        if j + 2 < NT:
            ph1_load(j + 2)
        ph1_a(j)
        if j >= 1:
            ph1_b(j - 1)
    ph1_b(NT - 1)
    if stage == 1:
        return finish((hT[:],))
    k.barrier()

    wst = A("wst", R0, [128, 8, 128], F32)
    Bwst = Buf()
    dw = k.dsem("d_w")

    def gates(col0, dst, Bdst, wg, Bwg, th, Bth, banks, wst, Bwst, dw):
        for c in range(4):
            w = wg[c % 2]
            load_w_scaled(dw, wst, Bwst, w, Bwg[c % 2], w_in_v[:, :, col0 + c * 128: col0 + (c + 1) * 128], 128)
            for i in range(NQ):
                n = c * NQ + i
                bk = banks[n % len(banks)]

                def mm():
                    for dmc in range(8):
                        ins = tn.matmul(ps[bk][:], w[:, dmc, :], hT[:, dmc, i * 512:(i + 1) * 512],
                                        start=(dmc == 0), stop=(dmc == 7))
                    return ins
                k.do(PE, [Bwg[c % 2]] + BhT[4 * i:4 * i + 4], [Bps[bk]], mm)
                t_ = th[n % 2]
                k.do(ACT, [Bps[bk]], [Bth[n % 2]], lambda: sc.activation(
                    out=t_[:], in_=ps[bk][:], func=AF.Tanh, scale=0.5))
                k.do(DVE, [Bps[bk], Bth[n % 2]], [Bdst[c][i]], lambda: ve.scalar_tensor_tensor(
                    dst[:, c, i * 512:(i + 1) * 512], t_[:], 1.0, ps[bk][:], ALU.add, ALU.mult))

    wg = [A(f"wg{i}", R0 + 4096 + i * 2048, [128, 8, 128], BF16) for i in range(2)]
    Bwg = [Buf(), Buf()]
    th = [A(f"th{i}", R0 + 8192 + i * 2048, [128, 512], F32) for i in range(2)]
    Bth = [Buf(), Buf()]
    gates(1536, mixA, BmixA, wg, Bwg, th, Bth, [0, 1, 2], wst, Bwst, dw)
    if stage == 2:
        return finish((mixA[:],))
    k.barrier()

    o = R0
    Qzz = A("Qzz", o, [128, 2, S], BF16)
    Qz = [Qzz[:, 0, :], Qzz[:, 1, :]]
    o += 16384
    KT = A("KT", o, [128, S], BF16)
    o += 8192
    Vp = A("Vp", o, [128, 32, 192], BF16)
    o += 12288
    btab = A("btab", o, [128, 6, 256], BF16)
    o += 3072
    wq = [A("wq0", o, [128, 8, 128], BF16)] * 2
    o += 2048
    wk = [A("wk0", o, [128, 8, 128], BF16)] * 2
    o += 2048
    wv = [A("wv0", o, [128, 8, 128], BF16)] * 2
    o += 2048
    VT = A("VT", o, [128, S], BF16)
    o += 8192
    stg = [A(f"stg{i}", o + i * 2048, [128, 512], F32) for i in range(2)]
    o += 4096
    wstA = A("wstA", o, [128, 8, 128], F32)
    o += 4096
    PT = [A(f"PT{i}", o + i * 1024, [128, 512], BF16) for i in range(4)]
    o += 4096
    sq = [A(f"sq{i}", o + i * 1024, [128, 512], BF16) for i in range(2)]
    o += 2048
    ssb = [A("ssb0", o, [128, 512], F32)] * 2
    o += 2048
    rstd = [A("rstd0", o, [128, 512], F32)] * 2
    o += 2048
    Rt = A("Rt", o, [128, 512], F32)
    o += 2048
    tfin = [A("tfin0", o, [128, 512], F32), stg[0]]
    o += 2048
    assert SB_BASE + o <= SB_LIMIT, o
    BQz = [[Buf() for _ in range(NQ)] for _ in range(2)]
    BKT = [Buf() for _ in range(NQ)]
    BVp = [Buf() for _ in range(4)]
    Bbt = Buf()
    Bwq, Bwk, Bwv = [Buf()] * 2, [Buf()] * 2, [Buf()] * 2
    BVT = [Buf() for _ in range(NQ)]
    Bstg = [Buf(), Buf()]
    BwstA = Buf()
    BPT = [Buf() for _ in range(4)]
    Bsq, Bssb, Brstd = [Buf(), Buf()], [Buf()] * 2, [Buf()] * 2
    BRt = Buf()
    Btfin = [Buf(), Buf()]
    Bacc = [Buf(), Buf()]
    dwa = k.dsem("d_wa")
    dbt = k.dsem("d_bt")

    BQzAll = Buf()
    k.do(POOL, [], [BQzAll], lambda: gp_.memset(Qz[0][64:128, :], 0.0))
    k.do(POOL, [], [BQzAll], lambda: gp_.memset(Qz[1][0:64, :], 0.0))
    BVpE = Buf()
    k.do(POOL, [], [BVpE], lambda: gp_.memset(Vp[:, :, 64:128], 1.0))
    k.do(POOL, [], [BRt], lambda: gp_.memset(Rt[:], 0.0))
    k.do(POOL, [], [Bacc[0]], lambda: gp_.memset(acc_e[64:128, :], 0.0))
    for b_ in BQz[0] + BQz[1]:
        b_.w = BQzAll.w
    for b_ in BVp:
        b_.w = BVpE.w

    nrm_ctr = [0]

    def blocknorm(bk_proj, bk_ss, bo_list, extra_reads=()):
        n = nrm_ctr[0]
        nrm_ctr[0] += 1
        s = n % 2
        for ci, bk in enumerate(bk_proj):
            k.do(ACT, [Bps[bk]], [Bsq[s]], lambda bk=bk: sc.activation(out=sq[s][:], in_=ps[bk][:], func=AF.Square))
            k.do(PE, [Bsq[s], Bconst], [Bps[bk_ss]], lambda ci=ci: tn.matmul(
                ps[bk_ss][:], bo_list, sq[s][:], start=(ci == 0), stop=(ci == len(bk_proj) - 1)))
        k.do(ACT, [Bps[bk_ss], Bconst], [Bssb[s]], lambda: sc.activation(
            out=ssb[s][:], in_=ps[bk_ss][:], func=AF.Ln, bias=epsT[:, 0:1]))
        k.do(ACT, [Bssb[s]], [Brstd[s]], lambda: sc.activation(out=rstd[s][:], in_=ssb[s][:], func=AF.Exp, scale=-0.5))
        return s

    def load_pair_weights(hp):
        s = hp % 2
        load_w_scaled(dwa, wstA, BwstA, wq[s], Bwq[s], w_in_v[:, :, hp * 128:(hp + 1) * 128], 128)
        load_w_scaled(dwa, wstA, BwstA, wk[s], Bwk[s], w_in_v[:, :, 512 + hp * 128:512 + (hp + 1) * 128], 128)
        load_w_scaled(dwa, wstA, BwstA, wv[s], Bwv[s], w_in_v[:, :, 1024 + hp * 128:1024 + (hp + 1) * 128], 128)

    def proj512(bk, w, Bw, i):
        def mm():
            for dmc in range(8):
                ins = tn.matmul(ps[bk][:], w[:, dmc, :], hT[:, dmc, i * 512:(i + 1) * 512],
                                start=(dmc == 0), stop=(dmc == 7))
            return ins
        k.do(PE, [Bw] + BhT[4 * i:4 * i + 4], [Bps[bk]], mm)

    def phaseA_pair(hp):
        s = hp % 2
        k.dma(POOL, dbt, btab[:], bt_d[hp], writes=[Bbt])
        for i in range(NQ):
            bq = [0, 1][i % 2]
            proj512(bq, wq[s], Bwq[s], i)
            r_ = blocknorm([bq], 2, bo64)
            k.do(DVE, [Bps[bq], Brstd[r_], Bconst], [BQz[0][i]], lambda: ve.scalar_tensor_tensor(
                Qz[0][0:64, i * 512:(i + 1) * 512], ps[bq][0:64, :], gp[0:64, 8:9], rstd[r_][0:64, :], ALU.mult, ALU.mult))
            k.do(DVE, [Bps[bq], Brstd[r_], Bconst], [BQz[1][i]], lambda: ve.scalar_tensor_tensor(
                Qz[1][64:128, i * 512:(i + 1) * 512], ps[bq][64:128, :], gp[64:128, 8:9], rstd[r_][64:128, :], ALU.mult, ALU.mult))
            bk_ = [3, 4][i % 2]
            proj512(bk_, wk[s], Bwk[s], i)
            r_ = blocknorm([bk_], 5, bo64)
            k.do(DVE, [Bps[bk_], Brstd[r_], Bconst], [BKT[i]], lambda: ve.scalar_tensor_tensor(
                KT[:, i * 512:(i + 1) * 512], ps[bk_][:], gp[:, 9:10], rstd[r_][:], ALU.mult, ALU.mult))
        for i in range(NQ):
            bv = [6, 7][i % 2]
            proj512(bv, wv[s], Bwv[s], i)
            k.do(DVE, [Bps[bv]], [BVT[i]], lambda: ve.tensor_copy(VT[:, i * 512:(i + 1) * 512], ps[bv][:]))
        if hp + 1 < npairs_a:
            load_pair_weights(hp + 1)
        for pi, r in enumerate(PATTERNS):
            nb = NT // r
            for g8 in range(4):
                bk = [6, 7][g8 % 2]
                ptv_ = ps[bk][:].bitcast(BF16)

                def mmv():
                    for tt in range(8):
                        n = g8 * 8 + tt
                        b, m = n // nb, n % nb
                        t0 = 128 * m * r + b
                        ins = tn.transpose(ptv_[:, tt * 128:(tt + 1) * 128], VT[:, _ssl(t0, 128, r)], ident)
                    return ins
                k.do(PE, [Bconst] + BVT, [Bps[bk]], mmv)
                k.do(ACT, [Bps[bk]], [BVp[g8]], lambda: sc.activation(
                    out=Vp[:, g8 * 8:(g8 + 1) * 8, :].rearrange("p n (s d) -> p n s d", s=3)[:, :, 0:3:2, :],
                    in_=ptv_.rearrange("p (n s d) -> p n s d", n=8, s=2), func=AF.Copy))
            tiles = [(b, m) for b in range(r) for m in range(nb)]
            oset_ctr = [0]

            def emit_S(n):
                b, m = tiles[n]
                bk = [0, 1, 2][n % 3]
                nq = 256 if m < nb - 1 else 128
                t0 = 128 * m * r + b

                def mm():
                    if nq == 256:
                        o_ = ps[bk][:].rearrange("p (h q) -> p h q", h=2)[:, :, 0:nq]
                        tn.matmul(o_, KT[:, _ssl(t0, 128, r)], Qzz[:, :, _ssl(t0, nq, r)], start=True, stop=False)
                        return tn.matmul(o_, ident, btab[:, pi * 2:pi * 2 + 2, 0:nq], start=False, stop=True)
                    for h in range(2):
                        o_ = ps[bk][:, h * 256:h * 256 + nq]
                        tn.matmul(o_, KT[:, _ssl(t0, 128, r)], Qz[h][:, _ssl(t0, nq, r)], start=True, stop=False)
                        ins = tn.matmul(o_, ident, btab[:, pi * 2 + h, 0:nq], start=False, stop=True)
                    return ins
                qtiles = sorted(set([(t0) // 512, min((t0 + nq * r - 1) // 512, NQ - 1)]))
                ktile = sorted(set([t0 // 512, min((t0 + 128 * r - 1) // 512, NQ - 1)]))
                rd = [Bbt, Bconst] + [BKT[x] for x in range(ktile[0], ktile[-1] + 1)] \
                    + [BQz[h][x] for h in range(2) for x in range(qtiles[0], qtiles[-1] + 1)]
                return rd, [Bps[bk]], mm

            def emit_E(n):
                b, m = tiles[n]
                bk = [0, 1, 2][n % 3]
                nq = 256 if m < nb - 1 else 128
                pv = ps[bk][:].rearrange("p (h q) -> p h q", h=2)[:, :, 0:nq]
                sg = stg[n % 2][:].rearrange("p (h q) -> p h q", h=2)[:, :, 0:nq]
                ptv = PT[n % 4][:].rearrange("p (h q) -> p h q", h=2)[:, :, 0:nq]
                k.do(ACT, [Bps[bk]], [BPT[n % 4]], lambda: sc.activation(out=ptv, in_=pv, func=AF.Exp))

            def emit_PV(n):
                b, m = tiles[n]
                cs = (m % 4) * 128
                st_ = oset_ctr[0] % 2
                be, bo_ = (3, 4) if st_ == 0 else (5, 6)
                vt = b * nb + m

                ne, no_ = (5, 6) if st_ == 0 else (3, 4)
                wr = [Bps[be], Bps[bo_]]
                if m % 4 == 3 and m < nb - 1:
                    wr += [Bps[ne], Bps[no_]]

                def mm():
                    for h in range(2):
                        bank, nbank = (be, ne) if h == 0 else (bo_, no_)
                        lhs = Vp[:, vt, 0:128] if h == 0 else Vp[:, vt, 64:192]
                        p0 = h * 256
                        if m == nb - 1:
                            ins = tn.matmul(ps[bank][:, cs:cs + 128], lhs, PT[n % 4][:, p0:p0 + 128],
                                            start=False, stop=True, skip_group_check=True)
                        elif m % 4 == 3:
                            tn.matmul(ps[bank][:, cs:cs + 128], lhs, PT[n % 4][:, p0:p0 + 128],
                                      start=False, stop=True, skip_group_check=True)
                            ins = tn.matmul(ps[nbank][:, 0:128], lhs, PT[n % 4][:, p0 + 128:p0 + 256],
                                            start=True, stop=True, skip_group_check=True)
                        else:
                            ins = tn.matmul(ps[bank][:, cs:cs + 256], lhs, PT[n % 4][:, p0:p0 + 256],
                                            start=(m == 0), stop=True, skip_group_check=True)
                    return ins
                rd = [BPT[n % 4], BVp[vt // 8]]
                return rd, wr, mm, (lambda: post_PV(n, b, m, cs, be, bo_))

            def post_PV(n, b, m, cs, be, bo_):
                if m % 4 == 3 or m == nb - 1:
                    ncol = cs + 128
                    m0 = m - (m % 4)
                    a0 = 128 * m0 * r + b
                    sl = _ssl(a0, ncol, r)
                    if pi == 0:
                        k.do(DVE, [Bps[be]], [Bacc[0]], lambda: ve.tensor_copy(acc_e[0:65, sl], ps[be][0:65, 0:ncol]))
                        k.do(DVE, [Bps[bo_]], [Bacc[1]], lambda: ve.tensor_copy(acc_o[:, sl], ps[bo_][:, 0:ncol]))
                    else:
                        k.do(DVE, [Bps[be]], [Bacc[0]], lambda: ve.tensor_tensor(
                            acc_e[0:65, sl], ps[be][0:65, 0:ncol], acc_e[0:65, sl], ALU.add))
                        k.do(DVE, [Bps[bo_]], [Bacc[1]], lambda: ve.tensor_tensor(
                            acc_o[:, sl], ps[bo_][:, 0:ncol], acc_o[:, sl], ALU.add))
                    oset_ctr[0] += 1

            ntile = len(tiles)
            for n in range(ntile + 2):
                if n < ntile:
                    sS = emit_S(n)
                    k.do(PE, sS[0], sS[1], sS[2])
                    emit_E(n)
                if n >= 2:
                    sP = emit_PV(n - 2)
                    k.do(PE, sP[0], sP[1], sP[2])
                    sP[3]()
        k.do(ACT, [], [Bacc[0]], lambda: sc.activation(out=acc_e[64:65, :], in_=acc_e[64:65, :], func=AF.Ln))
        k.do(ACT, [], [Bacc[1]], lambda: sc.activation(out=acc_o[0:1, :], in_=acc_o[0:1, :], func=AF.Ln))
        for i in range(NQ):
            cs = slice(i * 512, (i + 1) * 512)
            bkf = [7, 2][i % 2]

            def mmf():
                tn.matmul(ps[bkf][:], cf2[:, 0, :], acc_e[:, cs], start=True, stop=False)
                return tn.matmul(ps[bkf][:], cf2[:, 1, :], acc_o[:, cs], start=False, stop=True)
            k.do(PE, [Bacc[0], Bacc[1], Bconst], [Bps[bkf]], mmf)
            t_ = tfin[i % 2]
            k.do(ACT, [Bps[bkf], Bconst], [Btfin[i % 2]], lambda: sc.activation(
                out=t_[:], in_=ps[bkf][:], func=AF.Exp, scale=-1.0, bias=lnhalf[:, 0:1]))
            k.do(DVE, [Bacc[0]], [Btfin[i % 2]], lambda: ve.tensor_tensor(
                t_[0:64, :], acc_e[0:64, cs], t_[0:64, :], ALU.mult))
            k.do(DVE, [Bacc[1]], [Btfin[i % 2]], lambda: ve.tensor_tensor(
                t_[64:128, :], acc_o[64:128, cs], t_[64:128, :], ALU.mult))
            k.do(POOL, [Btfin[i % 2]], [BmixA[hp][i]], lambda: gp_.tensor_tensor(
                mixA[:, hp, cs], t_[:], mixA[:, hp, cs], ALU.mult))

    npairs_a = 4 if stage != 3 else 1
    load_pair_weights(0)
    for hp in range(npairs_a):
        phaseA_pair(hp)
    if stage in (3, 4):
        return finish((mixA[:],))
    k.barrier()

    o = R0
    TT = A("TT", o, [128, S], F32)
    TTi = A("TTi", OFF_MIXB, [128, S], I32)
    o += 16384
    cqn = A("cqn", o, [128, 2, S], BF16)
    o += 16384
    ckvn = A("ckvn", o, [128, S], BF16)
    o += 8192
    kT = [A(f"kT{i}", o + i * 8192, [128, S], BF16) for i in range(2)]
    o += 16384
    OFF_W = o
    wb = [A(f"wb{i}", o + i * 2048, [128, 8, 128], BF16) for i in range(2)]
    o += 4096
    wstB = A("wstB", o, [128, 8, 128], F32)
    o += 4096
    sq2 = [A(f"sq2{i}", o + i * 1024, [128, 512], BF16) for i in range(2)]
    o += 2048
    ssb2 = A("ssb2", o, [128, 512], F32)
    o += 2048
    rstd2 = A("rstd2", o, [128, 512], F32)
    o += 2048
    tk = A("tk", o, [128, 512], F32)
    o += 2048
    rstd2b = A("rstd2b", o, [128, 512], F32)
    Brstd2b = Buf()
    ta = A("ta", OFF_W + 4096, [128, 512], F32)
    tu = A("tu", OFF_W + 6144, [128, 512], F32)
    assert SB_BASE + o <= SB_LIMIT, o
    sq[0], sq[1] = sq2[0], sq2[1]
    ssb[0] = ssb[1] = ssb2
    rstd[0] = rstd[1] = rstd2
    Bsq[0], Bsq[1] = Buf(), Buf()
    Bssb[0] = Bssb[1] = Buf()
    Brstd[0] = Brstd[1] = Buf()
    BTT = Buf()
    Bcqn = [Buf() for _ in range(NQ)]
    Bckvn = [Buf() for _ in range(NQ)]
    BkT = [[Buf() for _ in range(NQ)] for _ in range(2)]
    Bwb = [Buf(), Buf()]
    BwstB = Buf()
    Btk, Bta, Btu = Buf(), Buf(), Buf()
    dwb = k.dsem("d_wb")
    dpos = k.dsem("d_pos")

    k.do(POOL, [], [BTT], lambda: gp_.memset(TT[0:64, :], 1.0))
    k.dma(SP, dpos, TTi[64:128, :], bass.AP(pos_t, 0, [[0, 64], [1, S]]), writes=[BTT])
    k.do(DVE, [], [BTT], lambda: ve.tensor_copy(TT[64:128, :], TTi[64:128, :]))
    MAGIC = 12582912.0
    TWO_PI = 2.0 * math.pi
    for i in range(NQ):
        cs = slice(i * 512, (i + 1) * 512)
        k.do(DVE, [BTT, Bconst], [Bta], lambda: ve.tensor_scalar(
            ta[64:128, :], TT[64:128, cs], rc[64:128, 0:1], rc[64:128, 1:2], ALU.mult, ALU.add))
        k.do(DVE, [Bta], [Btu], lambda: ve.tensor_scalar(tu[64:128, :], ta[64:128, :], 1.0 / TWO_PI, MAGIC, ALU.mult, ALU.add))
        k.do(DVE, [], [Btu], lambda: ve.tensor_scalar(tu[64:128, :], tu[64:128, :], -MAGIC, None, ALU.add))
        k.do(DVE, [Bta], [Btu], lambda: ve.scalar_tensor_tensor(
            tu[64:128, :], tu[64:128, :], -TWO_PI, ta[64:128, :], ALU.mult, ALU.add))
        k.do(DVE, [], [Btu], lambda: ve.tensor_scalar(tu[64:128, :], tu[64:128, :], -math.pi, math.pi, ALU.max, ALU.min))
        k.do(ACT, [Btu, Bconst], [BTT], lambda: sc.activation(
            out=TT[64:128, cs], in_=tu[64:128, :], func=AF.Sin, scale=rc[64:128, 2:3]))
    if stage == 5:
        return finish((TT[:],))
    k.barrier()

    k.do(POOL, [], [Bsq[0]], lambda: gp_.memset(sq2[0][:], 0.0))
    k.do(POOL, [], [Bsq[1]], lambda: gp_.memset(sq2[1][:], 0.0))
    k.do(POOL, [], [Btk], lambda: gp_.memset(tk[:], 0.0))

    wc = [A(f"wc{i}", OFF_HT - 0 + 0, [1, 1], F32) for i in range(0)]

    def prep_chunk_weights(dst_i, src_ap, ncols):
        load_w_scaled(dwb, wstB, BwstB, wb[dst_i], Bwb[dst_i], src_ap, ncols)

    for i in range(NQ):
        cs = slice(i * 512, (i + 1) * 512)
        if i == 0:
            prep_chunk_weights(0, w_in_v[:, :, 2048:2176], 128)
            prep_chunk_weights(1, w_in_v[:, :, 2176:2304], 128)
        proj512(0, wb[0], Bwb[0], i)
        proj512(1, wb[1], Bwb[1], i)
        r_ = blocknorm([0, 1], 2, ones256)
        for c in range(2):
            k.do(DVE, [Bps[c], Brstd[r_], Bconst], [Bcqn[i]], lambda c=c: ve.scalar_tensor_tensor(
                cqn[:, c, cs], ps[c][:], gp[:, 10 + c:11 + c], rstd[r_][:], ALU.mult, ALU.mult))
    if stage == 61:
        return finish((cqn[:],))
    for i in range(NQ):
        cs = slice(i * 512, (i + 1) * 512)
        if i == 0:
            prep_chunk_weights(0, w_in_v[:, :, 2304:2432], 128)
        bk = [0, 1][i % 2]
        proj512(bk, wb[0], Bwb[0], i)
        r_ = blocknorm([bk], 2, ones128)
        k.do(DVE, [Bps[bk], Brstd[r_], Bconst], [Bckvn[i]], lambda: ve.scalar_tensor_tensor(
            ckvn[:, cs], ps[bk][:], gp[:, 12:13], rstd[r_][:], ALU.mult, ALU.mult))
    if stage == 62:
        return finish((cqn[:],))
    for i in range(NQ):
        cs = slice(i * 512, (i + 1) * 512)
        if i == 0:
            prep_chunk_weights(1, w_kr_v, 128)
        bk = [3, 4][i % 2]

        def mm():
            for dmc in range(8):
                ins = tn.matmul(ps[bk][:], wb[1][:, dmc, :], hT[:, dmc, cs],
                                start=(dmc == 0), stop=(dmc == 7))
            return ins
        k.do(PE, [Bwb[1]] + BhT[4 * i:4 * i + 4], [Bps[bk]], mm)
        s_ = i % 2
        k.do(ACT, [Bps[bk]], [Bsq[s_]], lambda: sc.activation(out=sq2[s_][64:128, :], in_=ps[bk][64:128, :], func=AF.Square))
        k.do(PE, [Bsq[s_], Bconst], [Bps[5]], lambda: tn.matmul(ps[5][:], bo_q, sq2[s_][:], start=True, stop=True))
        k.do(ACT, [Bps[5], Bconst], [Bssb[0]], lambda: sc.activation(out=ssb2[:], in_=ps[5][:], func=AF.Ln, bias=epsT[:, 0:1]))
        k.do(ACT, [Bssb[0]], [Brstd[0]], lambda: sc.activation(out=rstd2[:], in_=ssb2[:], func=AF.Exp, scale=-0.5))
        k.do(DVE, [Bps[bk], BTT, Bconst], [Btk], lambda: ve.scalar_tensor_tensor(
            tk[64:128, :], ps[bk][64:128, :], gp[64:128, 14:15], TT[64:128, cs], ALU.mult, ALU.mult))
        k.do(DVE, [Brstd[0]], [Btk], lambda: ve.tensor_tensor(tk[64:128, :], tk[64:128, :], rstd2[64:128, :], ALU.mult))
        k.do(PE, [Btk, Bconst], [Bps[6]], lambda: tn.matmul(ps[6][:], msum, tk[:], start=True, stop=True))
        k.do(ACT, [Bps[6]], [BkT[0][i]], lambda: sc.activation(out=kT[0][64:128, cs], in_=ps[6][64:128, :], func=AF.Copy))
        k.do(ACT, [Bps[6]], [BkT[1][i]], lambda: sc.activation(out=kT[1][64:128, cs], in_=ps[6][64:128, :], func=AF.Copy))
        if stage == 60 and i == 0:
            dd = A("dd", OFF_MIXB, [128, 5, 512], F32)
            Bdd = Buf()
            k.do(POOL, [], [Bdd], lambda: gp_.memset(dd[:], 0.0))
            k.do(DVE, [Bps[bk]], [Bdd], lambda: ve.tensor_copy(dd[64:128, 0, :], ps[bk][64:128, :]))
            k.do(DVE, [Bssb[0]], [Bdd], lambda: ve.tensor_copy(dd[:, 1, :], ssb2[:]))
            k.do(DVE, [Brstd[0]], [Bdd], lambda: ve.tensor_copy(dd[:, 2, :], rstd2[:]))
            k.do(DVE, [Btk], [Bdd], lambda: ve.tensor_copy(dd[:, 3, :], tk[:]))
            k.do(DVE, [Bps[6]], [Bdd], lambda: ve.tensor_copy(dd[:, 4, :], ps[6][:]))
            return finish((dd[:],))
    if stage == 63:
        return finish((cqn[:],))
    gates(2464, mixB, BmixB, wb, Bwb, [ssb2, rstd2], [Bssb[0], Brstd[0]], [0, 1, 2], wstB, BwstB, dwb)
    if stage == 6:
        return finish((kT[0][:],))
    if stage == 7:
        return finish((cqn[:],))
    k.barrier()

    o = OFF_HT
    qT = [A(f"qT{i}", o + i * 8192, [128, S], BF16) for i in range(2)]
    o += 16384
    Vb = [A(f"Vb{i}", o + i * 12288, [128, 32, 192], BF16) for i in range(2)]
    o += 24576
    PTb = [A(f"PTb{i}", o + i * 1024, [128, 512], BF16) for i in range(4)]
    o += 4096
    tq = [A(f"tq{i}", o + i * 2048, [128, 512], F32) for i in range(2)]
    o += 4096
    bcs = [A(f"bcs{i}", o + i * 2048, [128, 512], F32) for i in range(2)]
    o += 4096
    RtB = A("RtB", o, [128, 512], F32)
    o += 2048
    tfb = [A(f"tfb{i}", o + i * 2048, [128, 512], F32) for i in range(2)]
    o += 4096
    assert o <= OFF_R
    wuq = A("wuq", OFF_W, [128, 2, 1024], BF16)
    wukk = A("wukk", OFF_W + 4096, [128, 576], BF16)
    wukv = A("wukv", OFF_W + 5248, [128, 512], BF16)
    BqT = [[Buf() for _ in range(NQ)] for _ in range(2)]
    BVb = [[Buf() for _ in range(8)] for _ in range(2)]
    BPTb = [Buf() for _ in range(4)]
    Btq = [Buf(), Buf()]
    Bbcs = [Buf(), Buf()]
    BRtB = Buf()
    Btfb = [Buf(), Buf()]
    Bwm = Buf()
    dwm = k.dsem("d_wm")
    k.dma(POOL, dwm, wuq[:], w_uq_v, writes=[Bwm])
    k.dma(POOL, dwm, wukk[:], w_ukvk_d, writes=[Bwm])
    k.dma(POOL, dwm, wukv[:], w_ukvv_d, writes=[Bwm])
    Bwm.w = (dwm, dwm.cnt)
    for s_ in range(2):
        k.do(POOL, [], [BVb[s_][0]], lambda: gp_.memset(Vb[s_][:, :, 64:128], 1.0))
        for g4 in range(1, 8):
            BVb[s_][g4].w = BVb[s_][0].w
    k.do(POOL, [], [BRtB], lambda: gp_.memset(RtB[:], 0.0))
    k.do(POOL, [], [Bsq[0]], lambda: gp_.memset(sq2[0][:], 0.0))
    k.do(POOL, [], [Bsq[1]], lambda: gp_.memset(sq2[1][:], 0.0))

    def phaseB_prep(hp):
        vs = hp % 2
        for g4 in range(8):
            bk = [6, 7][g4 % 2]

            def mmv():
                for tt in range(4):
                    n = g4 * 4 + tt
                    ins = tn.matmul(ps[bk][:, tt * 128:(tt + 1) * 128], ckvn[:, n * 128:(n + 1) * 128],
                                    wukv[:, hp * 128:(hp + 1) * 128], start=True, stop=True)
                return ins
            k.do(PE, [Bwm, Bckvn[g4 // 1 * 0 + (g4 * 4) // 4 // 1 if False else (g4 * 512) // 512]], [Bps[bk]], mmv)
            k.do(DVE, [Bps[bk]], [BVb[vs][g4]], lambda: ve.tensor_copy(
                Vb[vs][:, g4 * 4:(g4 + 1) * 4, :].rearrange("p n (s d) -> p n s d", s=3)[:, :, 0:3:2, :],
                ps[bk][:].rearrange("p (n s d) -> p n s d", n=4, s=2)))
        for hh in range(2):
            h = 2 * hp + hh
            def q_s1(i):
                cs = slice(i * 512, (i + 1) * 512)
                bq = [0, 1, 3, 4][i % 4]
                bss = [2, 5][i % 2]

                def mmq():
                    for c in range(2):
                        ins = tn.matmul(ps[bq][:], wuq[:, c, h * 128:(h + 1) * 128], cqn[:, c, cs],
                                        start=(c == 0), stop=(c == 1))
                    return ins
                k.do(PE, [Bwm, Bcqn[i]], [Bps[bq]], mmq)

            def q_s1b(i):
                bq = [0, 1, 3, 4][i % 4]
                bss = [2, 5][i % 2]
                s_ = i % 2
                k.do(ACT, [Bps[bq]], [Bsq[s_]], lambda: sc.activation(out=sq2[s_][:], in_=ps[bq][:], func=AF.Square))
                k.do(PE, [Bsq[s_], Bconst], [Bps[bss]], lambda: tn.matmul(ps[bss][:], bo_q, sq2[s_][:], start=True, stop=True))

            def q_s2(i):
                cs = slice(i * 512, (i + 1) * 512)
                bq = [0, 1, 3, 4][i % 4]
                bss = [2, 5][i % 2]
                rs, Brs = [(rstd2, Brstd[0]), (rstd2b, Brstd2b)][i % 2]
                k.do(ACT, [Bps[bss], Bconst], [Bssb[0]], lambda: sc.activation(out=ssb2[:], in_=ps[bss][:], func=AF.Ln, bias=epsT[:, 0:1]))
                k.do(ACT, [Bssb[0]], [Brs], lambda: sc.activation(out=rs[:], in_=ssb2[:], func=AF.Exp, scale=-0.5))
                t_ = tq[i % 2]
                k.do(DVE, [Bps[bq], BTT, Bconst], [Btq[i % 2]], lambda: ve.scalar_tensor_tensor(
                    t_[:], ps[bq][:], gp[:, 13:14], TT[:, cs], ALU.mult, ALU.mult))
                k.do(POOL, [Btq[i % 2], Brs], [BqT[hh][i]], lambda: gp_.tensor_tensor(
                    qT[hh][:, cs], t_[:], rs[:], ALU.mult))
            for i in range(NQ + 2):
                if i < NQ:
                    q_s1(i)
                if 1 <= i <= NQ:
                    q_s1b(i - 1)
                if i >= 2:
                    q_s2(i - 2)
            k.do(POOL, [], [Bsq[0]], lambda: gp_.memset(sq2[0][64:128, :], 0.0))
            k.do(POOL, [], [Bsq[1]], lambda: gp_.memset(sq2[1][64:128, :], 0.0))

            def k_s1(i):
                cs = slice(i * 512, (i + 1) * 512)
                bk_ = [3, 4, 0, 1][i % 4]
                bss = [5, 2][i % 2]
                k.do(PE, [Bwm, Bckvn[i]], [Bps[bk_]], lambda: tn.matmul(
                    ps[bk_][:], wukk[:, h * 64:h * 64 + 128], ckvn[:, cs], start=True, stop=True))

            def k_s1b(i):
                bk_ = [3, 4, 0, 1][i % 4]
                bss = [5, 2][i % 2]
                s_ = i % 2
                k.do(ACT, [Bps[bk_]], [Bsq[s_]], lambda: sc.activation(out=sq2[s_][0:64, :], in_=ps[bk_][0:64, :], func=AF.Square))
                k.do(PE, [Bsq[s_], Bconst], [Bps[bss]], lambda: tn.matmul(ps[bss][:], bo_q, sq2[s_][:], start=True, stop=True))

            def k_s2(i):
                cs = slice(i * 512, (i + 1) * 512)
                bk_ = [3, 4, 0, 1][i % 4]
                bss = [5, 2][i % 2]
                rs, Brs = [(rstd2, Brstd[0]), (rstd2b, Brstd2b)][i % 2]
                k.do(ACT, [Bps[bss], Bconst], [Bssb[0]], lambda: sc.activation(
                    out=ssb2[0:64, :], in_=ps[bss][0:64, :], func=AF.Ln, bias=epsT[0:64, 0:1]))
                k.do(ACT, [Bssb[0]], [Brs], lambda: sc.activation(out=rs[0:64, :], in_=ssb2[0:64, :], func=AF.Exp, scale=-0.5))
                k.do(DVE, [Bps[bk_], Brs, Bconst], [BkT[hh][i]], lambda: ve.scalar_tensor_tensor(
                    kT[hh][0:64, cs], ps[bk_][0:64, :], gp[0:64, 14:15], rs[0:64, :], ALU.mult, ALU.mult))
            for i in range(NQ + 2):
                if i < NQ:
                    k_s1(i)
                if 1 <= i <= NQ:
                    k_s1b(i - 1)
                if i >= 2:
                    k_s2(i - 2)

    def phaseB_attn(hp):
        vs = hp % 2
        tiles = []
        for i in range(NQ):
            for hh in range(2):
                for j in range(4 * i + 4):
                    tiles.append((i, hh, j))
        ntile = len(tiles)

        def emit_S(n):
            i, hh, j = tiles[n]
            bk = [0, 1, 2][n % 3]
            jj = j - 4 * i
            c0 = 128 * jj if jj > 0 else 0

            def mm():
                o_ = ps[bk][:, c0:512]
                ins = tn.matmul(o_, kT[hh][:, j * 128:(j + 1) * 128], qT[hh][:, i * 512 + c0:(i + 1) * 512],
                                start=True, stop=(jj < 0))
                if jj >= 0:
                    ins = tn.matmul(ps[bk][:, c0:c0 + 128], ident, causal, start=False, stop=True)
                return ins
            return [Bconst, BkT[hh][j // 4], BqT[hh][i]], [Bps[bk]], mm

        def emit_E(n):
            i, hh, j = tiles[n]
            bk = [0, 1, 2][n % 3]
            jj = j - 4 * i
            c0 = 128 * jj if jj > 0 else 0
            k.do(ACT, [Bps[bk]], [BPTb[n % 4]], lambda: sc.activation(
                out=PTb[n % 4][:, c0:512], in_=ps[bk][:, c0:512], func=AF.Exp))

        def emit_PV(n):
            i, hh, j = tiles[n]
            jj = j - 4 * i
            c0 = 128 * jj if jj > 0 else 0
            bank = [[3, 5], [4, 6]][hh][i % 2]
            lhs = Vb[vs][:, j, 0:128] if hh == 0 else Vb[vs][:, j, 64:192]
            last = (j == 4 * i + 3)
            return [BPTb[n % 4], BVb[vs][j // 4]], [Bps[bank]], (lambda: tn.matmul(
                ps[bank][:, c0:512], lhs, PTb[n % 4][:, c0:512], start=(j == 0), stop=last)), (last and hh == 1), i

        def finalize(i):
            cs = slice(i * 512, (i + 1) * 512)
            be, bo_ = [3, 5][i % 2], [4, 6][i % 2]
            k.do(ACT, [Bps[be]], [BRtB], lambda: sc.activation(out=RtB[64:65, :], in_=ps[be][64:65, :], func=AF.Ln))
            k.do(ACT, [Bps[bo_]], [BRtB], lambda: sc.activation(out=RtB[0:1, :], in_=ps[bo_][0:1, :], func=AF.Ln))
            k.do(PE, [BRtB, Bconst], [Bps[7]], lambda: tn.matmul(ps[7][:], sel, RtB[:], start=True, stop=True))
            b_ = bcs[i % 2]
            k.do(ACT, [Bps[7], Bconst], [Bbcs[i % 2]], lambda: sc.activation(
                out=b_[:], in_=ps[7][:], func=AF.Exp, scale=-1.0, bias=lnhalf[:, 0:1]))
            t_ = tfb[i % 2]
            k.do(DVE, [Bps[be], Bbcs[i % 2]], [Btfb[i % 2]], lambda: ve.tensor_tensor(
                t_[0:64, :], ps[be][0:64, :], b_[0:64, :], ALU.mult))
            k.do(DVE, [Bps[bo_], Bbcs[i % 2]], [Btfb[i % 2]], lambda: ve.tensor_tensor(
                t_[64:128, :], ps[bo_][64:128, :], b_[64:128, :], ALU.mult))
            k.do(POOL, [Btfb[i % 2]], [BmixB[hp][i]], lambda: gp_.tensor_tensor(
                mixB[:, hp, cs], t_[:], mixB[:, hp, cs], ALU.mult))

        for n in range(ntile + 2):
            if n < ntile:
                sS = emit_S(n)
                k.do(PE, sS[0], sS[1], sS[2])
                emit_E(n)
            if n >= 2:
                sP = emit_PV(n - 2)
                k.do(PE, sP[0], sP[1], sP[2])
                if sP[3]:
                    finalize(sP[4])

    npairs_b = 4 if stage != 8 else 1
    wo = A("wo", R0, [128, 8, 1024], BF16)
    Bwo = Buf()
    dwo = k.dsem("d_wo")
    w_out_pref = [False]
    for hp in range(npairs_b):
        phaseB_prep(hp)
        if hp == 3:
            for c in range(8):
                k.dma(POOL, dwo, wo[:, c, :], w_out_v[:, c, :], writes=([BTT] if c == 0 else []))
            Bwo.w = (dwo, dwo.cnt)
            w_out_pref[0] = True
        phaseB_attn(hp)
    if stage in (8, 9):
        return finish((mixB[:],))
    k.barrier()

    o = OFF_HT
    o += 16384
    xo = [A(f"xo{i}", o + i * 4096, [128, D], F32) for i in range(3)]
    o += 12288
    yo = [A(f"yo{i}", o + i * 4096, [128, D], F32) for i in range(3)]
    o += 12288
    Bxo = [Buf() for _ in range(3)]
    Byo = [Buf() for _ in range(3)]
    dxo = [k.dsem(f"d_xo{i}") for i in range(3)]
    dyo = [k.dsem(f"d_yo{i}") for i in range(3)]
    if not w_out_pref[0]:
        for c in range(8):
            k.dma(POOL, dwo, wo[:, c, :], w_out_v[:, c, :], writes=[Bwo])
        Bwo.w = (dwo, dwo.cnt)
    y_v = y_d.rearrange("(n p) d -> n p d", p=128)

    def p5_load(j):
        k.dma(SP, dxo[j % 3], xo[j % 3][:], x_v[j], writes=[Bxo[j % 3]])

    p5_load(0)
    p5_load(1)
    for j in range(NT):
        if j + 2 < NT:
            p5_load(j + 2)
        ts = slice(j * 128, (j + 1) * 128)
        for half in range(2):
            bk = [0, 1, 2, 3][(2 * j + half) % 4]

            def mm():
                for c in range(8):
                    src = mixA if c < 4 else mixB
                    ins = tn.matmul(ps[bk][:], src[:, c % 4, ts], wo[:, c, half * 512:(half + 1) * 512],
                                    start=(c == 0), stop=(c == 7))
                return ins
            k.do(PE, [Bwo] + [BmixA[c][j // 4] for c in range(4)] + [BmixB[c][j // 4] for c in range(4)], [Bps[bk]], mm)
            k.do(DVE, [Bps[bk], Bxo[j % 3]], [Byo[j % 3]], lambda: ve.tensor_tensor(
                yo[j % 3][:, half * 512:(half + 1) * 512], ps[bk][:], xo[j % 3][:, half * 512:(half + 1) * 512], ALU.add))
        k.dma(ACT, dyo[j % 3], y_v[j], yo[j % 3][:], reads=[Byo[j % 3]])
    k.barrier()
    k.es.close()
    return nc


def _t5_bucket(dist):
    d = np.maximum(dist.astype(np.float32), np.float32(1.0))
    large = 16 + (np.log(d / np.float32(16.0)) / np.float32(math.log(2048 / 16)) * np.float32(16.0)).astype(np.int32)
    large = np.minimum(large, 31)
    return np.where(dist < 16, dist, large)


def _consts():
    cbf = np.zeros((128, 6, 128), np.float32)
    p = np.arange(128)
    cbf[:, 0, :] = np.eye(128, dtype=np.float32)
    cbf[:, 1, :] = ((p[:, None] // 64) == (p[None, :] // 64)) / 64.0
    blk = np.where(p < 64, 0, np.where(p < 96, 1, 2))
    wgt = np.where(p < 64, 1.0 / 64, 1.0 / 32)
    cbf[:, 2, :] = (blk[:, None] == blk[None, :]) * wgt[None, :]
    cbf[:, 3, :] = 1.0 / 256
    cbf[:, 4, :] = 1.0 / 128
    cbf[:, 5, :] = np.where(p[None, :] >= p[:, None], 0.0, MASKV)
    cf = np.zeros((128, 2, 128), np.float32)
    cf[64, 0, 0:64] = 1.0
    cf[0, 0, 64:128] = 1.0
    for j in range(32):
        for a in (64 + j, 96 + j):
            for b in (64 + j, 96 + j):
                cf[a, 1, b] = 1.0
    cf2 = np.zeros((128, 2, 128), np.float32)
    cf2[64, 0, 0:64] = 1.0
    cf2[0, 1, 64:128] = 1.0
    rc = np.zeros((128, 4), np.float32)
    inv_freq = (np.float32(10000.0) ** (-np.arange(0, 32, 2, dtype=np.float32) / np.float32(32))).astype(np.float32)
    for rrow in range(64):
        prt = 64 + rrow
        rc[prt, 0] = inv_freq[rrow % 16]
        rc[prt, 1] = (math.pi / 2) if rrow < 32 else 0.0
        rc[prt, 2] = -1.0 if 32 <= rrow < 48 else 1.0
    return cbf, cf, cf2, rc


def _bias_tables(rel_bias):
    ki = np.arange(128)[:, None]
    c = np.arange(256)[None, :]
    j = c - ki
    valid = (j >= 0) & (j <= 128)
    bt = np.empty((4, 128, 6, 256), np.float32)
    for pi, r in enumerate(PATTERNS):
        bucket = _t5_bucket(np.maximum(j, 0) * r)
        for h in range(8):
            tab = np.where(valid, rel_bias[bucket, h], np.float32(MASKV)).astype(np.float32)
            bt[h // 2, :, pi * 2 + (h % 2), :] = tab
    return bt


_NC_CACHE = {}


def _prep_shared(inputs):
    f = lambda n: np.asarray(inputs[n], np.float32)
    w_in = np.ascontiguousarray(f("w_in")[0])
    kr = w_in[:, 2432:2464]
    w_kr = np.ascontiguousarray(np.concatenate([np.zeros((D, 64), np.float32), kr, kr[:, 16:32], kr[:, 0:16]], axis=1))
    w_uq = f("w_uq")[0]
    w_uq_ext = np.empty((256, 1024), np.float32)
    for h in range(8):
        blk = w_uq[:, h * 96:(h + 1) * 96]
        w_uq_ext[:, h * 128:h * 128 + 96] = blk
        w_uq_ext[:, h * 128 + 96:h * 128 + 112] = blk[:, 80:96]
        w_uq_ext[:, h * 128 + 112:h * 128 + 128] = blk[:, 64:80]
    w_ukv = f("w_ukv")[0].reshape(128, 8, 128)
    w_ukvk = np.ascontiguousarray(np.concatenate([w_ukv[:, :, 0:64].reshape(128, 512), np.zeros((128, 64), np.float32)], axis=1))
    w_ukvv = np.ascontiguousarray(w_ukv[:, :, 64:128].reshape(128, 512))
    w_out = np.ascontiguousarray(f("w_out")[0])
    gp = np.zeros((128, 16), np.float32)
    gp[:, 0:8] = f("norm_gain")[0].reshape(8, 128).T
    gp[:, 8] = np.tile(f("a_q_gain")[0], 2)
    gp[:, 9] = np.tile(f("a_k_gain")[0], 2)
    gp[:, 10:12] = f("q_c_gain")[0].reshape(2, 128).T
    gp[:, 12] = f("kv_c_gain")[0]
    qr = f("qr_gain")[0]
    kr_g = f("kr_gain")[0]
    gp[:, 13] = np.concatenate([f("qn_gain")[0], qr, qr[16:32], qr[0:16]])
    gp[:, 14] = np.concatenate([f("kn_gain")[0], kr_g, kr_g[16:32], kr_g[0:16]])
    cbf, cf, cf2, rc = _consts()
    bt = _bias_tables(f("rel_bias"))
    return dict(w_in=w_in, w_kr=w_kr, w_uq=w_uq_ext, w_ukvk=w_ukvk, w_ukvv=w_ukvv, w_out=w_out,
                cbf=cbf, cf=cf, cf2=cf2, gp=gp, rc=rc, bt=bt)


def kernel(**inputs):
    x = np.asarray(inputs["x"], np.float32)
    pos = np.asarray(inputs["positions"], np.int32)
    shared = _prep_shared(inputs)
    if "nc" not in _NC_CACHE:
        _NC_CACHE["nc"] = build()
    nc = _NC_CACHE["nc"]
    in_maps = []
    for b in range(8):
        m = dict(shared)
        m["x"] = np.ascontiguousarray(x[b])
        m["pos"] = np.ascontiguousarray(pos[b].reshape(1, S))
        in_maps.append(m)
    res = run_bass_kernel_spmd(nc, in_maps, core_ids=list(range(8)))
    return np.stack([np.asarray(r["y"], np.float32) for r in res.results], axis=0)
```
